# Optimizing a Trainium2 kernel written in Bass

```python
import math
import jax, jax.numpy as jnp
from jax import lax
import numpy as np

D_MODEL = 1024
BATCH = 16
SEQ = 4096
DEPTH = 4

GRID_W = 64
CTX_LEN = 256
D_MIX = 2 * D_MODEL
GROUP_W = D_MIX // 4
A_HEAD = 128
A_HEADS = GROUP_W // A_HEAD
A_CHUNK = 32
LB_FLOOR = 1e-30
B_HEADDIM = 64
B_HEADS = GROUP_W // B_HEADDIM
B_GROUPS = 2
B_STATE = 128
B_CONV = 4
XBC_W = GROUP_W + 2 * B_GROUPS * B_STATE
C_BLOCKS = 8
C_BLOCK = GROUP_W // C_BLOCKS
C_CONV = 4
C_POW = 8.0
D_HEAD = 128
D_HEADS = GROUP_W // D_HEAD
ROPE_BASE = 10000.0
SCAN_CHUNK = 64
CONV4_PAD = (2, 1)
D_FF = 2816
FFN_CONV = 3
ALPHA = (2 * DEPTH) ** 0.25
BETA = (8 * DEPTH) ** -0.25
EPS = 1e-6
IN_SPLITS = (GROUP_W,) * 5 + (GROUP_W, XBC_W, 2 * B_HEADS) + (GROUP_W,) * 2 + (GROUP_W,) * 4
D_IN = sum(IN_SPLITS)

kernel_name = "hybrid_hgrn2_ssd_rglru_retention_dit"


def layer_norm(x, g, b):
    xf = x.astype(jnp.float32)
    mu = jnp.mean(xf, -1, keepdims=True)
    var = jnp.mean(jnp.square(xf - mu), -1, keepdims=True)
    return ((xf - mu) * lax.rsqrt(var + EPS)).astype(x.dtype) * g + b


def head_norm(x):
    xf = x.astype(jnp.float32)
    mu = jnp.mean(xf, -1, keepdims=True)
    var = jnp.mean(jnp.square(xf - mu), -1, keepdims=True)
    return ((xf - mu) * lax.rsqrt(var + EPS)).astype(x.dtype)


def rms_norm(x, w):
    xf = x.astype(jnp.float32)
    return (xf * lax.rsqrt(jnp.mean(jnp.square(xf), -1, keepdims=True) + EPS)).astype(x.dtype) * w


def seg_flip(a):
    return jnp.concatenate([jnp.flip(a[:, :CTX_LEN], 1), jnp.flip(a[:, CTX_LEN:], 1)], axis=1)


def dir_stack(fwd, bwd):
    return jnp.concatenate([fwd, seg_flip(bwd)], axis=0)


def dir_merge(y):
    nb = y.shape[0] // 2
    return y[:nb] + seg_flip(y[nb:])


def dwconv(u, w, b, pad):
    out = lax.conv_general_dilated(u, w[:, None, :], window_strides=(1,), padding=[pad],
                                   dimension_numbers=("NWC", "WIO", "NWC"), feature_group_count=u.shape[-1])
    return out + b


def seg_dwconv(u, w, b, pad):
    return jnp.concatenate([dwconv(u[:, :CTX_LEN], w, b, pad), dwconv(u[:, CTX_LEN:], w, b, pad)], axis=1)


def masked_decay(mask, diff):
    return jnp.where(mask, jnp.exp(jnp.where(mask, diff, 0.0)), 0.0)


def gla_chunk_scan(q, k, v, log_f, chunk):
    nb, L, H, K = q.shape
    V = v.shape[-1]
    nc = L // chunk
    to_chunks = lambda a: jnp.moveaxis(a.astype(jnp.float32).reshape(nb, nc, chunk, *a.shape[2:]), 1, 0)
    mask = jnp.tril(jnp.ones((chunk, chunk), bool))[None, :, :, None, None]

    def step(S, inp):
        qc, kc, vc, lc = inp
        b = jnp.cumsum(lc, axis=1)
        rel = masked_decay(mask, b[:, :, None] - b[:, None, :])
        scores = jnp.sum(qc[:, :, None] * kc[:, None] * rel, axis=-1)
        y = (jnp.einsum("btsh,bshv->bthv", scores, vc)
             + jnp.einsum("bthk,bhkv->bthv", qc * jnp.exp(b), S))
        b_last = b[:, -1]
        S = (jnp.exp(b_last)[..., None] * S
             + jnp.einsum("bshk,bshv->bhkv", kc * jnp.exp(b_last[:, None] - b), vc))
        return S, y

    S0 = jnp.zeros((nb, H, K, V), jnp.float32)
    _, ys = lax.scan(step, S0, (to_chunks(q), to_chunks(k), to_chunks(v), to_chunks(log_f)))
    return jnp.moveaxis(ys, 0, 1).reshape(nb, L, H, V).astype(v.dtype)


def decay_chunk_scan(q, k, v, log_a, chunk):
    nb, L, G, N = q.shape
    E, P = v.shape[-2:]
    nc = L // chunk
    to_chunks = lambda a: jnp.moveaxis(a.astype(jnp.float32).reshape(nb, nc, chunk, *a.shape[2:]), 1, 0)
    mask = jnp.tril(jnp.ones((chunk, chunk), bool))

    def step(S, inp):
        qc, kc, vc, lc = inp
        b = jnp.cumsum(lc, axis=1)
        bt = jnp.moveaxis(b, 1, -1)
        rel = masked_decay(mask, bt[..., :, None] - bt[..., None, :])
        qk = jnp.einsum("btgn,bsgn->bgts", qc, kc)
        y = (jnp.einsum("bgets,bsgep->btgep", qk[:, :, None] * rel, vc)
             + jnp.einsum("btgn,bgenp->btgep", qc, S) * jnp.exp(b)[..., None])
        b_last = b[:, -1]
        S = (jnp.exp(b_last)[..., None, None] * S
             + jnp.einsum("bsgn,bsgep->bgenp", kc, vc * jnp.exp(b_last[:, None] - b)[..., None]))
        return S, y

    S0 = jnp.zeros((nb, G, E, N, P), jnp.float32)
    _, ys = lax.scan(step, S0, (to_chunks(q), to_chunks(k), to_chunks(v), to_chunks(log_a)))
    return jnp.moveaxis(ys, 0, 1).reshape(nb, L, G, E, P).astype(v.dtype)


def lin_combine(left, right):
    a1, b1 = left
    a2, b2 = right
    return a1 * a2, a2 * b1 + b2


def apply_rope(t, cos, sin):
    half = t.shape[-1] // 2
    t1, t2 = t[..., :half], t[..., half:]
    cs, sn = cos[None, :, None, :], sin[None, :, None, :]
    return jnp.concatenate([t1 * cs - t2 * sn, t1 * sn + t2 * cs], axis=-1)


def adaln(cond, w, b):
    return jnp.split(jax.nn.silu(cond) @ w + b, 6, axis=-1)


def hybrid_mixer(h, w_in, lb, hgrn_norm_w, ssm_conv_w, ssm_conv_b, ssm_dt_bias, ssm_a_log, ssm_d, ssm_norm_w,
                 lru_conv_w, lru_conv_b, lru_wa, lru_ba, lru_wi, lru_bi, lru_lambda, ret_log_gamma,
                 cos, sin, w_out):
    nb, L, _ = h.shape
    w_parts = jnp.split(w_in, np.cumsum(IN_SPLITS)[:-1].tolist(), axis=1)
    (a_q, a_i, a_fw, a_bw, a_g, b_z, b_xbc, b_dt, c_x, c_g,
     d_q, d_k, d_v, d_g) = [h @ wp for wp in w_parts]

    u = dir_stack(a_fw, a_bw)
    lb_s = jnp.repeat(lb, nb, axis=0)[:, None, :]
    log_f = jnp.logaddexp(jnp.log(jnp.maximum(lb_s, LB_FLOOR)), jnp.log1p(-lb_s) + jax.nn.log_sigmoid(u))
    k_a = (1.0 - lb_s) * jax.nn.sigmoid(-u)
    hd = lambda t: t.reshape(2 * nb, L, A_HEADS, A_HEAD)
    o_a = gla_chunk_scan(hd(dir_stack(a_q, a_q)), hd(k_a), hd(dir_stack(a_i, a_i)), hd(log_f), A_CHUNK)
    o_a = rms_norm(dir_merge(o_a), hgrn_norm_w.reshape(A_HEADS, A_HEAD)).reshape(nb, L, GROUP_W)
    out_a = o_a * jax.nn.silu(a_g)

    xbc = jax.nn.silu(seg_dwconv(b_xbc, ssm_conv_w, ssm_conv_b, CONV4_PAD))
    bx, bB, bC = jnp.split(xbc, [GROUP_W, GROUP_W + B_GROUPS * B_STATE], axis=-1)
    hpg = B_HEADS // B_GROUPS
    dt = jax.nn.softplus(dir_stack(b_dt[..., :B_HEADS], b_dt[..., B_HEADS:])
                         + jnp.repeat(ssm_dt_bias, nb, axis=0)[:, None, :])
    log_a_b = dt * jnp.repeat(-jnp.exp(ssm_a_log), nb, axis=0)[:, None, :]
    xs_b = dir_stack(bx, bx).reshape(2 * nb, L, B_GROUPS, hpg, B_HEADDIM)
    gs = lambda t: dir_stack(t, t).reshape(2 * nb, L, B_GROUPS, B_STATE)
    y_b = decay_chunk_scan(gs(bC), gs(bB), xs_b * dt.reshape(2 * nb, L, B_GROUPS, hpg)[..., None],
                           log_a_b.reshape(2 * nb, L, B_GROUPS, hpg), SCAN_CHUNK)
    y_b = (dir_merge(y_b).reshape(nb, L, B_HEADS, B_HEADDIM)
           + ssm_d[:, None] * bx.reshape(nb, L, B_HEADS, B_HEADDIM))
    y_b = (y_b.reshape(nb, L, GROUP_W) * jax.nn.silu(b_z)).reshape(nb, L, B_GROUPS, GROUP_W // B_GROUPS)
    out_b = rms_norm(y_b, ssm_norm_w.reshape(B_GROUPS, -1)).reshape(nb, L, GROUP_W)

    cx = seg_dwconv(c_x, lru_conv_w, lru_conv_b, CONV4_PAD)
    xs_c = dir_stack(cx, cx).reshape(2, nb, L, C_BLOCKS, C_BLOCK)
    r = jax.nn.sigmoid(jnp.einsum("dblhi,dhij->dblhj", xs_c, lru_wa).reshape(2, nb, L, GROUP_W)
                       + lru_ba[:, None, None])
    gi = jax.nn.sigmoid(jnp.einsum("dblhi,dhij->dblhj", xs_c, lru_wi).reshape(2, nb, L, GROUP_W)
                        + lru_bi[:, None, None])
    log_a_c = -C_POW * r * jax.nn.softplus(-lru_lambda)[:, None, None]
    b_in = jnp.sqrt(-jnp.expm1(2.0 * log_a_c)) * gi * xs_c.reshape(2, nb, L, GROUP_W)
    _, h_c = lax.associative_scan(lin_combine, (jnp.exp(log_a_c), b_in), axis=2)
    out_c = dir_merge(h_c.reshape(2 * nb, L, GROUP_W)) * jax.nn.gelu(c_g)

    qd = apply_rope(d_q.reshape(nb, L, D_HEADS, D_HEAD), cos, sin)
    kd = apply_rope(d_k.reshape(nb, L, D_HEADS, D_HEAD), cos, sin) * D_HEAD ** -0.5
    vd = d_v.reshape(nb, L, D_HEADS, 1, D_HEAD)
    log_g = jnp.broadcast_to(jnp.repeat(ret_log_gamma, nb, axis=0)[:, None, :, None], (2 * nb, L, D_HEADS, 1))
    y_d = decay_chunk_scan(dir_stack(qd, qd), dir_stack(kd, kd), dir_stack(vd, vd), log_g, SCAN_CHUNK)
    out_d = head_norm(dir_merge(y_d)[..., 0, :]).reshape(nb, L, GROUP_W) * jax.nn.silu(d_g)

    return jnp.concatenate([out_a, out_b, out_c, out_d], axis=-1) @ w_out


def conv_ffn(h, w_in, conv_w, conv_b, w_out):
    gate, up = jnp.split(h @ w_in, 2, axis=-1)
    gate = dwconv(gate, conv_w, conv_b, (1, 1))
    return (jax.nn.gelu(gate) * up) @ w_out


def setup_inputs(seed: int = 0) -> dict:
    key = jax.random.key(seed)
    ks = iter(jax.random.split(key, 40))
    f32 = jnp.float32
    nrm = lambda shape, s: s * jax.random.normal(next(ks), shape, f32)
    x = nrm((BATCH, SEQ, D_MODEL), 1.0)
    c = nrm((BATCH, D_MODEL), 1.0)
    ctx = nrm((BATCH, CTX_LEN, D_MODEL), 1.0)
    c_ctx = nrm((D_MODEL,), 1.0)
    ada_w = nrm((DEPTH, D_MODEL, 6 * D_MODEL), D_MODEL ** -0.5)
    ada_b = nrm((DEPTH, 6 * D_MODEL), 0.02)
    w_in = nrm((DEPTH, D_MODEL, D_IN), D_MODEL ** -0.5)
    hgrn_lb_logits = nrm((DEPTH, 2, GROUP_W), 0.1)
    hgrn_norm_w = 1.0 + nrm((DEPTH, GROUP_W), 0.02)
    ssm_conv_w = nrm((DEPTH, B_CONV, XBC_W), B_CONV ** -0.5)
    ssm_conv_b = nrm((DEPTH, XBC_W), 0.02)
    dt0 = jnp.exp(jax.random.uniform(next(ks), (DEPTH, 2, B_HEADS), f32, math.log(1e-3), math.log(1e-1)))
    ssm_dt_bias = dt0 + jnp.log(-jnp.expm1(-dt0))
    ssm_a_log = jnp.log(jax.random.uniform(next(ks), (DEPTH, 2, B_HEADS), f32, 1.0, 16.0))
    ssm_d = 1.0 + nrm((DEPTH, B_HEADS), 0.1)
    ssm_norm_w = 1.0 + nrm((DEPTH, GROUP_W), 0.02)
    lru_conv_w = nrm((DEPTH, C_CONV, GROUP_W), C_CONV ** -0.5)
    lru_conv_b = nrm((DEPTH, GROUP_W), 0.02)
    lru_wa = nrm((DEPTH, 2, C_BLOCKS, C_BLOCK, C_BLOCK), C_BLOCK ** -0.5)
    lru_ba = nrm((DEPTH, 2, GROUP_W), 0.02)
    lru_wi = nrm((DEPTH, 2, C_BLOCKS, C_BLOCK, C_BLOCK), C_BLOCK ** -0.5)
    lru_bi = nrm((DEPTH, 2, GROUP_W), 0.02)
    a_c = jax.random.uniform(next(ks), (DEPTH, 2, GROUP_W), f32, 0.9, 0.999)
    s = a_c ** (1.0 / C_POW)
    lru_lambda = jnp.log(s) - jnp.log1p(-s)
    ret_decay_logit = (jnp.log(2.0 ** (5.0 + jnp.arange(D_HEADS, dtype=f32)) - 1.0)
                       + nrm((DEPTH, 2, D_HEADS), 0.1))
    w_out = nrm((DEPTH, D_MIX, D_MODEL), BETA * D_MIX ** -0.5)
    ln1_g = 1.0 + nrm((DEPTH, D_MODEL), 0.02)
    ln1_b = nrm((DEPTH, D_MODEL), 0.02)
    ffn_w_in = nrm((DEPTH, D_MODEL, 2 * D_FF), D_MODEL ** -0.5)
    ffn_conv_w = nrm((DEPTH, FFN_CONV, D_FF), FFN_CONV ** -0.5)
    ffn_conv_b = nrm((DEPTH, D_FF), 0.02)
    ffn_w_out = nrm((DEPTH, D_FF, D_MODEL), BETA * D_FF ** -0.5)
    ln2_g = 1.0 + nrm((DEPTH, D_MODEL), 0.02)
    ln2_b = nrm((DEPTH, D_MODEL), 0.02)
    return {"x": x, "c": c, "ctx": ctx, "c_ctx": c_ctx, "ada_w": ada_w, "ada_b": ada_b, "w_in": w_in,
            "hgrn_lb_logits": hgrn_lb_logits, "hgrn_norm_w": hgrn_norm_w,
            "ssm_conv_w": ssm_conv_w, "ssm_conv_b": ssm_conv_b, "ssm_dt_bias": ssm_dt_bias,
            "ssm_a_log": ssm_a_log, "ssm_d": ssm_d, "ssm_norm_w": ssm_norm_w,
            "lru_conv_w": lru_conv_w, "lru_conv_b": lru_conv_b, "lru_wa": lru_wa, "lru_ba": lru_ba,
            "lru_wi": lru_wi, "lru_bi": lru_bi, "lru_lambda": lru_lambda, "ret_decay_logit": ret_decay_logit,
            "w_out": w_out, "ln1_g": ln1_g, "ln1_b": ln1_b, "ffn_w_in": ffn_w_in, "ffn_conv_w": ffn_conv_w,
            "ffn_conv_b": ffn_conv_b, "ffn_w_out": ffn_w_out, "ln2_g": ln2_g, "ln2_b": ln2_b}


def reference(x, c, ctx, c_ctx, ada_w, ada_b, w_in, hgrn_lb_logits, hgrn_norm_w,
              ssm_conv_w, ssm_conv_b, ssm_dt_bias, ssm_a_log, ssm_d, ssm_norm_w,
              lru_conv_w, lru_conv_b, lru_wa, lru_ba, lru_wi, lru_bi, lru_lambda, ret_decay_logit,
              w_out, ln1_g, ln1_b, ffn_w_in, ffn_conv_w, ffn_conv_b, ffn_w_out, ln2_g, ln2_b):
    n = x.shape[1]
    ROWS = n // GRID_W
    row = jnp.repeat(jnp.arange(ROWS), GRID_W)
    col = jnp.tile(jnp.arange(GRID_W), ROWS)
    n_freq = D_HEAD // 4
    inv = ROPE_BASE ** (-jnp.arange(n_freq, dtype=jnp.float32) / n_freq)
    ang = jnp.concatenate([row[:, None] * inv, col[:, None] * inv], axis=-1)
    ang = jnp.concatenate([jnp.zeros((CTX_LEN, D_HEAD // 2), jnp.float32), ang], axis=0)
    cos, sin = jnp.cos(ang).astype(x.dtype), jnp.sin(ang).astype(x.dtype)
    p = jax.nn.softmax(hgrn_lb_logits.astype(jnp.float32), axis=0)
    lb_all = jnp.cumsum(p, axis=0) - p
    ret_log_gamma = jax.nn.log_sigmoid(ret_decay_logit.astype(jnp.float32))

    h_lat, h_ctx = x, ctx
    for l in range(DEPTH):
        sh1, sc1, g1, sh2, sc2, g2 = [m[:, None] for m in adaln(c, ada_w[l], ada_b[l])]
        csh1, csc1, cg1, csh2, csc2, cg2 = adaln(c_ctx, ada_w[l], ada_b[l])
        u = jnp.concatenate([h_ctx * (1.0 + csc1) + csh1, h_lat * (1.0 + sc1) + sh1], axis=1)
        mix = hybrid_mixer(u, w_in[l], lb_all[l], hgrn_norm_w[l], ssm_conv_w[l], ssm_conv_b[l],
                           ssm_dt_bias[l], ssm_a_log[l], ssm_d[l], ssm_norm_w[l],
                           lru_conv_w[l], lru_conv_b[l], lru_wa[l], lru_ba[l], lru_wi[l], lru_bi[l],
                           lru_lambda[l], ret_log_gamma[l], cos, sin, w_out[l])
        h_lat = layer_norm(ALPHA * h_lat + g1 * mix[:, CTX_LEN:], ln1_g[l], ln1_b[l])
        f_lat = conv_ffn(h_lat * (1.0 + sc2) + sh2, ffn_w_in[l], ffn_conv_w[l], ffn_conv_b[l], ffn_w_out[l])
        h_lat = layer_norm(ALPHA * h_lat + g2 * f_lat, ln2_g[l], ln2_b[l])
        if l < DEPTH - 1:
            h_ctx = layer_norm(ALPHA * h_ctx + cg1 * mix[:, :CTX_LEN], ln1_g[l], ln1_b[l])
            f_ctx = conv_ffn(h_ctx * (1.0 + csc2) + csh2, ffn_w_in[l], ffn_conv_w[l], ffn_conv_b[l], ffn_w_out[l])
            h_ctx = layer_norm(ALPHA * h_ctx + cg2 * f_ctx, ln2_g[l], ln2_b[l])
    return h_lat
```

```python
import os
import numpy as np
from contextlib import ExitStack
import concourse.bass as bass
import concourse.mybir as mybir
from concourse.bass_utils import run_bass_kernel_spmd

F32 = mybir.dt.float32
BF16 = mybir.dt.bfloat16
AF = mybir.ActivationFunctionType
ALU = mybir.AluOpType
AX = mybir.AxisListType

D = 1024
CTX = 256
DEPTH = 4
D_IN = 7184
D_FF = 2816
NCC = 5632
NTC = 2576
ALPHA = (2 * DEPTH) ** 0.25
EPS = 1e-6
NEG = -30000.0

RC_SRC = [(0, 512), (1024, 1024), (3072, 1024), (4112, 1024), (5136, 1024), (7184, 1024)]
RC_Q, RC_U, RC_XBC, RC_CX, RC_CG, RC_DQ, RC_DK, RC_DQR, RC_DKR = 0, 512, 1536, 2560, 3072, 3584, 4096, 4608, 5120
RT_SRC = [(512, 512), (2048, 512), (2560, 512), (6160, 512), (6672, 512), (4096, 16)]
RT_AI, RT_AG, RT_BZ, RT_DV, RT_DG, RT_DT = 0, 512, 1024, 1536, 2048, 2560


class Buf:
    __slots__ = ("name", "w", "r")

    def __init__(self, name):
        self.name = name
        self.w = None
        self.r = {}


class Eng:
    def __init__(self, name, obj, sem):
        self.name = name
        self.obj = obj
        self.sem = sem
        self.count = 0
        self.seen = {}


class Sched:
    same_engine_sync = True

    def __init__(self, nc, es):
        self.nc = nc
        self.es = es
        self.E = {}
        for name, obj in (("pe", nc.tensor), ("dve", nc.vector), ("act", nc.scalar),
                          ("pool", nc.gpsimd), ("sp", nc.sync)):
            sem = es.enter_context(nc.semaphore("s_" + name))
            self.E[name] = Eng(name, obj, sem)
        self.sems = {e.name: e.sem for e in self.E.values()}
        self.dma_count = {}
        self.n_wait = 0
        self.n_ins = 0

    def _deps(self, reads, writes):
        deps = {}
        for b in reads:
            if b.w is not None:
                k, v = b.w
                if deps.get(k, 0) < v:
                    deps[k] = v
        for b in writes:
            if b.w is not None:
                k, v = b.w
                if deps.get(k, 0) < v:
                    deps[k] = v
            for (k, v) in b.r.items():
                if deps.get(k, 0) < v:
                    deps[k] = v
        return deps

    def _emit(self, eng, deps, fn):
        need = []
        for k, v in deps.items():
            if eng.seen.get(k, 0) >= v:
                continue
            if k == eng.name and (eng.name == "pe" or not self.same_engine_sync):
                continue
            if k in self.dma_count:
                v = self.dma_count[k]
            need.append((k, v))
            eng.seen[k] = v
        for (k, v) in need[1:]:
            eng.obj.wait_ge(self.sems[k], v)
            self.n_wait += 1
        ins = fn()
        if need:
            k, v = need[0]
            ins._wait_ge(self.sems[k], v)
        self.n_ins += 1
        return ins

    def _mark(self, t, reads, writes):
        k, v = t
        for b in reads:
            if b.r.get(k, 0) < v:
                b.r[k] = v
        for b in writes:
            b.w = t
            b.r = {}

    def op(self, en, fn, reads=(), writes=()):
        eng = self.E[en]
        ins = self._emit(eng, self._deps(reads, writes), fn)
        eng.count += 1
        ins.then_inc(eng.sem, 1)
        self._mark((eng.name, eng.count), reads, writes)
        return ins

    def dma(self, qn, out, in_, semkey, reads=(), writes=(), **kw):
        eng = self.E[qn]
        if semkey not in self.sems:
            self.sems[semkey] = self.es.enter_context(self.nc.semaphore("d_" + semkey))
            self.dma_count[semkey] = 0
        ins = self._emit(eng, self._deps(reads, writes), lambda: eng.obj.dma_start(out=out, in_=in_, **kw))
        self.dma_count[semkey] += 16
        ins.then_inc(self.sems[semkey], 16)
        self._mark((semkey, self.dma_count[semkey]), reads, writes)
        return ins

    def wait_all(self, en, bufs):
        eng = self.E[en]
        deps = self._deps(bufs, bufs)
        for k, v in deps.items():
            if eng.seen.get(k, 0) >= v:
                continue
            eng.obj.wait_ge(self.sems[k], v)
            eng.seen[k] = v


class T:
    __slots__ = ("t", "b")

    def __init__(self, t, name):
        self.t = t
        self.b = Buf(name)

    def __getitem__(self, k):
        return self.t[k]


class KB:
    def __init__(self, nc, es):
        self.nc = nc
        self.S = Sched(nc, es)
        self.uid = 0
        self.all_dram = []

    def sb(self, es, name, shape, dt):
        self.uid += 1
        return T(es.enter_context(self.nc.sbuf_tensor(f"{name}_{self.uid}", list(shape), dt)), name)

    def ps(self, es, name, shape, dt):
        self.uid += 1
        return T(es.enter_context(self.nc.psum_tensor(f"{name}_{self.uid}", list(shape), dt)), name)

    def dram(self, name, shape, dt, kind="Internal"):
        t = T(self.nc.dram_tensor(name, list(shape), dt, kind=kind), name)
        self.all_dram.append(t)
        return t

    def V(self, fn, r=(), w=()):
        return self.S.op("dve", fn, [x.b for x in r], [x.b for x in w])

    def A(self, fn, r=(), w=()):
        return self.S.op("act", fn, [x.b for x in r], [x.b for x in w])

    def G(self, fn, r=(), w=()):
        return self.S.op("pool", fn, [x.b for x in r], [x.b for x in w])

    def P(self, fn, r=(), w=()):
        return self.S.op("pe", fn, [x.b for x in r], [x.b for x in w])

    def dma(self, q, out, in_, key, r=(), w=(), **kw):
        return self.S.dma(q, out, in_, key, [x.b for x in r], [x.b for x in w], **kw)

    def barrier(self):
        S = self.S
        targets = {k: e.count for k, e in S.E.items() if e.count > 0}
        targets.update({k: v for k, v in S.dma_count.items() if v > 0})
        for en, eng in S.E.items():
            for k, v in targets.items():
                if k == en or eng.seen.get(k, 0) >= v:
                    continue
                eng.obj.wait_ge(S.sems[k], v)
                eng.seen[k] = v
                S.n_wait += 1


def seg_tiles(LAT):
    L = CTX + LAT
    return L // 128


def blocks_of(LAT, maxT=512):
    out = [(0, CTX, True)]
    t = CTX
    while t < CTX + LAT:
        T_ = min(maxT, CTX + LAT - t)
        out.append((t, T_, False))
        t += T_
    return out


def build_program(NL, LAT, dbg=False, stop_after=None, mixers="ABCD"):
    L = CTX + LAT
    NT = L // 128
    nc = bass.Bass("TRN2", target_bir_lowering=False)
    es0 = ExitStack()
    kb = KB(nc, es0)
    S = kb.S

    def ext(name, shape, dt=F32):
        return T(nc.dram_tensor(name, list(shape), dt, kind="ExternalInput"), name)

    h0 = ext("h0", [2, L, D])
    cT_in = ext("cT", [128, 8, 3])
    ada_w = ext("ada_w", [DEPTH, D, 6 * D])
    ada_bT = ext("ada_bT", [DEPTH, 128, 48])
    ada_b = ext("ada_b", [DEPTH, 6 * D])
    w_in = ext("w_in", [DEPTH, D, D_IN])
    ident_in = ext("ident", [128, 128])
    out_d = T(nc.dram_tensor("out", [2, LAT, D], F32, kind="ExternalOutput"), "out")
    pC_in = ext("pC", [DEPTH, 128, 4, 16])
    w_out_in = ext("w_out", [DEPTH, 2048, D])
    ffn_w_in = ext("ffn_w_in", [DEPTH, D, 2 * D_FF])
    ffn_w_out = ext("ffn_w_out", [DEPTH, D_FF, D])
    lnp_in = ext("lnp", [DEPTH, 4, D])
    pF_in = ext("pF", [DEPTH, 128, 22, 4])
    lbT_in = ext("lbT", [128, DEPTH, 8])
    anw_in = ext("hgrn_norm_w", [DEPTH, 512])
    cA_in = ext("cA", [128, 2312])
    pB_in = ext("pB", [DEPTH, 128, 8, 5])
    rowB_in = ext("rowB", [DEPTH, 552])
    cB_in = ext("cB", [128, 5 * 128 + 2 * 512])
    ropeT_in = ext("ropeT", [2, 128, L])
    cD_in = ext("cD", [128, 4 * 128 + 8])
    ret_logit = ext("ret_decay_logit", [DEPTH, 8])
    lru_wa = ext("lru_wa", [DEPTH, 2, 8, 64, 64])
    lru_wi = ext("lru_wi", [DEPTH, 2, 8, 64, 64])

    kind_s = "ExternalOutput" if dbg else "Internal"
    RC = kb.dram("RC", [2, NCC, L], BF16, kind_s)
    RT = kb.dram("RT", [2, L, NTC], BF16, kind_s)
    MODC = kb.dram("MODC", [128, 8 * 4 * 3], F32, kind_s)
    MODB = kb.dram("MODB", [2, 3, 128, D], F32)
    CC = kb.dram("CC", [2, 2048, L], BF16, kind_s)
    H1 = kb.dram("H1", [2, L, D], F32, kind_s)
    global HRES
    HRES = kb.dram("HRES", [2, L, D], F32, kind_s)
    pieces = [(n0, min(512, L - n0)) for n0 in range(0, L, 512)]
    segs = [(0, CTX), (CTX, L)]

    with ExitStack() as esG:
        ident = kb.sb(esG, "ident", [128, 128], BF16)
        identf = kb.sb(esG, "identf", [128, 128], F32)
        kb.dma("sp", identf[:], ident_in[:, :], "c0", w=[identf])
        kb.V(lambda: nc.vector.tensor_copy(ident[:], identf[:]), r=[identf], w=[ident])
        scT = kb.sb(esG, "scT", [128, 8, 3], F32)
        kb.dma("sp", scT[:], cT_in[:, :, :], "c0", w=[scT])
        kb.A(lambda: nc.scalar.activation(out=scT[:], in_=scT[:], func=AF.Silu), r=[scT], w=[scT])
        modc = kb.sb(esG, "modc", [128, 8, 4, 3], F32)

        _cur = [None]

        def _scope(name):
            if _cur[0] is not None:
                nc.leave_named_scope(_cur[0][0], _cur[0][1], False)
                _cur[0] = None
            if name is not None and os.environ.get("KSCOPES") == "1":
                sid, _ = nc.enter_named_scope(name, False)
                _cur[0] = (name, sid)

        for l in range(NL):
            _scope("P0")
            with ExitStack() as es:
                adw = [kb.sb(es, f"adw{i}", [128, 8, D], F32) for i in range(2)]
                scB = kb.sb(es, "scB", [128, 8, 3, 128], F32)
                for kc in range(8):
                    kb.V(lambda: nc.vector.tensor_copy(scB[:, kc], scT[:, kc, :].unsqueeze(2).to_broadcast([128, 3, 128])),
                         r=[scT], w=[scB])
                mst = [kb.sb(es, f"mst{i}", [128, D], F32) for i in range(2)]
                nmst = 0
                abT = kb.sb(es, "abT", [128, 48], F32)
                abB = kb.sb(es, "abB", [128, D], F32)
                pm = [kb.ps(es, f"pm{i}", [128, 512], F32) for i in range(4)]
                kb.dma("sp", abT[:], ada_bT[l], "p0b", w=[abT])
                npm = 0
                for j in range(6):
                    a = adw[j % 2]
                    kb.dma("sp", a[:], ada_w[l, :, j * D:(j + 1) * D].rearrange("(c p) n -> p c n", p=128),
                           f"p0w{j % 2}", w=[a])
                    if j in (2, 5):
                        jj = 0 if j == 2 else 1
                        kb.dma("sp", abB[:], ada_b[l:l + 1, j * D:(j + 1) * D].to_broadcast([128, D]), "p0b", w=[abB])
                        for v in range(3):
                            ms = mst[nmst % 2]
                            for n in range(2):
                                p = pm[npm % 4]; npm += 1
                                for kc in range(8):
                                    kb.P(lambda: nc.tensor.matmul(p[:], lhsT=scB[:, kc, v, :], rhs=a[:, kc, n * 512:(n + 1) * 512],
                                                                  start=(kc == 0), stop=(kc == 7)), r=[scB, a], w=[p])
                                kb.V(lambda: nc.vector.tensor_tensor(out=ms[:, n * 512:(n + 1) * 512], in0=p[:],
                                                                     in1=abB[:, n * 512:(n + 1) * 512], op=ALU.add),
                                     r=[p, abB], w=[ms])
                            kb.dma("sp", MODB[jj, v], ms[:], f"p0m{nmst % 2}", r=[ms], w=[MODB])
                            nmst += 1
                    else:
                        jj = {0: 0, 1: 1, 3: 2, 4: 3}[j]
                        for dc in range(8):
                            p = pm[npm % 4]; npm += 1
                            for kc in range(8):
                                kb.P(lambda: nc.tensor.matmul(p[:, 0:3], lhsT=a[:, kc, dc * 128:(dc + 1) * 128], rhs=scT[:, kc, :],
                                                              start=(kc == 0), stop=(kc == 7)), r=[scT, a], w=[p])
                            kb.V(lambda: nc.vector.tensor_scalar(out=modc[:, dc, jj, :], in0=p[:, 0:3],
                                                                 scalar1=abT[:, j * 8 + dc:j * 8 + dc + 1],
                                                                 scalar2=(1.0 if jj in (1, 3) else 0.0),
                                                                 op0=ALU.add, op1=ALU.add), r=[p, abT], w=[modc])
                if dbg:
                    kb.dma("sp", MODC[:, :], modc[:].rearrange("p a b c -> p (a b c)"), "dbg", r=[modc], w=[MODC])
            kb.barrier()
            if stop_after == "P0":
                break

            _scope("P1")
            with ExitStack() as es:
                W = kb.sb(es, "W", [128, 8, D_IN + 1024], BF16)
                for kc in range(8):
                    kb.dma("pool", W[:, kc, 0:D_IN], w_in[l, kc * 128:(kc + 1) * 128, :], "p1w", w=[W])
                for kc in range(8):
                    src = W[:, kc, 5136:6160].rearrange("p (h two e) -> p h two e", two=2, e=64)
                    dst = W[:, kc, D_IN:D_IN + 1024].rearrange("p (h two e) -> p h two e", two=2, e=64)
                    kb.V(lambda: nc.vector.tensor_scalar(out=dst[:, :, 0, :], in0=src[:, :, 1, :], scalar1=-1.0, scalar2=None,
                                                         op0=ALU.mult), r=[W], w=[W])
                    kb.G(lambda: nc.gpsimd.tensor_copy(dst[:, :, 1, :], src[:, :, 0, :]), r=[W], w=[W])
                hf = [kb.sb(es, f"hf{i}", [128, 4, D], F32) for i in range(1)]
                hb = [kb.sb(es, f"hb{i}", [128, 4, D], BF16) for i in range(1)]
                hT = [kb.sb(es, f"hT{i}", [128, 8, 512], BF16) for i in range(2)]
                stc = [kb.sb(es, f"stc{i}", [128, 4, 512], BF16) for i in range(2)]
                stt = [kb.sb(es, f"stt{i}", [128, NTC], BF16) for i in range(2)]
                ptr = [kb.ps(es, f"ptr{i}", [128, 512], BF16) for i in range(2)]
                pmm = [kb.ps(es, f"pmm{i}", [128, 512], F32) for i in range(5)]
                blks = [(b, t0, Tn, isc) for b in range(2) for (t0, Tn, isc) in blocks_of(LAT)]

                def load(i):
                    b, t0, Tn, isc = blks[i]
                    nt = Tn // 128
                    kb.dma("sp", hf[0][:, 0:nt, :], h0s[b, t0:t0 + Tn, :].rearrange("(n p) d -> p n d", p=128),
                           "p1h", r=[h0s_t], w=[hf[0]])

                h0s_t = h0 if l == 0 else HRES
                h0s = h0s_t.t
                load(0)
                npt = 0; npp = 0; nst = 0; nstt = 0; nev = 0
                for i, (b, t0, Tn, isc) in enumerate(blks):
                    nt = Tn // 128
                    vec = 2 if isc else b
                    f_, b_, hT_ = hf[0], hb[0], hT[i % 2]
                    for ti in range(nt):
                        if ti % 2 == 0:
                            kb.A(lambda: nc.scalar.copy(b_[:, ti, :], f_[:, ti, :]), r=[f_], w=[b_])
                        else:
                            kb.G(lambda: nc.gpsimd.tensor_copy(b_[:, ti, :], f_[:, ti, :]), r=[f_], w=[b_])
                    if i + 1 < len(blks):
                        load(i + 1)
                    for dc in range(8):
                        p = ptr[npt % 2]; npt += 1
                        for ti in range(nt):
                            kb.P(lambda: nc.tensor.transpose(p[:, ti * 128:(ti + 1) * 128], b_[:, ti, dc * 128:(dc + 1) * 128], ident[:]),
                                 r=[b_, ident], w=[p])
                        kb.V(lambda: nc.vector.tensor_scalar(out=hT_[:, dc, 0:Tn], in0=p[:, 0:Tn],
                                                             scalar1=modc[:, dc, 1, vec:vec + 1], scalar2=modc[:, dc, 0, vec:vec + 1],
                                                             op0=ALU.mult, op1=ALU.add), r=[p, modc], w=[hT_])
                    row = 0
                    for (src, n) in RC_SRC:
                        for ct in range(n // 128):
                            co = src + ct * 128
                            p = pmm[npp % 5]; npp += 1
                            for kc in range(8):
                                kb.P(lambda: nc.tensor.matmul(p[:, 0:Tn], lhsT=W[:, kc, co:co + 128], rhs=hT_[:, kc, 0:Tn],
                                                              start=(kc == 0), stop=(kc == 7)), r=[W, hT_], w=[p])
                            st = stc[nst % 2]
                            q = (row // 128) % 4
                            if nev % 2 == 0:
                                kb.A(lambda: nc.scalar.copy(st[:, q, 0:Tn], p[:, 0:Tn]), r=[p], w=[st])
                            else:
                                kb.V(lambda: nc.vector.tensor_copy(st[:, q, 0:Tn], p[:, 0:Tn]), r=[p], w=[st])
                            nev += 1
                            row += 128
                            if q == 3:
                                kb.dma("sp", RC[b, row - 512:row, t0:t0 + Tn].rearrange("(q p) t -> p q t", p=128), st[:, :, 0:Tn],
                                       f"p1sc{nst % 2}", r=[st], w=[RC])
                                nst += 1
                    for ti in range(nt):
                        st = stt[nstt % 2]
                        col = 0
                        for (src, n) in RT_SRC:
                            p = pmm[npp % 5]; npp += 1
                            for kc in range(8):
                                kb.P(lambda: nc.tensor.matmul(p[:, 0:n], lhsT=hT_[:, kc, ti * 128:(ti + 1) * 128], rhs=W[:, kc, src:src + n],
                                                              start=(kc == 0), stop=(kc == 7)), r=[W, hT_], w=[p])
                            if nev % 2 == 0:
                                kb.A(lambda: nc.scalar.copy(st[:, col:col + n], p[:, 0:n]), r=[p], w=[st])
                            else:
                                kb.V(lambda: nc.vector.tensor_copy(st[:, col:col + n], p[:, 0:n]), r=[p], w=[st])
                            nev += 1
                            col += n
                        kb.dma("sp", RT[b, t0 + ti * 128:t0 + (ti + 1) * 128, :], st[:], f"p1st{nstt % 2}", r=[st], w=[RT])
                        nstt += 1
            kb.barrier()
            if stop_after == "P1":
                break


            _scope("P2C")
            if "C" in mixers:
              with ExitStack() as es:
                pC = kb.sb(es, "pC", [128, 4, 16], F32)
                kb.dma("sp", pC[:], pC_in[l], "c0", w=[pC])
                cco = kb.sb(es, "cco", [128, 4, 6], F32)
                spt = kb.sb(es, "spt", [128, 4, 2], F32)
                kb.A(lambda: nc.scalar.activation(out=spt[:], in_=pC[:, :, 9:11], func=AF.Exp, scale=-1.0), r=[pC], w=[spt])
                kb.A(lambda: nc.scalar.activation(out=spt[:], in_=spt[:], func=AF.Ln, bias=1.0), r=[spt], w=[spt])
                kb.V(lambda: nc.vector.tensor_scalar(out=cco[:, :, 0:2], in0=spt[:], scalar1=-8.0, scalar2=None, op0=ALU.mult), r=[spt], w=[cco])
                kb.V(lambda: nc.vector.tensor_scalar(out=cco[:, :, 2:4], in0=spt[:], scalar1=-16.0, scalar2=None, op0=ALU.mult), r=[spt], w=[cco])
                kb.V(lambda: nc.vector.tensor_scalar(out=cco[:, :, 4:6], in0=spt[:], scalar1=8.0, scalar2=None, op0=ALU.mult), r=[spt], w=[cco])
                WGf = kb.sb(es, "WGf", [128, 16, 128], F32)
                WGb = kb.sb(es, "WGb", [128, 16, 128], BF16)
                kb.V(lambda: nc.vector.memset(WGf[:], 0.0), w=[WGf])
                for gate, wsrc in enumerate((lru_wa, lru_wi)):
                    for d in range(2):
                        for ct in range(4):
                            ix = gate * 8 + d * 4 + ct
                            kb.dma("sp", WGf[0:64, ix, 0:64], wsrc[l, d, 2 * ct], "c0", w=[WGf])
                            kb.dma("sp", WGf[64:128, ix, 64:128], wsrc[l, d, 2 * ct + 1], "c0", w=[WGf])
                kb.V(lambda: nc.vector.tensor_copy(WGb[:], WGf[:]), r=[WGf], w=[WGb])
                Rr = [kb.sb(es, f"Rr{i}", [128, L], F32) for i in range(8)]
                xr = kb.sb(es, "xr", [128, L], BF16)
                gr = kb.sb(es, "gr", [128, L], BF16)
                cxb = kb.sb(es, "cxb", [128, L], BF16)
                ob = kb.sb(es, "ob", [128, L], BF16)
                pg = [kb.ps(es, f"pg{i}", [128, 512], F32) for i in range(4)]
                npg = 0
                for b in range(2):
                    for ct in range(4):
                        kb.dma("sp", xr[:], RC[b, RC_CX + ct * 128:RC_CX + (ct + 1) * 128, :], "c_x", r=[RC], w=[xr])
                        kb.dma("sp", gr[:], RC[b, RC_CG + ct * 128:RC_CG + (ct + 1) * 128, :], "c_g", r=[RC], w=[gr])
                        cx = Rr[0]
                        for (s0, s1) in segs:
                            kb.V(lambda: nc.vector.tensor_scalar(out=cx[:, s0:s1], in0=xr[:, s0:s1], scalar1=pC[:, ct, 2:3], scalar2=pC[:, ct, 4:5],
                                                                 op0=ALU.mult, op1=ALU.add), r=[xr, pC], w=[cx])
                            for (j, off) in ((0, -2), (1, -1), (3, 1)):
                                o0, o1 = max(s0, s0 - off), min(s1, s1 - off)
                                kb.V(lambda: nc.vector.scalar_tensor_tensor(out=cx[:, o0:o1], in0=xr[:, o0 + off:o1 + off], scalar=pC[:, ct, j:j + 1],
                                                                            in1=cx[:, o0:o1], op0=ALU.mult, op1=ALU.add), r=[xr, pC, cx], w=[cx])
                        kb.G(lambda: nc.gpsimd.tensor_copy(cxb[:], cx[:]), r=[cx], w=[cxb])
                        for d in range(2):
                            rr, gg, aa, e2, th = Rr[1], Rr[2], Rr[3], Rr[4], Rr[5]
                            hh = Rr[6 + d]
                            for (n0, wn) in pieces:
                                for gate, dst, bcol in ((0, rr, 5 + d), (1, gg, 7 + d)):
                                    p = pg[npg % 4]; npg += 1
                                    kb.P(lambda: nc.tensor.matmul(p[:, 0:wn], lhsT=WGb[:, gate * 8 + d * 4 + ct, :], rhs=cxb[:, n0:n0 + wn],
                                                                  start=True, stop=True), r=[WGb, cxb], w=[p])
                                    kb.A(lambda: nc.scalar.activation(out=dst[:, n0:n0 + wn], in_=p[:, 0:wn], func=AF.Sigmoid,
                                                                      bias=pC[:, ct, bcol:bcol + 1], scale=1.0), r=[p, pC], w=[dst])
                            kb.A(lambda: nc.scalar.activation(out=aa[:], in_=rr[:], func=AF.Exp, scale=cco[:, ct, d:d + 1]), r=[rr, cco], w=[aa])
                            kb.A(lambda: nc.scalar.activation(out=e2[:], in_=rr[:], func=AF.Exp, scale=cco[:, ct, 2 + d:3 + d]), r=[rr, cco], w=[e2])
                            kb.A(lambda: nc.scalar.activation(out=th[:], in_=rr[:], func=AF.Tanh, scale=cco[:, ct, 4 + d:5 + d]), r=[rr, cco], w=[th])
                            kb.V(lambda: nc.vector.scalar_tensor_tensor(out=e2[:], in0=e2[:], scalar=1.0, in1=th[:], op0=ALU.add, op1=ALU.mult),
                                 r=[e2, th], w=[e2])
                            kb.A(lambda: nc.scalar.activation(out=e2[:], in_=e2[:], func=AF.Sqrt), r=[e2], w=[e2])
                            kb.G(lambda: nc.gpsimd.tensor_tensor(out=th[:], in0=e2[:], in1=gg[:], op=ALU.mult), r=[e2, gg], w=[th])
                            kb.G(lambda: nc.gpsimd.tensor_tensor(out=th[:], in0=th[:], in1=cx[:], op=ALU.mult), r=[th, cx], w=[th])
                            if d == 0:
                                kb.V(lambda: nc.vector.tensor_tensor_scan(out=hh[:], data0=aa[:], data1=th[:], initial=0.0, op0=ALU.mult, op1=ALU.add),
                                     r=[aa, th], w=[hh])
                            else:
                                kb.V(lambda: nc.vector.tensor_tensor_scan(out=hh[:, CTX - 1::-1], data0=aa[:, CTX - 1::-1], data1=th[:, CTX - 1::-1],
                                                                          initial=0.0, op0=ALU.mult, op1=ALU.add), r=[aa, th], w=[hh])
                                kb.V(lambda: nc.vector.tensor_tensor_scan(out=hh[:, L - 1:CTX - 1:-1], data0=aa[:, L - 1:CTX - 1:-1],
                                                                          data1=th[:, L - 1:CTX - 1:-1], initial=hh[:, 0:1],
                                                                          op0=ALU.mult, op1=ALU.add), r=[aa, th, hh], w=[hh])
                        kb.A(lambda: nc.scalar.activation(out=Rr[1][:], in_=gr[:], func=AF.Gelu), r=[gr], w=[Rr[1]])
                        kb.V(lambda: nc.vector.tensor_tensor(out=Rr[6][:], in0=Rr[6][:], in1=Rr[7][:], op=ALU.add), r=[Rr[6], Rr[7]], w=[Rr[6]])
                        kb.G(lambda: nc.gpsimd.tensor_tensor(out=ob[:], in0=Rr[6][:], in1=Rr[1][:], op=ALU.mult), r=[Rr[6], Rr[1]], w=[ob])
                        kb.dma("sp", CC[b, 1024 + ct * 128:1024 + (ct + 1) * 128, :], ob[:], "c_o", r=[ob], w=[CC])
              kb.barrier()

            _scope("P2A")
            if "A" in mixers:
              with ExitStack() as es:
                cA = kb.sb(es, "cA", [128, 2312], F32)
                kb.dma("sp", cA[:], cA_in[:, :], "c0", w=[cA])
                anw = kb.sb(es, "anw", [128, 512], F32)
                kb.dma("sp", anw[:], anw_in[l:l + 1, :].to_broadcast([128, 512]), "c0", w=[anw])
                lbe = kb.sb(es, "lbe", [128, DEPTH, 8], F32)
                kb.dma("sp", lbe[:], lbT_in[:, :, :], "c0", w=[lbe])
                kb.A(lambda: nc.scalar.activation(out=lbe[:], in_=lbe[:], func=AF.Exp), r=[lbe], w=[lbe])
                lbs = kb.sb(es, "lbs", [128, 8], F32)
                lbv = kb.sb(es, "lbv", [128, 3, 8], F32)
                kb.V(lambda: nc.vector.tensor_reduce(out=lbs[:], in_=lbe[:].rearrange("p l c -> p c l"), axis=AX.X, op=ALU.add), r=[lbe], w=[lbs])
                kb.V(lambda: nc.vector.reciprocal(out=lbs[:], in_=lbs[:]), r=[lbs], w=[lbs])
                kb.V(lambda: nc.vector.memset(lbv[:, 0, :], 0.0), w=[lbv])
                for l2 in range(l):
                    kb.V(lambda: nc.vector.tensor_tensor(out=lbv[:, 0, :], in0=lbv[:, 0, :], in1=lbe[:, l2, :], op=ALU.add), r=[lbv, lbe], w=[lbv])
                kb.V(lambda: nc.vector.tensor_tensor(out=lbv[:, 0, :], in0=lbv[:, 0, :], in1=lbs[:], op=ALU.mult), r=[lbv, lbs], w=[lbv])
                kb.V(lambda: nc.vector.tensor_scalar(out=lbv[:, 1, :], in0=lbv[:, 0, :], scalar1=-1.0, scalar2=1.0, op0=ALU.mult, op1=ALU.add), r=[lbv], w=[lbv])
                kb.V(lambda: nc.vector.tensor_scalar(out=lbv[:, 2, :], in0=lbv[:, 1, :], scalar1=-1.0, scalar2=None, op0=ALU.mult), r=[lbv], w=[lbv])
                QTe = [[kb.sb(es, f"QT{e}{h}", [128, L], BF16) for h in range(4)] for e in range(2)]
                for e in range(2):
                    for h in range(4):
                        kb.G(lambda: nc.gpsimd.memset(QTe[e][h][:], 0.0), w=[QTe[e][h]])
                KT = [kb.sb(es, f"KT{h}", [128, L], BF16) for h in range(4)]
                EL = kb.sb(es, "EL", [128, 4, L // 16], F32)
                uraw = kb.sb(es, "uraw", [128, L], BF16)
                qraw = [kb.sb(es, f"qraw{h}", [128, L], BF16) for h in range(1)]
                tp = [kb.sb(es, f"tpa{i}", [128, 1024], F32) for i in range(4)]
                YF = kb.sb(es, "YFa", [128, NT, 512], BF16)
                vt = [kb.sb(es, f"vta{i}", [128, 512], BF16) for i in range(2)]
                gt = [kb.sb(es, f"gta{i}", [128, 512], BF16) for i in range(2)]
                ktoks = [[kb.sb(es, f"ktok{j}{e}", [128, 512], BF16) for e in range(2)] for j in range(2)]
                ktokf = kb.sb(es, "ktokf", [128, 512], BF16)
                scTs = [kb.sb(es, f"scTa{j}", [128, 512], BF16) for j in range(2)]
                Sb = kb.sb(es, "Sba", [128, 512], BF16)
                yf2 = kb.sb(es, "yf2a", [128, 512], F32)
                sq_ = kb.sb(es, "sqa", [128, 512], F32)
                stt_ = kb.sb(es, "stta", [128, 8], F32)
                on_ = kb.sb(es, "ona", [128, 512], BF16)
                ot_ = [kb.sb(es, f"ota{i}", [128, 4, 128], BF16) for i in range(2)]
                pk = kb.ps(es, "pka", [128, 512], BF16)
                pt = kb.ps(es, "pta", [128, 512], BF16)
                psc = kb.ps(es, "psca", [128, 512], F32)
                pos = [kb.ps(es, f"poa{j}", [128, 512], F32) for j in range(2)]
                pS = [kb.ps(es, f"pSa{i}", [128, 512], F32) for i in range(2)]
                p1k = [(n0, min(1024, L - n0)) for n0 in range(0, L, 1024)]
                nvt = 0; nfin = 0; nps = 0
                for b in range(2):
                    for d in (1, 0):
                        for h in range(4):
                            col = d * 4 + h
                            kb.dma("sp", uraw[:], RC[b, RC_U + d * 512 + h * 128:RC_U + d * 512 + (h + 1) * 128, :], "a_u", r=[RC], w=[uraw])
                            if True:
                                kb.dma("sp", qraw[0][:], RC[b, RC_Q + h * 128:RC_Q + (h + 1) * 128, :], "a_q", r=[RC], w=[qraw[0]])
                            for (n0, wn) in p1k:
                                sg, lf, bT, kk = tp
                                kb.A(lambda: nc.scalar.activation(out=sg[:, 0:wn], in_=uraw[:, n0:n0 + wn], func=AF.Sigmoid), r=[uraw], w=[sg])
                                kb.A(lambda: nc.scalar.activation(out=lf[:, 0:wn], in_=sg[:, 0:wn], func=AF.Ln, scale=lbv[:, 1, col:col + 1], bias=lbv[:, 0, col:col + 1]),
                                     r=[sg, lbv], w=[lf])
                                kb.V(lambda: nc.vector.tensor_scalar(out=kk[:, 0:wn], in0=sg[:, 0:wn], scalar1=lbv[:, 2, col:col + 1], scalar2=lbv[:, 1, col:col + 1],
                                                                     op0=ALU.mult, op1=ALU.add), r=[sg, lbv], w=[kk])
                                if d == 0:
                                    kb.V(lambda: nc.vector.tensor_tensor_scan(out=bT[:, 0:wn], data0=cA[:, 0:wn], data1=lf[:, 0:wn], initial=0.0, op0=ALU.mult, op1=ALU.add),
                                         r=[cA, lf], w=[bT])
                                else:
                                    kb.V(lambda: nc.vector.tensor_tensor_scan(out=bT[:, wn - 1::-1], data0=cA[:, 1024 + wn - 1:1023:-1], data1=lf[:, wn - 1::-1], initial=0.0,
                                                                              op0=ALU.mult, op1=ALU.add), r=[cA, lf], w=[bT])
                                kb.A(lambda: nc.scalar.activation(out=sg[:, 0:wn], in_=bT[:, 0:wn], func=AF.Exp), r=[bT], w=[sg])
                                kb.A(lambda: nc.scalar.activation(out=lf[:, 0:wn], in_=bT[:, 0:wn], func=AF.Exp, scale=-1.0), r=[bT], w=[lf])
                                for e in range(2):
                                    kb.G(lambda: nc.gpsimd.tensor_tensor(out=QTe[e][h][:, n0:n0 + wn].rearrange("p (c two t) -> p c two t", two=2, t=16)[:, :, e, :],
                                                                         in0=qraw[0][:, n0:n0 + wn].rearrange("p (c two t) -> p c two t", two=2, t=16)[:, :, e, :],
                                                                         in1=sg[:, 0:wn].rearrange("p (c two t) -> p c two t", two=2, t=16)[:, :, e, :], op=ALU.mult),
                                         r=[qraw[0], sg], w=[QTe[e][h]])
                                kb.G(lambda: nc.gpsimd.tensor_tensor(out=KT[h][:, n0:n0 + wn], in0=kk[:, 0:wn], in1=lf[:, 0:wn], op=ALU.mult), r=[kk, lf], w=[KT[h]])
                                lastpos = 15 if d == 0 else 0
                                kb.V(lambda: nc.vector.tensor_copy(EL[:, h, n0 // 16:(n0 + wn) // 16], sg[:, lastpos:wn:16]), r=[sg], w=[EL])
                        dbgA = int(os.environ.get("DBGA", "0"))
                        if dbgA == 1:
                            continue
                        order = list(range(NT)) if d == 0 else [1, 0] + list(range(NT - 1, 1, -1))
                        MK_d = cA[:, 2048 + d * 128:2048 + (d + 1) * 128]
                        kb.V(lambda: nc.vector.memset(Sb[:], 0.0), w=[Sb])

                        def ldv(i):
                            tt = order[i]
                            kb.dma("sp", vt[(nvt + i) % 2][:], RT[b, tt * 128:(tt + 1) * 128, RT_AI:RT_AI + 512], f"a_v{(nvt + i) % 2}", r=[RT], w=[vt[(nvt + i) % 2]])
                            if d == 0:
                                kb.dma("sp", gt[(nvt + i) % 2][:], RT[b, tt * 128:(tt + 1) * 128, RT_AG:RT_AG + 512], f"a_g{(nvt + i) % 2}", r=[RT], w=[gt[(nvt + i) % 2]])
                        ldv(0)
                        if NT > 1:
                            ldv(1)

                        def front(i):
                            tt = order[i]
                            c0 = tt * 128
                            v_ = vt[(nvt + i) % 2]
                            ktok = ktoks[i % 2]; scT_ = scTs[i % 2]; po = pos[i % 2]
                            for h in range(4):
                                kb.P(lambda: nc.tensor.transpose(pk[:, h * 128:(h + 1) * 128], KT[h][:, c0:c0 + 128], ident[:]), r=[KT[h], ident], w=[pk])
                            kb.A(lambda: nc.scalar.copy(ktokf[:], pk[:]), r=[pk], w=[ktokf])
                            kb.G(lambda: nc.gpsimd.tensor_scalar(out=ktok[0][:], in0=ktokf[:], scalar1=cA[:, 2304:2305], scalar2=None, op0=ALU.mult), r=[ktokf, cA], w=[ktok[0]])
                            kb.G(lambda: nc.gpsimd.tensor_scalar(out=ktok[1][:], in0=ktokf[:], scalar1=cA[:, 2305:2306], scalar2=None, op0=ALU.mult), r=[ktokf, cA], w=[ktok[1]])
                            for h in range(4):
                                for e in range(2):
                                    kb.P(lambda: nc.tensor.matmul(psc[:, h * 128:(h + 1) * 128], lhsT=KT[h][:, c0:c0 + 128], rhs=QTe[e][h][:, c0:c0 + 128], start=(e == 0), stop=(e == 1)),
                                         r=[KT[h], QTe[e][h]], w=[psc])
                            kb.V(lambda: nc.vector.tensor_tensor(out=scT_[:].rearrange("p (h t) -> p h t", h=4), in0=psc[:].rearrange("p (h t) -> p h t", h=4),
                                                                 in1=MK_d.unsqueeze(1).to_broadcast([128, 4, 128]), op=ALU.mult), r=[psc, cA], w=[scT_])
                            for h in range(4):
                                kb.P(lambda: nc.tensor.matmul(po[:, h * 128:(h + 1) * 128], lhsT=scT_[:, h * 128:(h + 1) * 128], rhs=v_[:, h * 128:(h + 1) * 128], start=(h == 0), stop=False),
                                     r=[scT_, v_], w=[po])

                        def back(i):
                            nonlocal nps, nfin
                            tt = order[i]
                            c0 = tt * 128
                            v_ = vt[(nvt + i) % 2]; g_ = gt[(nvt + i) % 2]
                            ktok = ktoks[i % 2]; po = pos[i % 2]
                            for c in (range(8) if d == 0 else range(7, -1, -1)):
                                if dbgA == 2:
                                    break
                                r0 = 32 * (c // 2)
                                e = c % 2
                                gch = tt * 8 + c
                                pS_ = pS[nps % 2]; nps += 1
                                for h in range(4):
                                    if dbgA == 4:
                                        break
                                    kb.P(lambda: nc.tensor.matmul(po[r0:r0 + 32, h * 128:(h + 1) * 128], lhsT=QTe[e][h][:, c0 + r0:c0 + r0 + 32], rhs=Sb[:, h * 128:(h + 1) * 128],
                                                                  start=False, stop=(e == (1 if d == 0 else 0)), tile_position=(0, r0)), r=[QTe[e][h], Sb], w=[po])
                                if dbgA == 3:
                                    continue
                                for h in range(4):
                                    kb.P(lambda: nc.tensor.matmul(pS_[:, h * 128:(h + 1) * 128], lhsT=ident[:], rhs=Sb[:, h * 128:(h + 1) * 128], start=True, stop=False),
                                         r=[ident, Sb], w=[pS_])
                                    kb.P(lambda: nc.tensor.matmul(pS_[:, h * 128:(h + 1) * 128], lhsT=ktok[e][r0:r0 + 32, h * 128:(h + 1) * 128], rhs=v_[r0:r0 + 32, h * 128:(h + 1) * 128],
                                                                  start=False, stop=True, tile_position=(r0, 0)), r=[ktok[e], v_], w=[pS_])
                                kb.V(lambda: nc.vector.tensor_tensor(out=Sb[:].rearrange("p (h e) -> p h e", h=4), in0=pS_[:].rearrange("p (h e) -> p h e", h=4),
                                                                     in1=EL[:, :, gch:gch + 1].to_broadcast([128, 4, 128]), op=ALU.mult), r=[pS_, EL], w=[Sb])
                            if d == 1:
                                kb.A(lambda: nc.scalar.copy(YF[:, tt, :], po[:]), r=[po], w=[YF])
                            else:
                                kb.V(lambda: nc.vector.tensor_tensor(out=yf2[:], in0=po[:], in1=YF[:, tt, :], op=ALU.add), r=[po, YF], w=[yf2])
                                kb.A(lambda: nc.scalar.activation(out=sq_[:], in_=yf2[:], func=AF.Square), r=[yf2], w=[sq_])
                                kb.V(lambda: nc.vector.tensor_reduce(out=stt_[:, 0:4], in_=sq_[:].rearrange("p (h e) -> p h e", h=4), axis=AX.X, op=ALU.add), r=[sq_], w=[stt_])
                                kb.A(lambda: nc.scalar.activation(out=stt_[:, 0:4], in_=stt_[:, 0:4], func=AF.Sqrt, scale=1.0 / 128, bias=EPS), r=[stt_], w=[stt_])
                                kb.V(lambda: nc.vector.reciprocal(out=stt_[:, 0:4], in_=stt_[:, 0:4]), r=[stt_], w=[stt_])
                                kb.G(lambda: nc.gpsimd.tensor_tensor(out=yf2[:].rearrange("p (h e) -> p h e", h=4), in0=yf2[:].rearrange("p (h e) -> p h e", h=4),
                                                                     in1=stt_[:, 0:4].unsqueeze(2).to_broadcast([128, 4, 128]), op=ALU.mult), r=[yf2, stt_], w=[yf2])
                                kb.G(lambda: nc.gpsimd.tensor_tensor(out=yf2[:], in0=yf2[:], in1=anw[:], op=ALU.mult), r=[yf2, anw], w=[yf2])
                                kb.A(lambda: nc.scalar.activation(out=sq_[:], in_=g_[:], func=AF.Silu), r=[g_], w=[sq_])
                                kb.G(lambda: nc.gpsimd.tensor_tensor(out=on_[:], in0=yf2[:], in1=sq_[:], op=ALU.mult), r=[yf2, sq_], w=[on_])
                                for h in range(4):
                                    kb.P(lambda: nc.tensor.transpose(pt[:, h * 128:(h + 1) * 128], on_[:, h * 128:(h + 1) * 128], ident[:]), r=[on_, ident], w=[pt])
                                o_ = ot_[nfin % 2]
                                kb.A(lambda: nc.scalar.copy(o_[:].rearrange("p h t -> p (h t)"), pt[:]), r=[pt], w=[o_])
                                kb.dma("sp", CC[b, 0:512, c0:c0 + 128].rearrange("(h p) t -> p h t", p=128), o_[:], f"a_o{nfin % 2}", r=[o_], w=[CC])
                                nfin += 1

                        front(0)
                        for i in range(NT):
                            if i + 1 < NT:
                                front(i + 1)
                            back(i)
                            if i + 2 < NT:
                                ldv(i + 2)
                        nvt += NT
              kb.barrier()

            _scope("P2B")
            if "B" in mixers:
              with ExitStack() as es:
                pB = kb.sb(es, "pB", [128, 8, 5], F32)
                kb.dma("sp", pB[:], pB_in[l], "c0", w=[pB])
                rowB = kb.sb(es, "rowB", [128, 552], F32)
                kb.dma("sp", rowB[:], rowB_in[l:l + 1, :].to_broadcast([128, 552]), "c0", w=[rowB])
                cB = kb.sb(es, "cB", [128, 5 * 128 + 1024], F32)
                kb.dma("sp", cB[:], cB_in[:, :], "c0", w=[cB])
                mkb = kb.sb(es, "mkb", [128, 2, 512], BF16)
                kb.V(lambda: nc.vector.tensor_copy(mkb[:].rearrange("p a b -> p (a b)"), cB[:, 640:1664]), r=[cB], w=[mkb])
                acoef = kb.sb(es, "acoef", [128, 16], F32)
                kb.A(lambda: nc.scalar.activation(out=acoef[:], in_=rowB[:, 16:32], func=AF.Exp), r=[rowB], w=[acoef])
                kb.V(lambda: nc.vector.tensor_scalar(out=acoef[:], in0=acoef[:], scalar1=-1.0, scalar2=None, op0=ALU.mult), r=[acoef], w=[acoef])
                XB = [kb.sb(es, f"XB{i}", [128, L], BF16) for i in range(8)]
                cvt = kb.sb(es, "cvt", [128, L], F32)
                rawb = kb.sb(es, "rawb", [128, L], BF16)
                YF = kb.sb(es, "YFb", [128, NT, 512], BF16)
                DTr = kb.sb(es, "DTr", [128, NT, 16], BF16)
                DTV = kb.sb(es, "DTV", [128, NT, 16], F32)
                LA = kb.sb(es, "LA", [128, NT, 16], F32)
                R1 = kb.sb(es, "R1", [128, 8, 128], F32)
                R2 = kb.sb(es, "R2", [128, 8, 128], F32)
                rel = kb.sb(es, "rel", [128, 8, 128], F32)
                scTb = kb.sb(es, "scTb", [128, 8, 128], BF16)
                xt_ = kb.sb(es, "xtb", [128, 512], BF16)
                Bt_ = kb.sb(es, "Btb", [128, 256], BF16)
                v_ = kb.sb(es, "vb", [128, 512], BF16)
                vw_ = kb.sb(es, "vwb", [128, 512], BF16)
                eb16 = kb.sb(es, "eb16", [128, 16], F32)
                Sf = kb.sb(es, "Sf", [128, 512], F32)
                Sb = kb.sb(es, "Sb", [128, 512], BF16)
                yf1 = kb.sb(es, "yf1b", [128, 512], F32)
                yf2 = kb.sb(es, "yf2b", [128, 512], F32)
                sq_ = kb.sb(es, "sqb", [128, 512], F32)
                zt = [kb.sb(es, f"zt{i}", [128, 512], BF16) for i in range(2)]
                stt_ = kb.sb(es, "sttb", [128, 8], F32)
                on_ = kb.sb(es, "onb", [128, 512], BF16)
                ot_ = [kb.sb(es, f"otb{i}", [128, 4, 128], BF16) for i in range(2)]
                Dp = [kb.ps(es, f"Dp{i}", [128, 512], F32) for i in range(2)]
                pq = kb.ps(es, "pq", [128, 512], F32)
                px6 = kb.ps(es, "px6", [128, 768], BF16)
                py = kb.ps(es, "pyb", [128, 512], F32)
                pz = kb.ps(es, "pzb", [128, 512], F32)
                pS = kb.ps(es, "pSb", [128, 512], F32)
                nfin = 0; nz = 0
                for b in range(2):
                    for ct in range(8):
                        kb.dma("sp", rawb[:], RC[b, RC_XBC + ct * 128:RC_XBC + (ct + 1) * 128, :], "b_r", r=[RC], w=[rawb])
                        for (s0, s1) in segs:
                            kb.V(lambda: nc.vector.tensor_scalar(out=cvt[:, s0:s1], in0=rawb[:, s0:s1], scalar1=pB[:, ct, 2:3], scalar2=pB[:, ct, 4:5],
                                                                 op0=ALU.mult, op1=ALU.add), r=[rawb, pB], w=[cvt])
                            for (j, off) in ((0, -2), (1, -1), (3, 1)):
                                o0, o1 = max(s0, s0 - off), min(s1, s1 - off)
                                kb.V(lambda: nc.vector.scalar_tensor_tensor(out=cvt[:, o0:o1], in0=rawb[:, o0 + off:o1 + off], scalar=pB[:, ct, j:j + 1],
                                                                            in1=cvt[:, o0:o1], op0=ALU.mult, op1=ALU.add), r=[rawb, pB, cvt], w=[cvt])
                        kb.A(lambda: nc.scalar.activation(out=XB[ct][:], in_=cvt[:], func=AF.Silu), r=[cvt], w=[XB[ct]])
                    for n0 in range(0, NT, 8):
                        n1 = min(NT, n0 + 8)
                        kb.dma("sp", DTr[:, n0:n1, :], RT[b, n0 * 128:n1 * 128, RT_DT:RT_DT + 16].rearrange("(n p) c -> p n c", p=128), "b_dt",
                               r=[RT], w=[DTr], allow_slow_non_contiguous=True)
                    kb.V(lambda: nc.vector.tensor_tensor(out=DTV[:], in0=DTr[:], in1=rowB[:, 0:16].unsqueeze(1).to_broadcast([128, NT, 16]), op=ALU.add),
                         r=[DTr, rowB], w=[DTV])
                    kb.A(lambda: nc.scalar.activation(out=DTV[:], in_=DTV[:], func=AF.Exp), r=[DTV], w=[DTV])
                    kb.A(lambda: nc.scalar.activation(out=DTV[:], in_=DTV[:], func=AF.Ln, bias=1.0), r=[DTV], w=[DTV])
                    kb.V(lambda: nc.vector.tensor_tensor(out=LA[:], in0=DTV[:], in1=acoef[:].unsqueeze(1).to_broadcast([128, NT, 16]), op=ALU.mult),
                         r=[DTV, acoef], w=[LA])
                    for d in (1, 0):
                        order = list(range(NT)) if d == 0 else [1, 0] + list(range(NT - 1, 1, -1))
                        tl = 127 if d == 0 else 0
                        U_d = cB[:, d * 128:(d + 1) * 128]
                        nU_d = cB[:, 256 + d * 128:256 + (d + 1) * 128]
                        ones_ = cB[:, 512:640]
                        kb.V(lambda: nc.vector.memset(Sf[:], 0.0), w=[Sf])
                        kb.G(lambda: nc.gpsimd.memset(Sb[:], 0.0), w=[Sb])
                        for i, tt in enumerate(order):
                            c0 = tt * 128
                            la8 = LA[:, tt, d * 8:(d + 1) * 8]
                            dt8 = DTV[:, tt, d * 8:(d + 1) * 8]
                            if d == 0:
                                z_ = zt[nz % 2]; nz += 1
                                kb.dma("sp", z_[:], RT[b, c0:c0 + 128, RT_BZ:RT_BZ + 512], f"b_z{nz % 2}", r=[RT], w=[z_])
                            kb.V(lambda: nc.vector.tensor_tensor(out=R1[:], in0=U_d.unsqueeze(1).to_broadcast([128, 8, 128]),
                                                                 in1=la8.unsqueeze(2).to_broadcast([128, 8, 128]), op=ALU.mult), r=[cB, LA], w=[R1])
                            kb.G(lambda: nc.gpsimd.tensor_copy(R2[:], la8.unsqueeze(2).to_broadcast([128, 8, 128])), r=[LA], w=[R2])
                            for q in range(2):
                                kb.P(lambda: nc.tensor.matmul(Dp[q][:], lhsT=ones_, rhs=R1[:, 4 * q:4 * q + 4, :].rearrange("p h t -> p (h t)"), start=True, stop=False),
                                     r=[cB, R1], w=[Dp[q]])
                                kb.P(lambda: nc.tensor.matmul(Dp[q][:], lhsT=nU_d, rhs=R2[:, 4 * q:4 * q + 4, :].rearrange("p h t -> p (h t)"), start=False, stop=False),
                                     r=[cB, R2], w=[Dp[q]])
                                kb.P(lambda: nc.tensor.matmul(Dp[q][:], lhsT=ident[:], rhs=mkb[:, d, :], start=False, stop=True), r=[ident, mkb], w=[Dp[q]])
                                kb.A(lambda: nc.scalar.activation(out=rel[:, 4 * q:4 * q + 4, :].rearrange("p h t -> p (h t)"), in_=Dp[q][:], func=AF.Exp),
                                     r=[Dp[q]], w=[rel])
                            for g in range(2):
                                kb.P(lambda: nc.tensor.matmul(pq[:, g * 128:(g + 1) * 128], lhsT=XB[4 + g][:, c0:c0 + 128], rhs=XB[6 + g][:, c0:c0 + 128], start=True, stop=True),
                                     r=[XB[4 + g], XB[6 + g]], w=[pq])
                            kb.P(lambda: nc.tensor.matmul(pq[:, 256:264], lhsT=U_d, rhs=la8, start=True, stop=True), r=[cB, LA], w=[pq])
                            kb.P(lambda: nc.tensor.matmul(pq[:, 264:272], lhsT=ones_, rhs=la8, start=True, stop=True), r=[cB, LA], w=[pq])
                            kb.A(lambda: nc.scalar.activation(out=eb16[:], in_=pq[:, 256:272], func=AF.Exp), r=[pq], w=[eb16])
                            kb.V(lambda: nc.vector.tensor_tensor(out=scTb[:].rearrange("p (g e) t -> p g e t", g=2), in0=rel[:].rearrange("p (g e) t -> p g e t", g=2),
                                                                 in1=pq[:, 0:256].rearrange("p (g t) -> p g t", g=2).unsqueeze(2).to_broadcast([128, 2, 4, 128]), op=ALU.mult),
                                 r=[rel, pq], w=[scTb])
                            for ct in range(4):
                                kb.P(lambda: nc.tensor.transpose(px6[:, ct * 128:(ct + 1) * 128], XB[ct][:, c0:c0 + 128], ident[:]), r=[XB[ct], ident], w=[px6])
                            for g in range(2):
                                kb.P(lambda: nc.tensor.transpose(px6[:, 512 + g * 128:512 + (g + 1) * 128], XB[4 + g][:, c0:c0 + 128], ident[:]), r=[XB[4 + g], ident], w=[px6])
                            kb.A(lambda: nc.scalar.copy(xt_[:], px6[:, 0:512]), r=[px6], w=[xt_])
                            kb.A(lambda: nc.scalar.copy(Bt_[:], px6[:, 512:768]), r=[px6], w=[Bt_])
                            kb.G(lambda: nc.gpsimd.tensor_tensor(out=v_[:].rearrange("p (h e) -> p h e", h=8), in0=xt_[:].rearrange("p (h e) -> p h e", h=8),
                                                                 in1=dt8.unsqueeze(2).to_broadcast([128, 8, 64]), op=ALU.mult), r=[xt_, DTV], w=[v_])
                            for h in range(8):
                                kb.P(lambda: nc.tensor.matmul(py[:, h * 64:(h + 1) * 64], lhsT=scTb[:, h, :], rhs=v_[:, h * 64:(h + 1) * 64], start=True, stop=True),
                                     r=[scTb, v_], w=[py])
                            for g in range(2):
                                kb.P(lambda: nc.tensor.matmul(pz[:, g * 256:(g + 1) * 256], lhsT=XB[6 + g][:, c0:c0 + 128], rhs=Sb[:, g * 256:(g + 1) * 256], start=True, stop=True),
                                     r=[XB[6 + g], Sb], w=[pz])
                            kb.V(lambda: nc.vector.tensor_tensor(out=yf1[:].rearrange("p (h e) -> p h e", h=8), in0=pz[:].rearrange("p (h e) -> p h e", h=8),
                                                                 in1=eb16[:, 0:8].unsqueeze(2).to_broadcast([128, 8, 64]), op=ALU.mult), r=[pz, eb16], w=[yf1])
                            kb.G(lambda: nc.gpsimd.tensor_tensor(out=vw_[:].rearrange("p (h e) -> p h e", h=8), in0=v_[:].rearrange("p (h e) -> p h e", h=8),
                                                                 in1=rel[:, :, tl:tl + 1].to_broadcast([128, 8, 64]), op=ALU.mult), r=[v_, rel], w=[vw_])
                            for g in range(2):
                                kb.P(lambda: nc.tensor.matmul(pS[:, g * 256:(g + 1) * 256], lhsT=Bt_[:, g * 128:(g + 1) * 128], rhs=vw_[:, g * 256:(g + 1) * 256], start=True, stop=True),
                                     r=[Bt_, vw_], w=[pS])
                            kb.G(lambda: nc.gpsimd.tensor_tensor(out=Sf[:].rearrange("p (h e) -> p h e", h=8), in0=Sf[:].rearrange("p (h e) -> p h e", h=8),
                                                                 in1=eb16[:, 8:16].unsqueeze(2).to_broadcast([128, 8, 64]), op=ALU.mult), r=[Sf, eb16], w=[Sf])
                            kb.V(lambda: nc.vector.tensor_tensor(out=Sf[:], in0=Sf[:], in1=pS[:], op=ALU.add), r=[Sf, pS], w=[Sf])
                            kb.A(lambda: nc.scalar.copy(Sb[:], Sf[:]), r=[Sf], w=[Sb])
                            if d == 1:
                                kb.V(lambda: nc.vector.tensor_tensor(out=YF[:, tt, :], in0=yf1[:], in1=py[:], op=ALU.add), r=[yf1, py], w=[YF])
                            else:
                                kb.V(lambda: nc.vector.tensor_tensor(out=yf2[:], in0=yf1[:], in1=py[:], op=ALU.add), r=[yf1, py], w=[yf2])
                                kb.G(lambda: nc.gpsimd.tensor_tensor(out=yf2[:], in0=yf2[:], in1=YF[:, tt, :], op=ALU.add), r=[yf2, YF], w=[yf2])
                                kb.G(lambda: nc.gpsimd.tensor_tensor(out=yf1[:].rearrange("p (h e) -> p h e", h=8), in0=xt_[:].rearrange("p (h e) -> p h e", h=8),
                                                                     in1=rowB[:, 32:40].unsqueeze(2).to_broadcast([128, 8, 64]), op=ALU.mult), r=[xt_, rowB], w=[yf1])
                                kb.G(lambda: nc.gpsimd.tensor_tensor(out=yf2[:], in0=yf2[:], in1=yf1[:], op=ALU.add), r=[yf2, yf1], w=[yf2])
                                kb.A(lambda: nc.scalar.activation(out=sq_[:], in_=z_[:], func=AF.Silu), r=[z_], w=[sq_])
                                kb.V(lambda: nc.vector.tensor_tensor(out=yf2[:], in0=yf2[:], in1=sq_[:], op=ALU.mult), r=[yf2, sq_], w=[yf2])
                                kb.A(lambda: nc.scalar.activation(out=sq_[:], in_=yf2[:], func=AF.Square), r=[yf2], w=[sq_])
                                kb.V(lambda: nc.vector.tensor_reduce(out=stt_[:, 0:2], in_=sq_[:].rearrange("p (g e) -> p g e", g=2), axis=AX.X, op=ALU.add), r=[sq_], w=[stt_])
                                kb.A(lambda: nc.scalar.activation(out=stt_[:, 0:2], in_=stt_[:, 0:2], func=AF.Sqrt, scale=1.0 / 256, bias=EPS), r=[stt_], w=[stt_])
                                kb.V(lambda: nc.vector.reciprocal(out=stt_[:, 0:2], in_=stt_[:, 0:2]), r=[stt_], w=[stt_])
                                kb.G(lambda: nc.gpsimd.tensor_tensor(out=yf2[:].rearrange("p (g e) -> p g e", g=2), in0=yf2[:].rearrange("p (g e) -> p g e", g=2),
                                                                     in1=stt_[:, 0:2].unsqueeze(2).to_broadcast([128, 2, 256]), op=ALU.mult), r=[yf2, stt_], w=[yf2])
                                kb.G(lambda: nc.gpsimd.tensor_tensor(out=on_[:], in0=yf2[:], in1=rowB[:, 40:552], op=ALU.mult), r=[yf2, rowB], w=[on_])
                                for h in range(4):
                                    kb.P(lambda: nc.tensor.transpose(px6[:, h * 128:(h + 1) * 128], on_[:, h * 128:(h + 1) * 128], ident[:]), r=[on_, ident], w=[px6])
                                o_ = ot_[nfin % 2]
                                kb.A(lambda: nc.scalar.copy(o_[:].rearrange("p h t -> p (h t)"), px6[:, 0:512]), r=[px6], w=[o_])
                                kb.dma("sp", CC[b, 512:1024, c0:c0 + 128].rearrange("(h p) t -> p h t", p=128), o_[:], f"b_o{nfin % 2}", r=[o_], w=[CC])
                                nfin += 1
              kb.barrier()

            _scope("P2D")
            if "D" in mixers:
              with ExitStack() as es:
                cD = kb.sb(es, "cD", [128, 520], F32)
                kb.dma("sp", cD[:], cD_in[:, :], "c0", w=[cD])
                rope = kb.sb(es, "rope", [128, 2, L], F32)
                kb.dma("sp", rope[:], ropeT_in.t.rearrange("a p t -> p a t"), "c0", w=[rope])
                lg = kb.sb(es, "lg", [128, 8], F32)
                kb.dma("sp", lg[:], ret_logit[l:l + 1, :].to_broadcast([128, 8]), "c0", w=[lg])
                kb.A(lambda: nc.scalar.activation(out=lg[:], in_=lg[:], func=AF.Exp, scale=-1.0), r=[lg], w=[lg])
                kb.A(lambda: nc.scalar.activation(out=lg[:], in_=lg[:], func=AF.Ln, bias=1.0), r=[lg], w=[lg])
                kb.V(lambda: nc.vector.tensor_scalar(out=lg[:], in0=lg[:], scalar1=-1.0, scalar2=None, op0=ALU.mult), r=[lg], w=[lg])
                Dm = kb.sb(es, "Dm", [128, 2, 4, 128], F32)
                DG = kb.sb(es, "DG", [128, 2, 4, 128], BF16)
                wcol = kb.sb(es, "wcol", [128, 2, 4], F32)
                gcol = kb.sb(es, "gcol", [128, 2, 4], F32)
                g128 = kb.sb(es, "g128", [128, 8], F32)
                kb.A(lambda: nc.scalar.activation(out=g128[:], in_=lg[:], func=AF.Exp, scale=128.0), r=[lg], w=[g128])
                for d in range(2):
                    kb.A(lambda: nc.scalar.activation(out=wcol[:, d, :], in_=lg[:, d * 4:(d + 1) * 4], func=AF.Exp, scale=cD[:, 512 + d:513 + d]),
                         r=[lg, cD], w=[wcol])
                    kb.A(lambda: nc.scalar.activation(out=gcol[:, d, :], in_=lg[:, d * 4:(d + 1) * 4], func=AF.Exp, scale=cD[:, 514 + d:515 + d]),
                         r=[lg, cD], w=[gcol])
                    for h in range(4):
                        kb.A(lambda: nc.scalar.activation(out=Dm[:, d, h, :], in_=cD[:, d * 128:(d + 1) * 128], func=AF.Exp,
                                                          scale=lg[:, d * 4 + h:d * 4 + h + 1]), r=[lg, cD], w=[Dm])
                        kb.V(lambda: nc.vector.tensor_tensor(out=Dm[:, d, h, :], in0=Dm[:, d, h, :], in1=cD[:, 256 + d * 128:256 + (d + 1) * 128],
                                                             op=ALU.mult), r=[Dm, cD], w=[Dm])
                        kb.V(lambda: nc.vector.tensor_scalar(out=DG[:, d, h, :], in0=identf[:], scalar1=g128[:, d * 4 + h:d * 4 + h + 1], scalar2=None,
                                                             op0=ALU.mult), r=[identf, g128], w=[DG])
                kb.V(lambda: nc.vector.tensor_scalar(out=wcol[:], in0=wcol[:], scalar1=float(128 ** -0.5), scalar2=None, op0=ALU.mult), r=[wcol], w=[wcol])
                QR = [kb.sb(es, f"QR{h}", [128, L], BF16) for h in range(4)]
                KR = [kb.sb(es, f"KR{h}", [128, L], BF16) for h in range(4)]
                raw = [kb.sb(es, f"raw{i}", [128, L], BF16) for i in range(2)]
                tmp = [kb.sb(es, f"tmpd{i}", [128, 1024], F32) for i in range(2)]
                YF = kb.sb(es, "YF", [128, NT, 512], BF16)
                Sst = kb.sb(es, "Sst", [128, 4, 128], BF16)
                vt = [kb.sb(es, f"vt{i}", [128, 512], BF16) for i in range(2)]
                gt = [kb.sb(es, f"gt{i}", [128, 512], BF16) for i in range(2)]
                khat = kb.sb(es, "khat", [128, 512], BF16)
                scT_ = kb.sb(es, "scTd", [128, 512], BF16)
                yf1 = kb.sb(es, "yf1", [128, 512], F32)
                yf2 = kb.sb(es, "yf2", [128, 512], F32)
                sq_ = kb.sb(es, "sqd", [128, 512], F32)
                stt_ = kb.sb(es, "sttd", [128, 16], F32)
                on_ = kb.sb(es, "ond", [128, 512], BF16)
                ot_ = [kb.sb(es, f"otd{i}", [128, 4, 128], BF16) for i in range(2)]
                pk = kb.ps(es, "pk", [128, 512], BF16)
                pt = kb.ps(es, "pt", [128, 512], BF16)
                psc = kb.ps(es, "psc", [128, 512], F32)
                py = kb.ps(es, "py", [128, 512], F32)
                pz = kb.ps(es, "pz", [128, 512], F32)
                pS = kb.ps(es, "pS", [128, 512], F32)
                p1k = [(n0, min(1024, L - n0)) for n0 in range(0, L, 1024)]
                nvt = 0; nfin = 0
                for b in range(2):
                    for h in range(4):
                        for (dst, r0, r1) in ((QR[h], RC_DQ, RC_DQR), (KR[h], RC_DK, RC_DKR)):
                            kb.dma("sp", raw[0][:], RC[b, r0 + h * 128:r0 + (h + 1) * 128, :], "d_r0", r=[RC], w=[raw[0]])
                            kb.dma("sp", raw[1][:], RC[b, r1 + h * 128:r1 + (h + 1) * 128, :], "d_r1", r=[RC], w=[raw[1]])
                            for (n0, wn) in p1k:
                                kb.V(lambda: nc.vector.tensor_tensor(out=tmp[0][:, 0:wn], in0=raw[0][:, n0:n0 + wn], in1=rope[:, 0, n0:n0 + wn], op=ALU.mult),
                                     r=[raw[0], rope], w=[tmp[0]])
                                kb.G(lambda: nc.gpsimd.tensor_tensor(out=tmp[1][:, 0:wn], in0=raw[1][:, n0:n0 + wn], in1=rope[:, 1, n0:n0 + wn], op=ALU.mult),
                                     r=[raw[1], rope], w=[tmp[1]])
                                kb.V(lambda: nc.vector.tensor_tensor(out=dst[:, n0:n0 + wn], in0=tmp[0][:, 0:wn], in1=tmp[1][:, 0:wn], op=ALU.add),
                                     r=[tmp[0], tmp[1]], w=[dst])
                    for d in (1, 0):
                        order = list(range(NT)) if d == 0 else [1, 0] + list(range(NT - 1, 1, -1))
                        kb.V(lambda: nc.vector.memset(Sst[:], 0.0), w=[Sst])

                        def ldv(i):
                            tt = order[i]
                            kb.dma("sp", vt[(nvt + i) % 2][:], RT[b, tt * 128:(tt + 1) * 128, RT_DV:RT_DV + 512], f"d_v{(nvt + i) % 2}", r=[RT], w=[vt[(nvt + i) % 2]])
                            if d == 0:
                                kb.dma("sp", gt[(nvt + i) % 2][:], RT[b, tt * 128:(tt + 1) * 128, RT_DG:RT_DG + 512], f"d_g{(nvt + i) % 2}", r=[RT], w=[gt[(nvt + i) % 2]])
                        ldv(0)
                        for i, tt in enumerate(order):
                            c0 = tt * 128
                            if i + 1 < NT:
                                ldv(i + 1)
                            v_ = vt[(nvt + i) % 2]; g_ = gt[(nvt + i) % 2]
                            for h in range(4):
                                kb.P(lambda: nc.tensor.transpose(pk[:, h * 128:(h + 1) * 128], KR[h][:, c0:c0 + 128], ident[:]), r=[KR[h], ident], w=[pk])
                            kb.V(lambda: nc.vector.tensor_tensor(out=khat[:].rearrange("p (h e) -> p h e", h=4), in0=pk[:].rearrange("p (h e) -> p h e", h=4),
                                                                 in1=wcol[:, d, :].unsqueeze(2).to_broadcast([128, 4, 128]), op=ALU.mult), r=[pk, wcol], w=[khat])
                            for h in range(4):
                                kb.P(lambda: nc.tensor.matmul(psc[:, h * 128:(h + 1) * 128], lhsT=KR[h][:, c0:c0 + 128], rhs=QR[h][:, c0:c0 + 128], start=True, stop=True),
                                     r=[KR[h], QR[h]], w=[psc])
                            kb.V(lambda: nc.vector.tensor_tensor(out=scT_[:], in0=psc[:], in1=Dm[:, d].rearrange("p h t -> p (h t)"), op=ALU.mult), r=[psc, Dm], w=[scT_])
                            for h in range(4):
                                kb.P(lambda: nc.tensor.matmul(py[:, h * 128:(h + 1) * 128], lhsT=scT_[:, h * 128:(h + 1) * 128], rhs=v_[:, h * 128:(h + 1) * 128], start=True, stop=True),
                                     r=[scT_, v_], w=[py])
                            for h in range(4):
                                kb.P(lambda: nc.tensor.matmul(pz[:, h * 128:(h + 1) * 128], lhsT=QR[h][:, c0:c0 + 128], rhs=Sst[:, h, :], start=True, stop=True),
                                     r=[QR[h], Sst], w=[pz])
                            for h in range(4):
                                kb.P(lambda: nc.tensor.matmul(pS[:, h * 128:(h + 1) * 128], lhsT=DG[:, d, h, :], rhs=Sst[:, h, :], start=True, stop=False), r=[DG, Sst], w=[pS])
                                kb.P(lambda: nc.tensor.matmul(pS[:, h * 128:(h + 1) * 128], lhsT=khat[:, h * 128:(h + 1) * 128], rhs=v_[:, h * 128:(h + 1) * 128], start=False, stop=True),
                                     r=[khat, v_], w=[pS])
                            kb.A(lambda: nc.scalar.copy(Sst[:].rearrange("p h e -> p (h e)"), pS[:]), r=[pS], w=[Sst])
                            kb.V(lambda: nc.vector.tensor_tensor(out=yf1[:].rearrange("p (h e) -> p h e", h=4), in0=pz[:].rearrange("p (h e) -> p h e", h=4),
                                                                 in1=gcol[:, d, :].unsqueeze(2).to_broadcast([128, 4, 128]), op=ALU.mult), r=[pz, gcol], w=[yf1])
                            if d == 1:
                                kb.V(lambda: nc.vector.tensor_tensor(out=YF[:, tt, :], in0=yf1[:], in1=py[:], op=ALU.add), r=[yf1, py], w=[YF])
                            else:
                                kb.V(lambda: nc.vector.tensor_tensor(out=yf2[:], in0=yf1[:], in1=py[:], op=ALU.add), r=[yf1, py], w=[yf2])
                                kb.G(lambda: nc.gpsimd.tensor_tensor(out=yf2[:], in0=yf2[:], in1=YF[:, tt, :], op=ALU.add), r=[yf2, YF], w=[yf2])
                                y3 = yf2[:].rearrange("p (h e) -> p h e", h=4)
                                kb.V(lambda: nc.vector.tensor_reduce(out=stt_[:, 0:4], in_=y3, axis=AX.X, op=ALU.add), r=[yf2], w=[stt_])
                                kb.A(lambda: nc.scalar.activation(out=sq_[:], in_=yf2[:], func=AF.Square), r=[yf2], w=[sq_])
                                kb.V(lambda: nc.vector.tensor_reduce(out=stt_[:, 4:8], in_=sq_[:].rearrange("p (h e) -> p h e", h=4), axis=AX.X, op=ALU.add), r=[sq_], w=[stt_])
                                kb.V(lambda: nc.vector.tensor_scalar(out=stt_[:, 0:8], in0=stt_[:, 0:8], scalar1=1.0 / 128, scalar2=None, op0=ALU.mult), r=[stt_], w=[stt_])
                                kb.V(lambda: nc.vector.tensor_tensor(out=stt_[:, 8:12], in0=stt_[:, 0:4], in1=stt_[:, 0:4], op=ALU.mult), r=[stt_], w=[stt_])
                                kb.V(lambda: nc.vector.tensor_tensor(out=stt_[:, 8:12], in0=stt_[:, 4:8], in1=stt_[:, 8:12], op=ALU.subtract), r=[stt_], w=[stt_])
                                kb.A(lambda: nc.scalar.activation(out=stt_[:, 8:12], in_=stt_[:, 8:12], func=AF.Sqrt, bias=EPS), r=[stt_], w=[stt_])
                                kb.V(lambda: nc.vector.reciprocal(out=stt_[:, 8:12], in_=stt_[:, 8:12]), r=[stt_], w=[stt_])
                                kb.G(lambda: nc.gpsimd.tensor_tensor(out=y3, in0=y3, in1=stt_[:, 0:4].unsqueeze(2).to_broadcast([128, 4, 128]), op=ALU.subtract), r=[yf2, stt_], w=[yf2])
                                kb.G(lambda: nc.gpsimd.tensor_tensor(out=y3, in0=y3, in1=stt_[:, 8:12].unsqueeze(2).to_broadcast([128, 4, 128]), op=ALU.mult), r=[yf2, stt_], w=[yf2])
                                kb.A(lambda: nc.scalar.activation(out=sq_[:], in_=g_[:], func=AF.Silu), r=[g_], w=[sq_])
                                kb.G(lambda: nc.gpsimd.tensor_tensor(out=on_[:], in0=yf2[:], in1=sq_[:], op=ALU.mult), r=[yf2, sq_], w=[on_])
                                for h in range(4):
                                    kb.P(lambda: nc.tensor.transpose(pt[:, h * 128:(h + 1) * 128], on_[:, h * 128:(h + 1) * 128], ident[:]), r=[on_, ident], w=[pt])
                                o_ = ot_[nfin % 2]
                                kb.A(lambda: nc.scalar.copy(o_[:].rearrange("p h t -> p (h t)"), pt[:]), r=[pt], w=[o_])
                                kb.dma("sp", CC[b, 1536:2048, c0:c0 + 128].rearrange("(h p) t -> p h t", p=128), o_[:], f"d_o{nfin % 2}", r=[o_], w=[CC])
                                nfin += 1
                        nvt += NT
              kb.barrier()

            if stop_after == "P2":
                break
            _scope("P3")
            def ln_epilogue(es_t, pmm2, hres, gb, lng, lnb, o_):
                s_, st6, mv = es_t
                for n in range(2):
                    kb.V(lambda: nc.vector.tensor_tensor(out=s_[:, n * 512:(n + 1) * 512], in0=pmm2[n][:], in1=gb[:, n * 512:(n + 1) * 512], op=ALU.mult),
                         r=[pmm2[n], gb], w=[s_])
                kb.V(lambda: nc.vector.scalar_tensor_tensor(out=s_[:], in0=hres, scalar=float(ALPHA), in1=s_[:], op0=ALU.mult, op1=ALU.add), r=[s_] + hres_r[0], w=[s_])
                for n in range(2):
                    kb.V(lambda: nc.vector.bn_stats(out=st6[:, n, :], in_=s_[:, n * 512:(n + 1) * 512]), r=[s_], w=[st6])
                kb.V(lambda: nc.vector.bn_aggr(out=mv[:, 0:2], in_=st6[:].rearrange("p a b -> p (a b)")), r=[st6], w=[mv])
                kb.A(lambda: nc.scalar.activation(out=mv[:, 2:3], in_=mv[:, 1:2], func=AF.Sqrt, bias=EPS), r=[mv], w=[mv])
                kb.V(lambda: nc.vector.reciprocal(out=mv[:, 2:3], in_=mv[:, 2:3]), r=[mv], w=[mv])
                kb.V(lambda: nc.vector.tensor_scalar(out=s_[:], in0=s_[:], scalar1=mv[:, 0:1], scalar2=mv[:, 2:3], op0=ALU.subtract, op1=ALU.mult), r=[s_, mv], w=[s_])
                kb.G(lambda: nc.gpsimd.tensor_tensor(out=s_[:], in0=s_[:], in1=lng[:], op=ALU.mult), r=[s_, lng], w=[s_])
                kb.G(lambda: nc.gpsimd.tensor_tensor(out=o_[:], in0=s_[:], in1=lnb[:], op=ALU.add), r=[s_, lnb], w=[o_])

            hres_r = [[]]
            with ExitStack() as es:
                Wo = kb.sb(es, "Wo", [128, 16, D], BF16)
                for kc in range(16):
                    kb.dma("pool", Wo[:, kc, :], w_out_in[l, kc * 128:(kc + 1) * 128, :], "p3w", w=[Wo])
                lng = kb.sb(es, "lng", [128, D], F32); lnb = kb.sb(es, "lnb", [128, D], F32)
                kb.dma("sp", lng[:], lnp_in[l, 0:1, :].to_broadcast([128, D]), "c0", w=[lng])
                kb.dma("sp", lnb[:], lnp_in[l, 1:2, :].to_broadcast([128, D]), "c0", w=[lnb])
                gb = [kb.sb(es, f"gb{v}", [128, D], F32) for v in range(3)]
                for v in range(3):
                    kb.dma("sp", gb[v][:], MODB[0, v], "c0", r=[MODB], w=[gb[v]])
                cct = [kb.sb(es, f"cct{i}", [128, 16, 512], BF16) for i in range(2)]
                hrt = [kb.sb(es, f"hrt{i}", [128, D], F32) for i in range(2)]
                s_ = kb.sb(es, "s3", [128, D], F32); st6 = kb.sb(es, "st6", [128, 2, 6], F32); mv = kb.sb(es, "mv3", [128, 4], F32)
                o3 = [kb.sb(es, f"o3{i}", [128, D], F32) for i in range(2)]
                pm3 = [kb.ps(es, f"pm3{i}", [128, 512], F32) for i in range(4)]
                blks = [(b, t0, Tn, isc) for b in range(2) for (t0, Tn, isc) in blocks_of(LAT)]
                hsrc = h0 if l == 0 else HRES

                def ldc(i):
                    b, t0, Tn, isc = blks[i]
                    kb.dma("sp", cct[i % 2][:, :, 0:Tn], CC[b, :, t0:t0 + Tn].rearrange("(k p) t -> p k t", p=128), f"p3c{i % 2}", r=[CC], w=[cct[i % 2]])
                ldc(0)
                ntile = 0
                for i, (b, t0, Tn, isc) in enumerate(blks):
                    if i + 1 < len(blks):
                        ldc(i + 1)
                    vec = 2 if isc else b
                    for ti in range(Tn // 128):
                        c0 = t0 + ti * 128
                        hr = hrt[ntile % 2]; o_ = o3[ntile % 2]
                        kb.dma("sp", hr[:], hsrc[b, c0:c0 + 128, :], f"p3h{ntile % 2}", r=[hsrc], w=[hr])
                        pp = pm3[(ntile % 2) * 2:(ntile % 2) * 2 + 2]
                        for n in range(2):
                            for kc in range(16):
                                kb.P(lambda: nc.tensor.matmul(pp[n][:], lhsT=cct[i % 2][:, kc, ti * 128:(ti + 1) * 128], rhs=Wo[:, kc, n * 512:(n + 1) * 512],
                                                              start=(kc == 0), stop=(kc == 15)), r=[cct[i % 2], Wo], w=[pp[n]])
                        hres_r[0] = [hr]
                        ln_epilogue((s_, st6, mv), pp, hr[:], gb[vec], lng, lnb, o_)
                        kb.dma("sp", H1[b, c0:c0 + 128, :], o_[:], f"p3o{ntile % 2}", r=[o_], w=[H1])
                        ntile += 1
            kb.barrier()
            if stop_after == "P3":
                break

            _scope("P4")
            with ExitStack() as es:
                Wfi = kb.sb(es, "Wfi", [128, 8, 2 * D_FF], BF16)
                for kc in range(8):
                    kb.dma("pool", Wfi[:, kc, :], ffn_w_in[l, kc * 128:(kc + 1) * 128, :], "p4w", w=[Wfi])
                Wfo = kb.sb(es, "Wfo", [128, 22, D], BF16)
                for kc in range(22):
                    kb.dma("pool", Wfo[:, kc, :], ffn_w_out[l, kc * 128:(kc + 1) * 128, :], "p4w2", w=[Wfo])
                pF = kb.sb(es, "pF", [128, 22, 4], F32)
                kb.dma("sp", pF[:], pF_in[l], "c0", w=[pF])
                lng = kb.sb(es, "lng4", [128, D], F32); lnb = kb.sb(es, "lnb4", [128, D], F32)
                kb.dma("sp", lng[:], lnp_in[l, 2:3, :].to_broadcast([128, D]), "c0", w=[lng])
                kb.dma("sp", lnb[:], lnp_in[l, 3:4, :].to_broadcast([128, D]), "c0", w=[lnb])
                gbt = kb.sb(es, "gb4", [128, D], F32)
                h1f = [kb.sb(es, f"h1f{i}", [128, 2, D], F32) for i in range(2)]
                hal = [kb.sb(es, f"hal{i}", [2, D], F32) for i in range(2)]
                halb = kb.sb(es, "halb", [128, D], BF16)
                kb.V(lambda: nc.vector.memset(halb[:], 0.0), w=[halb])
                htmp = kb.sb(es, "htmp", [128, 8, 2], F32)
                hb4 = kb.sb(es, "hb4", [128, 2, D], BF16)
                hT4 = kb.sb(es, "hT4", [128, 8, 258], BF16)
                prodT = kb.sb(es, "prodT", [128, 22, 256], BF16)
                gs = [kb.sb(es, f"gs{i}", [128, 258], F32) for i in range(2)]
                cv = [kb.sb(es, f"cv{i}", [128, 256], F32) for i in range(2)]
                s_ = kb.sb(es, "s4", [128, D], F32); st6 = kb.sb(es, "st64", [128, 2, 6], F32); mv = kb.sb(es, "mv4", [128, 4], F32)
                o4 = [kb.sb(es, f"o4{i}", [128, D], F32) for i in range(2)]
                pT4 = [kb.ps(es, f"pT4{i}", [128, 512], BF16) for i in range(2)]
                pg4 = [kb.ps(es, f"pg4{i}", [128, 512], F32) for i in range(2)]
                pu4 = [kb.ps(es, f"pu4{i}", [128, 512], F32) for i in range(2)]
                pf4 = [kb.ps(es, f"pf4{i}", [128, 512], F32) for i in range(2)]
                blks = [(b, t0, Tn, isc) for b in range(2) for (t0, Tn, isc) in blocks_of(LAT, 256)]

                def ld4(i):
                    b, t0, Tn, isc = blks[i]
                    s0, s1 = (0, CTX) if isc else (CTX, L)
                    kb.dma("sp", h1f[i % 2][:], H1[b, t0:t0 + Tn, :].rearrange("(n p) d -> p n d", p=128), f"p4h{i % 2}", r=[H1], w=[h1f[i % 2]])
                    hl = hal[i % 2]
                    kb.G(lambda: nc.gpsimd.memset(hl[:], 0.0), w=[hl])
                    if t0 > s0:
                        kb.dma("sp", hl[0:1, :], H1[b, t0 - 1:t0, :], f"p4l{i % 2}", r=[H1], w=[hl])
                    if t0 + Tn < s1:
                        kb.dma("sp", hl[1:2, :], H1[b, t0 + Tn:t0 + Tn + 1, :], f"p4l{i % 2}", r=[H1], w=[hl])
                ld4(0)
                npt = 0; nct = 0; ntile = 0
                for i, (b, t0, Tn, isc) in enumerate(blks):
                    assert Tn == 256
                    s0, s1 = (0, CTX) if isc else (CTX, L)
                    vec = 2 if isc else b
                    h1_, hl = h1f[i % 2], hal[i % 2]
                    if i + 1 < len(blks):
                        ld4(i + 1)
                    kb.dma("sp", gbt[:], MODB[1, vec], "p4g", r=[MODB], w=[gbt])
                    kb.A(lambda: nc.scalar.copy(hb4[:, 0, :], h1_[:, 0, :]), r=[h1_], w=[hb4])
                    kb.G(lambda: nc.gpsimd.tensor_copy(hb4[:, 1, :], h1_[:, 1, :]), r=[h1_], w=[hb4])
                    kb.A(lambda: nc.scalar.copy(halb[0:2, :], hl[:]), r=[hl], w=[halb])
                    for dc in range(8):
                        p = pT4[npt % 2]; npt += 1
                        for ti in range(2):
                            kb.P(lambda: nc.tensor.transpose(p[:, ti * 128:(ti + 1) * 128], hb4[:, ti, dc * 128:(dc + 1) * 128], ident[:]), r=[hb4, ident], w=[p])
                        kb.V(lambda: nc.vector.tensor_scalar(out=hT4[:, dc, 0:256], in0=p[:, 0:256], scalar1=modc[:, dc, 3, vec:vec + 1], scalar2=modc[:, dc, 2, vec:vec + 1],
                                                             op0=ALU.mult, op1=ALU.add), r=[p, modc], w=[hT4])
                    ph = pf4[1]
                    for dc in range(8):
                        kb.P(lambda: nc.tensor.matmul(ph[:, dc * 2:(dc + 1) * 2], lhsT=halb[:, dc * 128:(dc + 1) * 128], rhs=ident[:, 0:2], start=True, stop=True),
                             r=[halb, ident], w=[ph])
                    kb.V(lambda: nc.vector.tensor_tensor(out=htmp[:], in0=ph[:, 0:16].rearrange("p (a b) -> p a b", b=2), in1=modc[:, :, 3, vec:vec + 1].to_broadcast([128, 8, 2]), op=ALU.mult),
                         r=[ph, modc], w=[htmp])
                    kb.V(lambda: nc.vector.tensor_tensor(out=hT4[:, :, 256:258], in0=htmp[:], in1=modc[:, :, 2, vec:vec + 1].to_broadcast([128, 8, 2]), op=ALU.add),
                         r=[htmp, modc], w=[hT4])
                    if not (t0 > s0):
                        kb.V(lambda: nc.vector.memset(hT4[:, :, 256:257], 0.0), w=[hT4])
                    if not (t0 + Tn < s1):
                        kb.V(lambda: nc.vector.memset(hT4[:, :, 257:258], 0.0), w=[hT4])
                    for ct in range(22):
                        pg = pg4[nct % 2]; pu = pu4[nct % 2]; g_ = gs[nct % 2]; c_ = cv[nct % 2]; nct += 1
                        for kc in range(8):
                            kb.P(lambda: nc.tensor.matmul(pg[:, 0:258], lhsT=Wfi[:, kc, ct * 128:(ct + 1) * 128], rhs=hT4[:, kc, :], start=(kc == 0), stop=(kc == 7)),
                                 r=[Wfi, hT4], w=[pg])
                        for kc in range(8):
                            kb.P(lambda: nc.tensor.matmul(pu[:, 0:256], lhsT=Wfi[:, kc, D_FF + ct * 128:D_FF + (ct + 1) * 128], rhs=hT4[:, kc, 0:256], start=(kc == 0), stop=(kc == 7)),
                                 r=[Wfi, hT4], w=[pu])
                        kb.A(lambda: nc.scalar.copy(g_[:, 1:257], pg[:, 0:256]), r=[pg], w=[g_])
                        kb.A(lambda: nc.scalar.copy(g_[:, 0:258:257], pg[:, 256:258]), r=[pg], w=[g_])
                        kb.V(lambda: nc.vector.tensor_scalar(out=c_[:], in0=g_[:, 1:257], scalar1=pF[:, ct, 1:2], scalar2=pF[:, ct, 3:4], op0=ALU.mult, op1=ALU.add),
                             r=[g_, pF], w=[c_])
                        kb.V(lambda: nc.vector.scalar_tensor_tensor(out=c_[:], in0=g_[:, 0:256], scalar=pF[:, ct, 0:1], in1=c_[:], op0=ALU.mult, op1=ALU.add), r=[g_, pF, c_], w=[c_])
                        kb.V(lambda: nc.vector.scalar_tensor_tensor(out=c_[:], in0=g_[:, 2:258], scalar=pF[:, ct, 2:3], in1=c_[:], op0=ALU.mult, op1=ALU.add), r=[g_, pF, c_], w=[c_])
                        kb.A(lambda: nc.scalar.activation(out=c_[:], in_=c_[:], func=AF.Gelu), r=[c_], w=[c_])
                        kb.V(lambda: nc.vector.tensor_tensor(out=prodT[:, ct, :], in0=c_[:], in1=pu[:, 0:256], op=ALU.mult), r=[c_, pu], w=[prodT])
                    for ti in range(2):
                        c0 = t0 + ti * 128
                        o_ = o4[ntile % 2]; ntile += 1
                        for n in range(2):
                            for ct in range(22):
                                kb.P(lambda: nc.tensor.matmul(pf4[n][:], lhsT=prodT[:, ct, ti * 128:(ti + 1) * 128], rhs=Wfo[:, ct, n * 512:(n + 1) * 512],
                                                              start=(ct == 0), stop=(ct == 21)), r=[prodT, Wfo], w=[pf4[n]])
                        hres_r[0] = [h1_]
                        ln_epilogue((s_, st6, mv), pf4, h1_[:, ti, :], gbt, lng, lnb, o_)
                        if l < NL - 1:
                            kb.dma("sp", HRES[b, c0:c0 + 128, :], o_[:], f"p4o{ntile % 2}", r=[o_], w=[HRES])
                        if l == NL - 1 and not isc:
                            kb.dma("sp", out_d[b, c0 - CTX:c0 - CTX + 128, :], o_[:], f"p4o{ntile % 2}", r=[o_], w=[out_d])
            kb.barrier()

        _scope(None)
        kb.S.wait_all("sp", [t.b for t in kb.all_dram] + [out_d.b])
    print("program built: n_ins", S.n_ins, "extra waits", S.n_wait)
    return nc


HRES = None


def _rope_tables(LAT):
    GRID_W = 64
    rows = LAT // GRID_W
    row = np.repeat(np.arange(rows), GRID_W)
    col = np.tile(np.arange(GRID_W), rows)
    n_freq = 32
    inv = (10000.0 ** (-np.arange(n_freq, dtype=np.float32) / n_freq)).astype(np.float32)
    ang = np.concatenate([row[:, None].astype(np.float32) * inv, col[:, None].astype(np.float32) * inv], axis=-1)
    ang = np.concatenate([np.zeros((CTX, 64), np.float32), ang], axis=0)
    cos = np.cos(ang).astype(np.float32)
    sin = np.sin(ang).astype(np.float32)
    cosT = np.concatenate([cos, cos], axis=1).T
    sinT = np.concatenate([sin, sin], axis=1).T
    return np.ascontiguousarray(np.stack([cosT, sinT], 0))


def _constD():
    s = np.arange(128)[:, None]
    t = np.arange(128)[None, :]
    sc = np.float32(128 ** -0.5)
    c = np.zeros((128, 520), np.float32)
    c[:, 0:128] = np.maximum(t - s, 0)
    c[:, 128:256] = np.maximum(s - t, 0)
    c[:, 256:384] = (s <= t) * sc
    c[:, 384:512] = (s >= t) * sc
    p = np.arange(128)
    c[:, 512] = 127 - p
    c[:, 513] = p
    c[:, 514] = p + 1
    c[:, 515] = 128 - p
    return c


def _constB():
    j = np.arange(128)[:, None]
    t = np.arange(128)[None, :]
    c = np.zeros((128, 5 * 128 + 1024), np.float32)
    UF = (j <= t).astype(np.float32)
    UB = (j >= t).astype(np.float32)
    c[:, 0:128] = UF
    c[:, 128:256] = UB
    c[:, 256:384] = -UF
    c[:, 384:512] = -UB
    c[:, 512:640] = 1.0
    MF = np.where(j <= t, 0.0, NEG).astype(np.float32)
    MB = np.where(j >= t, 0.0, NEG).astype(np.float32)
    c[:, 640:1152] = np.tile(MF, (1, 4))
    c[:, 1152:1664] = np.tile(MB, (1, 4))
    return c


def _constA():
    c = np.zeros((128, 2312), np.float32)
    t = np.arange(1024)
    c[:, 0:1024] = (t % 16 != 0).astype(np.float32)[None, :]
    c[:, 1024:2048] = (t % 16 != 15).astype(np.float32)[None, :]
    s = np.arange(128)[:, None]
    tt = np.arange(128)[None, :]
    same = (s // 16) == (tt // 16)
    c[:, 2048:2176] = (same & (s <= tt)).astype(np.float32)
    c[:, 2176:2304] = (same & (s >= tt)).astype(np.float32)
    p = np.arange(128)
    c[:, 2304] = ((p // 16) % 2 == 0)
    c[:, 2305] = ((p // 16) % 2 == 1)
    return c


def _shared_inputs(inp, LAT):
    f = lambda a: np.ascontiguousarray(np.asarray(a, dtype=np.float32))
    m = {}
    m["ada_w"] = f(inp["ada_w"])
    m["ada_b"] = f(inp["ada_b"])
    m["ada_bT"] = f(np.asarray(inp["ada_b"]).reshape(DEPTH, 48, 128).transpose(0, 2, 1))
    m["w_in"] = f(inp["w_in"])
    m["ident"] = np.eye(128, dtype=np.float32)
    cm = lambda a, n: np.asarray(a).reshape(DEPTH, n, 128).transpose(0, 2, 1)
    pC = np.zeros((DEPTH, 128, 4, 16), np.float32)
    for j in range(4):
        pC[:, :, :, j] = cm(np.asarray(inp["lru_conv_w"])[:, j, :], 4)
    pC[:, :, :, 4] = cm(inp["lru_conv_b"], 4)
    for d in range(2):
        pC[:, :, :, 5 + d] = cm(np.asarray(inp["lru_ba"])[:, d], 4)
        pC[:, :, :, 7 + d] = cm(np.asarray(inp["lru_bi"])[:, d], 4)
        pC[:, :, :, 9 + d] = cm(np.asarray(inp["lru_lambda"])[:, d], 4)
    m["pC"] = pC
    m["lru_wa"] = f(inp["lru_wa"])
    m["lru_wi"] = f(inp["lru_wi"])
    m["ropeT"] = _rope_tables(LAT)
    m["cD"] = _constD()
    m["ret_decay_logit"] = f(np.asarray(inp["ret_decay_logit"]).reshape(DEPTH, 8))
    pB = np.zeros((DEPTH, 128, 8, 5), np.float32)
    for j in range(4):
        pB[:, :, :, j] = cm(np.asarray(inp["ssm_conv_w"])[:, j, :], 8)
    pB[:, :, :, 4] = cm(inp["ssm_conv_b"], 8)
    m["pB"] = pB
    m["rowB"] = f(np.concatenate([np.asarray(inp["ssm_dt_bias"]).reshape(DEPTH, 16), np.asarray(inp["ssm_a_log"]).reshape(DEPTH, 16),
                                  np.asarray(inp["ssm_d"]), np.asarray(inp["ssm_norm_w"])], axis=1))
    m["cB"] = _constB()
    m["lbT"] = f(np.asarray(inp["hgrn_lb_logits"]).reshape(DEPTH, 2, 4, 128).transpose(3, 0, 1, 2).reshape(128, DEPTH, 8))
    m["hgrn_norm_w"] = f(inp["hgrn_norm_w"])
    m["cA"] = _constA()
    m["w_out"] = f(inp["w_out"])
    m["ffn_w_in"] = f(inp["ffn_w_in"])
    m["ffn_w_out"] = f(inp["ffn_w_out"])
    m["lnp"] = f(np.stack([np.asarray(inp["ln1_g"]), np.asarray(inp["ln1_b"]), np.asarray(inp["ln2_g"]), np.asarray(inp["ln2_b"])], axis=1))
    pF = np.zeros((DEPTH, 128, 22, 4), np.float32)
    for j in range(3):
        pF[:, :, :, j] = cm(np.asarray(inp["ffn_conv_w"])[:, j, :], 22)
    pF[:, :, :, 3] = cm(inp["ffn_conv_b"], 22)
    m["pF"] = pF
    return m


def _core_inputs(inp, shared, core, LAT):
    b0 = 2 * core
    m = dict(shared)
    x = np.asarray(inp["x"])
    ctx = np.asarray(inp["ctx"])
    c = np.asarray(inp["c"])
    m["h0"] = np.ascontiguousarray(np.concatenate([ctx[b0:b0 + 2], x[b0:b0 + 2, :LAT]], axis=1).astype(np.float32))
    cv = np.stack([c[b0], c[b0 + 1], np.asarray(inp["c_ctx"])], 0).astype(np.float32)
    m["cT"] = np.ascontiguousarray(cv.reshape(3, 8, 128).transpose(2, 1, 0))
    return m


def kernel(**inputs):
    LAT = int(np.asarray(inputs["x"]).shape[1])
    NB = int(np.asarray(inputs["x"]).shape[0])
    ncores = NB // 2
    nc = build_program(DEPTH, LAT)
    shared = _shared_inputs(inputs, LAT)
    in_maps = [_core_inputs(inputs, shared, core, LAT) for core in range(ncores)]
    res = run_bass_kernel_spmd(nc, in_maps, core_ids=list(range(ncores)))
    out = np.concatenate([np.asarray(r["out"], dtype=np.float32) for r in res.results], axis=0)
    return out
```

```python
import os
import numpy as np
from contextlib import ExitStack
import concourse.bass as bass
import concourse.mybir as mybir
from concourse.bass_utils import run_bass_kernel_spmd

F32 = mybir.dt.float32
BF16 = mybir.dt.bfloat16
AF = mybir.ActivationFunctionType
ALU = mybir.AluOpType
AX = mybir.AxisListType

D = 1024
CTX = 256
DEPTH = 4
D_IN = 7184
D_FF = 2816
NCC = 5632
NTC = 2576
ALPHA = (2 * DEPTH) ** 0.25
EPS = 1e-6
NEG = -30000.0

RC_SRC = [(0, 512), (1024, 1024), (3072, 1024), (4112, 1024), (5136, 1024), (7184, 1024)]
RC_Q, RC_U, RC_XBC, RC_CX, RC_CG, RC_DQ, RC_DK, RC_DQR, RC_DKR = 0, 512, 1536, 2560, 3072, 3584, 4096, 4608, 5120
RT_SRC = [(512, 512), (2048, 512), (2560, 512), (6160, 512), (6672, 512), (4096, 16)]
RT_AI, RT_AG, RT_BZ, RT_DV, RT_DG, RT_DT = 0, 512, 1024, 1536, 2048, 2560


class Buf:
    __slots__ = ("name", "w", "r")

    def __init__(self, name):
        self.name = name
        self.w = None
        self.r = {}


class Eng:
    def __init__(self, name, obj, sem):
        self.name = name
        self.obj = obj
        self.sem = sem
        self.count = 0
        self.seen = {}


class Sched:
    same_engine_sync = True

    def __init__(self, nc, es):
        self.nc = nc
        self.es = es
        self.E = {}
        for name, obj in (("pe", nc.tensor), ("dve", nc.vector), ("act", nc.scalar),
                          ("pool", nc.gpsimd), ("sp", nc.sync)):
            sem = es.enter_context(nc.semaphore("s_" + name))
            self.E[name] = Eng(name, obj, sem)
        self.sems = {e.name: e.sem for e in self.E.values()}
        self.dma_count = {}
        self.n_wait = 0
        self.n_ins = 0

    def _deps(self, reads, writes):
        deps = {}
        for b in reads:
            if b.w is not None:
                k, v = b.w
                if deps.get(k, 0) < v:
                    deps[k] = v
        for b in writes:
            if b.w is not None:
                k, v = b.w
                if deps.get(k, 0) < v:
                    deps[k] = v
            for (k, v) in b.r.items():
                if deps.get(k, 0) < v:
                    deps[k] = v
        return deps

    def _emit(self, eng, deps, fn):
        need = []
        for k, v in deps.items():
            if eng.seen.get(k, 0) >= v:
                continue
            if k == eng.name and (eng.name == "pe" or not self.same_engine_sync):
                continue
            if k in self.dma_count:
                v = self.dma_count[k]
            need.append((k, v))
            eng.seen[k] = v
        for (k, v) in need[1:]:
            eng.obj.wait_ge(self.sems[k], v)
            self.n_wait += 1
        ins = fn()
        if need:
            k, v = need[0]
            ins._wait_ge(self.sems[k], v)
        self.n_ins += 1
        return ins

    def _mark(self, t, reads, writes):
        k, v = t
        for b in reads:
            if b.r.get(k, 0) < v:
                b.r[k] = v
        for b in writes:
            b.w = t
            b.r = {}

    def op(self, en, fn, reads=(), writes=()):
        eng = self.E[en]
        ins = self._emit(eng, self._deps(reads, writes), fn)
        eng.count += 1
        ins.then_inc(eng.sem, 1)
        self._mark((eng.name, eng.count), reads, writes)
        return ins

    def dma(self, qn, out, in_, semkey, reads=(), writes=(), **kw):
        eng = self.E[qn]
        if semkey not in self.sems:
            self.sems[semkey] = self.es.enter_context(self.nc.semaphore("d_" + semkey))
            self.dma_count[semkey] = 0
        ins = self._emit(eng, self._deps(reads, writes), lambda: eng.obj.dma_start(out=out, in_=in_, **kw))
        self.dma_count[semkey] += 16
        ins.then_inc(self.sems[semkey], 16)
        self._mark((semkey, self.dma_count[semkey]), reads, writes)
        return ins

    def wait_all(self, en, bufs):
        eng = self.E[en]
        deps = self._deps(bufs, bufs)
        for k, v in deps.items():
            if eng.seen.get(k, 0) >= v:
                continue
            eng.obj.wait_ge(self.sems[k], v)
            eng.seen[k] = v


class T:
    __slots__ = ("t", "b")

    def __init__(self, t, name):
        self.t = t
        self.b = Buf(name)

    def __getitem__(self, k):
        return self.t[k]


class KB:
    def __init__(self, nc, es):
        self.nc = nc
        self.S = Sched(nc, es)
        self.uid = 0
        self.all_dram = []

    def sb(self, es, name, shape, dt):
        self.uid += 1
        return T(es.enter_context(self.nc.sbuf_tensor(f"{name}_{self.uid}", list(shape), dt)), name)

    def ps(self, es, name, shape, dt):
        self.uid += 1
        return T(es.enter_context(self.nc.psum_tensor(f"{name}_{self.uid}", list(shape), dt)), name)

    def dram(self, name, shape, dt, kind="Internal"):
        t = T(self.nc.dram_tensor(name, list(shape), dt, kind=kind), name)
        self.all_dram.append(t)
        return t

    def V(self, fn, r=(), w=()):
        return self.S.op("dve", fn, [x.b for x in r], [x.b for x in w])

    def A(self, fn, r=(), w=()):
        return self.S.op("act", fn, [x.b for x in r], [x.b for x in w])

    def G(self, fn, r=(), w=()):
        return self.S.op("pool", fn, [x.b for x in r], [x.b for x in w])

    def P(self, fn, r=(), w=()):
        return self.S.op("pe", fn, [x.b for x in r], [x.b for x in w])

    def dma(self, q, out, in_, key, r=(), w=(), **kw):
        return self.S.dma(q, out, in_, key, [x.b for x in r], [x.b for x in w], **kw)

    def barrier(self):
        S = self.S
        targets = {k: e.count for k, e in S.E.items() if e.count > 0}
        targets.update({k: v for k, v in S.dma_count.items() if v > 0})
        for en, eng in S.E.items():
            for k, v in targets.items():
                if k == en or eng.seen.get(k, 0) >= v:
                    continue
                eng.obj.wait_ge(S.sems[k], v)
                eng.seen[k] = v
                S.n_wait += 1


def seg_tiles(LAT):
    L = CTX + LAT
    return L // 128


def blocks_of(LAT, maxT=512):
    out = [(0, CTX, True)]
    t = CTX
    while t < CTX + LAT:
        T_ = min(maxT, CTX + LAT - t)
        out.append((t, T_, False))
        t += T_
    return out


def build_program(NL, LAT, dbg=False, stop_after=None, mixers="ABCD"):
    L = CTX + LAT
    NT = L // 128
    nc = bass.Bass("TRN2", target_bir_lowering=False)
    es0 = ExitStack()
    kb = KB(nc, es0)
    S = kb.S

    def ext(name, shape, dt=F32):
        return T(nc.dram_tensor(name, list(shape), dt, kind="ExternalInput"), name)

    h0 = ext("h0", [2, L, D])
    cT_in = ext("cT", [128, 8, 3])
    ada_w = ext("ada_w", [DEPTH, D, 6 * D])
    ada_bT = ext("ada_bT", [DEPTH, 128, 48])
    ada_b = ext("ada_b", [DEPTH, 6 * D])
    w_in = ext("w_in", [DEPTH, D, D_IN])
    ident_in = ext("ident", [128, 128])
    out_d = T(nc.dram_tensor("out", [2, LAT, D], F32, kind="ExternalOutput"), "out")
    pC_in = ext("pC", [DEPTH, 128, 4, 16])
    w_out_in = ext("w_out", [DEPTH, 2048, D])
    ffn_w_in = ext("ffn_w_in", [DEPTH, D, 2 * D_FF])
    ffn_w_out = ext("ffn_w_out", [DEPTH, D_FF, D])
    lnp_in = ext("lnp", [DEPTH, 4, D])
    pF_in = ext("pF", [DEPTH, 128, 22, 4])
    lbT_in = ext("lbT", [128, DEPTH, 8])
    anw_in = ext("hgrn_norm_w", [DEPTH, 512])
    cA_in = ext("cA", [128, 2312])
    pB_in = ext("pB", [DEPTH, 128, 8, 5])
    rowB_in = ext("rowB", [DEPTH, 552])
    cB_in = ext("cB", [128, 5 * 128 + 2 * 512])
    ropeT_in = ext("ropeT", [2, 128, L])
    cD_in = ext("cD", [128, 4 * 128 + 8])
    ret_logit = ext("ret_decay_logit", [DEPTH, 8])
    lru_wa = ext("lru_wa", [DEPTH, 2, 8, 64, 64])
    lru_wi = ext("lru_wi", [DEPTH, 2, 8, 64, 64])

    kind_s = "ExternalOutput" if dbg else "Internal"
    RC = kb.dram("RC", [2, NCC, L], BF16, kind_s)
    RT = kb.dram("RT", [2, L, NTC], BF16, kind_s)
    MODC = kb.dram("MODC", [128, 8 * 4 * 3], F32, kind_s)
    MODB = kb.dram("MODB", [2, 3, 128, D], F32)
    CC = kb.dram("CC", [2, 2048, L], BF16, kind_s)
    H1 = kb.dram("H1", [2, L, D], F32, kind_s)
    global HRES
    HRES = kb.dram("HRES", [2, L, D], F32, kind_s)
    pieces = [(n0, min(512, L - n0)) for n0 in range(0, L, 512)]
    segs = [(0, CTX), (CTX, L)]

    with ExitStack() as esG:
        ident = kb.sb(esG, "ident", [128, 128], BF16)
        identf = kb.sb(esG, "identf", [128, 128], F32)
        kb.dma("sp", identf[:], ident_in[:, :], "c0", w=[identf])
        kb.V(lambda: nc.vector.tensor_copy(ident[:], identf[:]), r=[identf], w=[ident])
        scT = kb.sb(esG, "scT", [128, 8, 3], F32)
        kb.dma("sp", scT[:], cT_in[:, :, :], "c0", w=[scT])
        kb.A(lambda: nc.scalar.activation(out=scT[:], in_=scT[:], func=AF.Silu), r=[scT], w=[scT])
        modc = kb.sb(esG, "modc", [128, 8, 4, 3], F32)

        _cur = [None]

        def _scope(name):
            if _cur[0] is not None:
                nc.leave_named_scope(_cur[0][0], _cur[0][1], False)
                _cur[0] = None
            if name is not None and os.environ.get("KSCOPES") == "1":
                sid, _ = nc.enter_named_scope(name, False)
                _cur[0] = (name, sid)

        for l in range(NL):
            _scope("P0")
            with ExitStack() as es:
                adw = [kb.sb(es, f"adw{i}", [128, 8, D], F32) for i in range(2)]
                scB = kb.sb(es, "scB", [128, 8, 3, 128], F32)
                for kc in range(8):
                    kb.V(lambda: nc.vector.tensor_copy(scB[:, kc], scT[:, kc, :].unsqueeze(2).to_broadcast([128, 3, 128])),
                         r=[scT], w=[scB])
                mst = [kb.sb(es, f"mst{i}", [128, D], F32) for i in range(2)]
                nmst = 0
                abT = kb.sb(es, "abT", [128, 48], F32)
                abB = kb.sb(es, "abB", [128, D], F32)
                pm = [kb.ps(es, f"pm{i}", [128, 512], F32) for i in range(4)]
                kb.dma("sp", abT[:], ada_bT[l], "p0b", w=[abT])
                npm = 0
                for j in range(6):
                    a = adw[j % 2]
                    kb.dma("sp", a[:], ada_w[l, :, j * D:(j + 1) * D].rearrange("(c p) n -> p c n", p=128),
                           f"p0w{j % 2}", w=[a])
                    if j in (2, 5):
                        jj = 0 if j == 2 else 1
                        kb.dma("sp", abB[:], ada_b[l:l + 1, j * D:(j + 1) * D].to_broadcast([128, D]), "p0b", w=[abB])
                        for v in range(3):
                            ms = mst[nmst % 2]
                            for n in range(2):
                                p = pm[npm % 4]; npm += 1
                                for kc in range(8):
                                    kb.P(lambda: nc.tensor.matmul(p[:], lhsT=scB[:, kc, v, :], rhs=a[:, kc, n * 512:(n + 1) * 512],
                                                                  start=(kc == 0), stop=(kc == 7)), r=[scB, a], w=[p])
                                kb.V(lambda: nc.vector.tensor_tensor(out=ms[:, n * 512:(n + 1) * 512], in0=p[:],
                                                                     in1=abB[:, n * 512:(n + 1) * 512], op=ALU.add),
                                     r=[p, abB], w=[ms])
                            kb.dma("sp", MODB[jj, v], ms[:], f"p0m{nmst % 2}", r=[ms], w=[MODB])
                            nmst += 1
                    else:
                        jj = {0: 0, 1: 1, 3: 2, 4: 3}[j]
                        for dc in range(8):
                            p = pm[npm % 4]; npm += 1
                            for kc in range(8):
                                kb.P(lambda: nc.tensor.matmul(p[:, 0:3], lhsT=a[:, kc, dc * 128:(dc + 1) * 128], rhs=scT[:, kc, :],
                                                              start=(kc == 0), stop=(kc == 7)), r=[scT, a], w=[p])
                            kb.V(lambda: nc.vector.tensor_scalar(out=modc[:, dc, jj, :], in0=p[:, 0:3],
                                                                 scalar1=abT[:, j * 8 + dc:j * 8 + dc + 1],
                                                                 scalar2=(1.0 if jj in (1, 3) else 0.0),
                                                                 op0=ALU.add, op1=ALU.add), r=[p, abT], w=[modc])
                if dbg:
                    kb.dma("sp", MODC[:, :], modc[:].rearrange("p a b c -> p (a b c)"), "dbg", r=[modc], w=[MODC])
            kb.barrier()
            if stop_after == "P0":
                break

            _scope("P1")
            with ExitStack() as es:
                W = kb.sb(es, "W", [128, 8, D_IN + 1024], BF16)
                for kc in range(8):
                    kb.dma("pool", W[:, kc, 0:D_IN], w_in[l, kc * 128:(kc + 1) * 128, :], "p1w", w=[W])
                for kc in range(8):
                    src = W[:, kc, 5136:6160].rearrange("p (h two e) -> p h two e", two=2, e=64)
                    dst = W[:, kc, D_IN:D_IN + 1024].rearrange("p (h two e) -> p h two e", two=2, e=64)
                    kb.V(lambda: nc.vector.tensor_scalar(out=dst[:, :, 0, :], in0=src[:, :, 1, :], scalar1=-1.0, scalar2=None,
                                                         op0=ALU.mult), r=[W], w=[W])
                    kb.G(lambda: nc.gpsimd.tensor_copy(dst[:, :, 1, :], src[:, :, 0, :]), r=[W], w=[W])
                hf = [kb.sb(es, f"hf{i}", [128, 4, D], F32) for i in range(1)]
                hb = [kb.sb(es, f"hb{i}", [128, 4, D], BF16) for i in range(1)]
                hT = [kb.sb(es, f"hT{i}", [128, 8, 512], BF16) for i in range(2)]
                stc = [kb.sb(es, f"stc{i}", [128, 4, 512], BF16) for i in range(2)]
                stt = [kb.sb(es, f"stt{i}", [128, NTC], BF16) for i in range(2)]
                ptr = [kb.ps(es, f"ptr{i}", [128, 512], BF16) for i in range(2)]
                pmm = [kb.ps(es, f"pmm{i}", [128, 512], F32) for i in range(5)]
                blks = [(b, t0, Tn, isc) for b in range(2) for (t0, Tn, isc) in blocks_of(LAT)]

                def load(i):
                    b, t0, Tn, isc = blks[i]
                    nt = Tn // 128
                    kb.dma("sp", hf[0][:, 0:nt, :], h0s[b, t0:t0 + Tn, :].rearrange("(n p) d -> p n d", p=128),
                           "p1h", r=[h0s_t], w=[hf[0]])

                h0s_t = h0 if l == 0 else HRES
                h0s = h0s_t.t
                load(0)
                npt = 0; npp = 0; nst = 0; nstt = 0; nev = 0
                for i, (b, t0, Tn, isc) in enumerate(blks):
                    nt = Tn // 128
                    vec = 2 if isc else b
                    f_, b_, hT_ = hf[0], hb[0], hT[i % 2]
                    for ti in range(nt):
                        if ti % 2 == 0:
                            kb.A(lambda: nc.scalar.copy(b_[:, ti, :], f_[:, ti, :]), r=[f_], w=[b_])
                        else:
                            kb.G(lambda: nc.gpsimd.tensor_copy(b_[:, ti, :], f_[:, ti, :]), r=[f_], w=[b_])
                    if i + 1 < len(blks):
                        load(i + 1)
                    for dc in range(8):
                        p = ptr[npt % 2]; npt += 1
                        for ti in range(nt):
                            kb.P(lambda: nc.tensor.transpose(p[:, ti * 128:(ti + 1) * 128], b_[:, ti, dc * 128:(dc + 1) * 128], ident[:]),
                                 r=[b_, ident], w=[p])
                        kb.V(lambda: nc.vector.tensor_scalar(out=hT_[:, dc, 0:Tn], in0=p[:, 0:Tn],
                                                             scalar1=modc[:, dc, 1, vec:vec + 1], scalar2=modc[:, dc, 0, vec:vec + 1],
                                                             op0=ALU.mult, op1=ALU.add), r=[p, modc], w=[hT_])
                    row = 0
                    for (src, n) in RC_SRC:
                        for ct in range(n // 128):
                            co = src + ct * 128
                            p = pmm[npp % 5]; npp += 1
                            for kc in range(8):
                                kb.P(lambda: nc.tensor.matmul(p[:, 0:Tn], lhsT=W[:, kc, co:co + 128], rhs=hT_[:, kc, 0:Tn],
                                                              start=(kc == 0), stop=(kc == 7)), r=[W, hT_], w=[p])
                            st = stc[nst % 2]
                            q = (row // 128) % 4
                            if nev % 2 == 0:
                                kb.A(lambda: nc.scalar.copy(st[:, q, 0:Tn], p[:, 0:Tn]), r=[p], w=[st])
                            else:
                                kb.V(lambda: nc.vector.tensor_copy(st[:, q, 0:Tn], p[:, 0:Tn]), r=[p], w=[st])
                            nev += 1
                            row += 128
                            if q == 3:
                                kb.dma("sp", RC[b, row - 512:row, t0:t0 + Tn].rearrange("(q p) t -> p q t", p=128), st[:, :, 0:Tn],
                                       f"p1sc{nst % 2}", r=[st], w=[RC])
                                nst += 1
                    for ti in range(nt):
                        st = stt[nstt % 2]
                        col = 0
                        for (src, n) in RT_SRC:
                            p = pmm[npp % 5]; npp += 1
                            for kc in range(8):
                                kb.P(lambda: nc.tensor.matmul(p[:, 0:n], lhsT=hT_[:, kc, ti * 128:(ti + 1) * 128], rhs=W[:, kc, src:src + n],
                                                              start=(kc == 0), stop=(kc == 7)), r=[W, hT_], w=[p])
                            if nev % 2 == 0:
                                kb.A(lambda: nc.scalar.copy(st[:, col:col + n], p[:, 0:n]), r=[p], w=[st])
                            else:
                                kb.V(lambda: nc.vector.tensor_copy(st[:, col:col + n], p[:, 0:n]), r=[p], w=[st])
                            nev += 1
                            col += n
                        kb.dma("sp", RT[b, t0 + ti * 128:t0 + (ti + 1) * 128, :], st[:], f"p1st{nstt % 2}", r=[st], w=[RT])
                        nstt += 1
            kb.barrier()
            if stop_after == "P1":
                break


            _scope("P2C")
            if "C" in mixers:
              with ExitStack() as es:
                pC = kb.sb(es, "pC", [128, 4, 16], F32)
                kb.dma("sp", pC[:], pC_in[l], "c0", w=[pC])
                cco = kb.sb(es, "cco", [128, 4, 6], F32)
                spt = kb.sb(es, "spt", [128, 4, 2], F32)
                kb.A(lambda: nc.scalar.activation(out=spt[:], in_=pC[:, :, 9:11], func=AF.Exp, scale=-1.0), r=[pC], w=[spt])
                kb.A(lambda: nc.scalar.activation(out=spt[:], in_=spt[:], func=AF.Ln, bias=1.0), r=[spt], w=[spt])
                kb.V(lambda: nc.vector.tensor_scalar(out=cco[:, :, 0:2], in0=spt[:], scalar1=-8.0, scalar2=None, op0=ALU.mult), r=[spt], w=[cco])
                kb.V(lambda: nc.vector.tensor_scalar(out=cco[:, :, 2:4], in0=spt[:], scalar1=-16.0, scalar2=None, op0=ALU.mult), r=[spt], w=[cco])
                kb.V(lambda: nc.vector.tensor_scalar(out=cco[:, :, 4:6], in0=spt[:], scalar1=8.0, scalar2=None, op0=ALU.mult), r=[spt], w=[cco])
                WGf = kb.sb(es, "WGf", [128, 16, 128], F32)
                WGb = kb.sb(es, "WGb", [128, 16, 128], BF16)
                kb.V(lambda: nc.vector.memset(WGf[:], 0.0), w=[WGf])
                for gate, wsrc in enumerate((lru_wa, lru_wi)):
                    for d in range(2):
                        for ct in range(4):
                            ix = gate * 8 + d * 4 + ct
                            kb.dma("sp", WGf[0:64, ix, 0:64], wsrc[l, d, 2 * ct], "c0", w=[WGf])
                            kb.dma("sp", WGf[64:128, ix, 64:128], wsrc[l, d, 2 * ct + 1], "c0", w=[WGf])
                kb.V(lambda: nc.vector.tensor_copy(WGb[:], WGf[:]), r=[WGf], w=[WGb])
                Rr = [kb.sb(es, f"Rr{i}", [128, L], F32) for i in range(8)]
                xr = kb.sb(es, "xr", [128, L], BF16)
                gr = kb.sb(es, "gr", [128, L], BF16)
                cxb = kb.sb(es, "cxb", [128, L], BF16)
                ob = kb.sb(es, "ob", [128, L], BF16)
                pg = [kb.ps(es, f"pg{i}", [128, 512], F32) for i in range(4)]
                npg = 0
                for b in range(2):
                    for ct in range(4):
                        kb.dma("sp", xr[:], RC[b, RC_CX + ct * 128:RC_CX + (ct + 1) * 128, :], "c_x", r=[RC], w=[xr])
                        kb.dma("sp", gr[:], RC[b, RC_CG + ct * 128:RC_CG + (ct + 1) * 128, :], "c_g", r=[RC], w=[gr])
                        cx = Rr[0]
                        for (s0, s1) in segs:
                            kb.V(lambda: nc.vector.tensor_scalar(out=cx[:, s0:s1], in0=xr[:, s0:s1], scalar1=pC[:, ct, 2:3], scalar2=pC[:, ct, 4:5],
                                                                 op0=ALU.mult, op1=ALU.add), r=[xr, pC], w=[cx])
                            for (j, off) in ((0, -2), (1, -1), (3, 1)):
                                o0, o1 = max(s0, s0 - off), min(s1, s1 - off)
                                kb.V(lambda: nc.vector.scalar_tensor_tensor(out=cx[:, o0:o1], in0=xr[:, o0 + off:o1 + off], scalar=pC[:, ct, j:j + 1],
                                                                            in1=cx[:, o0:o1], op0=ALU.mult, op1=ALU.add), r=[xr, pC, cx], w=[cx])
                        kb.G(lambda: nc.gpsimd.tensor_copy(cxb[:], cx[:]), r=[cx], w=[cxb])
                        for d in range(2):
                            rr, gg, aa, e2, th = Rr[1], Rr[2], Rr[3], Rr[4], Rr[5]
                            hh = Rr[6 + d]
                            for (n0, wn) in pieces:
                                for gate, dst, bcol in ((0, rr, 5 + d), (1, gg, 7 + d)):
                                    p = pg[npg % 4]; npg += 1
                                    kb.P(lambda: nc.tensor.matmul(p[:, 0:wn], lhsT=WGb[:, gate * 8 + d * 4 + ct, :], rhs=cxb[:, n0:n0 + wn],
                                                                  start=True, stop=True), r=[WGb, cxb], w=[p])
                                    kb.A(lambda: nc.scalar.activation(out=dst[:, n0:n0 + wn], in_=p[:, 0:wn], func=AF.Sigmoid,
                                                                      bias=pC[:, ct, bcol:bcol + 1], scale=1.0), r=[p, pC], w=[dst])
                            kb.A(lambda: nc.scalar.activation(out=aa[:], in_=rr[:], func=AF.Exp, scale=cco[:, ct, d:d + 1]), r=[rr, cco], w=[aa])
                            kb.A(lambda: nc.scalar.activation(out=e2[:], in_=rr[:], func=AF.Exp, scale=cco[:, ct, 2 + d:3 + d]), r=[rr, cco], w=[e2])
                            kb.A(lambda: nc.scalar.activation(out=th[:], in_=rr[:], func=AF.Tanh, scale=cco[:, ct, 4 + d:5 + d]), r=[rr, cco], w=[th])
                            kb.V(lambda: nc.vector.scalar_tensor_tensor(out=e2[:], in0=e2[:], scalar=1.0, in1=th[:], op0=ALU.add, op1=ALU.mult),
                                 r=[e2, th], w=[e2])
                            kb.A(lambda: nc.scalar.activation(out=e2[:], in_=e2[:], func=AF.Sqrt), r=[e2], w=[e2])
                            kb.G(lambda: nc.gpsimd.tensor_tensor(out=th[:], in0=e2[:], in1=gg[:], op=ALU.mult), r=[e2, gg], w=[th])
                            kb.G(lambda: nc.gpsimd.tensor_tensor(out=th[:], in0=th[:], in1=cx[:], op=ALU.mult), r=[th, cx], w=[th])
                            if d == 0:
                                kb.V(lambda: nc.vector.tensor_tensor_scan(out=hh[:], data0=aa[:], data1=th[:], initial=0.0, op0=ALU.mult, op1=ALU.add),
                                     r=[aa, th], w=[hh])
                            else:
                                kb.V(lambda: nc.vector.tensor_tensor_scan(out=hh[:, CTX - 1::-1], data0=aa[:, CTX - 1::-1], data1=th[:, CTX - 1::-1],
                                                                          initial=0.0, op0=ALU.mult, op1=ALU.add), r=[aa, th], w=[hh])
                                kb.V(lambda: nc.vector.tensor_tensor_scan(out=hh[:, L - 1:CTX - 1:-1], data0=aa[:, L - 1:CTX - 1:-1],
                                                                          data1=th[:, L - 1:CTX - 1:-1], initial=hh[:, 0:1],
                                                                          op0=ALU.mult, op1=ALU.add), r=[aa, th, hh], w=[hh])
                        kb.A(lambda: nc.scalar.activation(out=Rr[1][:], in_=gr[:], func=AF.Gelu), r=[gr], w=[Rr[1]])
                        kb.V(lambda: nc.vector.tensor_tensor(out=Rr[6][:], in0=Rr[6][:], in1=Rr[7][:], op=ALU.add), r=[Rr[6], Rr[7]], w=[Rr[6]])
                        kb.G(lambda: nc.gpsimd.tensor_tensor(out=ob[:], in0=Rr[6][:], in1=Rr[1][:], op=ALU.mult), r=[Rr[6], Rr[1]], w=[ob])
                        kb.dma("sp", CC[b, 1024 + ct * 128:1024 + (ct + 1) * 128, :], ob[:], "c_o", r=[ob], w=[CC])
              kb.barrier()

            _scope("P2A")
            if "A" in mixers:
              with ExitStack() as es:
                cA = kb.sb(es, "cA", [128, 2312], F32)
                kb.dma("sp", cA[:], cA_in[:, :], "c0", w=[cA])
                anw = kb.sb(es, "anw", [128, 512], F32)
                kb.dma("sp", anw[:], anw_in[l:l + 1, :].to_broadcast([128, 512]), "c0", w=[anw])
                lbe = kb.sb(es, "lbe", [128, DEPTH, 8], F32)
                kb.dma("sp", lbe[:], lbT_in[:, :, :], "c0", w=[lbe])
                kb.A(lambda: nc.scalar.activation(out=lbe[:], in_=lbe[:], func=AF.Exp), r=[lbe], w=[lbe])
                lbs = kb.sb(es, "lbs", [128, 8], F32)
                lbv = kb.sb(es, "lbv", [128, 3, 8], F32)
                kb.V(lambda: nc.vector.tensor_reduce(out=lbs[:], in_=lbe[:].rearrange("p l c -> p c l"), axis=AX.X, op=ALU.add), r=[lbe], w=[lbs])
                kb.V(lambda: nc.vector.reciprocal(out=lbs[:], in_=lbs[:]), r=[lbs], w=[lbs])
                kb.V(lambda: nc.vector.memset(lbv[:, 0, :], 0.0), w=[lbv])
                for l2 in range(l):
                    kb.V(lambda: nc.vector.tensor_tensor(out=lbv[:, 0, :], in0=lbv[:, 0, :], in1=lbe[:, l2, :], op=ALU.add), r=[lbv, lbe], w=[lbv])
                kb.V(lambda: nc.vector.tensor_tensor(out=lbv[:, 0, :], in0=lbv[:, 0, :], in1=lbs[:], op=ALU.mult), r=[lbv, lbs], w=[lbv])
                kb.V(lambda: nc.vector.tensor_scalar(out=lbv[:, 1, :], in0=lbv[:, 0, :], scalar1=-1.0, scalar2=1.0, op0=ALU.mult, op1=ALU.add), r=[lbv], w=[lbv])
                kb.V(lambda: nc.vector.tensor_scalar(out=lbv[:, 2, :], in0=lbv[:, 1, :], scalar1=-1.0, scalar2=None, op0=ALU.mult), r=[lbv], w=[lbv])
                QTe = [[kb.sb(es, f"QT{e}{h}", [128, L], BF16) for h in range(4)] for e in range(2)]
                for e in range(2):
                    for h in range(4):
                        kb.G(lambda: nc.gpsimd.memset(QTe[e][h][:], 0.0), w=[QTe[e][h]])
                KT = [kb.sb(es, f"KT{h}", [128, L], BF16) for h in range(4)]
                EL = kb.sb(es, "EL", [128, 4, L // 16], F32)
                uraw = kb.sb(es, "uraw", [128, L], BF16)
                qraw = [kb.sb(es, f"qraw{h}", [128, L], BF16) for h in range(1)]
                tp = [kb.sb(es, f"tpa{i}", [128, 1024], F32) for i in range(4)]
                YF = kb.sb(es, "YFa", [128, NT, 512], BF16)
                vt = [kb.sb(es, f"vta{i}", [128, 512], BF16) for i in range(2)]
                gt = [kb.sb(es, f"gta{i}", [128, 512], BF16) for i in range(2)]
                ktoks = [[kb.sb(es, f"ktok{j}{e}", [128, 512], BF16) for e in range(2)] for j in range(2)]
                ktokf = kb.sb(es, "ktokf", [128, 512], BF16)
                scTs = [kb.sb(es, f"scTa{j}", [128, 512], BF16) for j in range(2)]
                Sb = kb.sb(es, "Sba", [128, 512], BF16)
                yf2 = kb.sb(es, "yf2a", [128, 512], F32)
                sq_ = kb.sb(es, "sqa", [128, 512], F32)
                stt_ = kb.sb(es, "stta", [128, 8], F32)
                on_ = kb.sb(es, "ona", [128, 512], BF16)
                ot_ = [kb.sb(es, f"ota{i}", [128, 4, 128], BF16) for i in range(2)]
                pk = kb.ps(es, "pka", [128, 512], BF16)
                pt = kb.ps(es, "pta", [128, 512], BF16)
                psc = kb.ps(es, "psca", [128, 512], F32)
                pos = [kb.ps(es, f"poa{j}", [128, 512], F32) for j in range(2)]
                pS = [kb.ps(es, f"pSa{i}", [128, 512], F32) for i in range(2)]
                p1k = [(n0, min(1024, L - n0)) for n0 in range(0, L, 1024)]
                nvt = 0; nfin = 0; nps = 0
                for b in range(2):
                    for d in (1, 0):
                        for h in range(4):
                            col = d * 4 + h
                            kb.dma("sp", uraw[:], RC[b, RC_U + d * 512 + h * 128:RC_U + d * 512 + (h + 1) * 128, :], "a_u", r=[RC], w=[uraw])
                            if True:
                                kb.dma("sp", qraw[0][:], RC[b, RC_Q + h * 128:RC_Q + (h + 1) * 128, :], "a_q", r=[RC], w=[qraw[0]])
                            for (n0, wn) in p1k:
                                sg, lf, bT, kk = tp
                                kb.A(lambda: nc.scalar.activation(out=sg[:, 0:wn], in_=uraw[:, n0:n0 + wn], func=AF.Sigmoid), r=[uraw], w=[sg])
                                kb.A(lambda: nc.scalar.activation(out=lf[:, 0:wn], in_=sg[:, 0:wn], func=AF.Ln, scale=lbv[:, 1, col:col + 1], bias=lbv[:, 0, col:col + 1]),
                                     r=[sg, lbv], w=[lf])
                                kb.V(lambda: nc.vector.tensor_scalar(out=kk[:, 0:wn], in0=sg[:, 0:wn], scalar1=lbv[:, 2, col:col + 1], scalar2=lbv[:, 1, col:col + 1],
                                                                     op0=ALU.mult, op1=ALU.add), r=[sg, lbv], w=[kk])
                                if d == 0:
                                    kb.V(lambda: nc.vector.tensor_tensor_scan(out=bT[:, 0:wn], data0=cA[:, 0:wn], data1=lf[:, 0:wn], initial=0.0, op0=ALU.mult, op1=ALU.add),
                                         r=[cA, lf], w=[bT])
                                else:
                                    kb.V(lambda: nc.vector.tensor_tensor_scan(out=bT[:, wn - 1::-1], data0=cA[:, 1024 + wn - 1:1023:-1], data1=lf[:, wn - 1::-1], initial=0.0,
                                                                              op0=ALU.mult, op1=ALU.add), r=[cA, lf], w=[bT])
                                kb.A(lambda: nc.scalar.activation(out=sg[:, 0:wn], in_=bT[:, 0:wn], func=AF.Exp), r=[bT], w=[sg])
                                kb.A(lambda: nc.scalar.activation(out=lf[:, 0:wn], in_=bT[:, 0:wn], func=AF.Exp, scale=-1.0), r=[bT], w=[lf])
                                for e in range(2):
                                    kb.G(lambda: nc.gpsimd.tensor_tensor(out=QTe[e][h][:, n0:n0 + wn].rearrange("p (c two t) -> p c two t", two=2, t=16)[:, :, e, :],
                                                                         in0=qraw[0][:, n0:n0 + wn].rearrange("p (c two t) -> p c two t", two=2, t=16)[:, :, e, :],
                                                                         in1=sg[:, 0:wn].rearrange("p (c two t) -> p c two t", two=2, t=16)[:, :, e, :], op=ALU.mult),
                                         r=[qraw[0], sg], w=[QTe[e][h]])
                                kb.G(lambda: nc.gpsimd.tensor_tensor(out=KT[h][:, n0:n0 + wn], in0=kk[:, 0:wn], in1=lf[:, 0:wn], op=ALU.mult), r=[kk, lf], w=[KT[h]])
                                lastpos = 15 if d == 0 else 0
                                kb.V(lambda: nc.vector.tensor_copy(EL[:, h, n0 // 16:(n0 + wn) // 16], sg[:, lastpos:wn:16]), r=[sg], w=[EL])
                        dbgA = int(os.environ.get("DBGA", "0"))
                        if dbgA == 1:
                            continue
                        order = list(range(NT)) if d == 0 else [1, 0] + list(range(NT - 1, 1, -1))
                        MK_d = cA[:, 2048 + d * 128:2048 + (d + 1) * 128]
                        kb.V(lambda: nc.vector.memset(Sb[:], 0.0), w=[Sb])

                        def ldv(i):
                            tt = order[i]
                            kb.dma("sp", vt[(nvt + i) % 2][:], RT[b, tt * 128:(tt + 1) * 128, RT_AI:RT_AI + 512], f"a_v{(nvt + i) % 2}", r=[RT], w=[vt[(nvt + i) % 2]])
                            if d == 0:
                                kb.dma("sp", gt[(nvt + i) % 2][:], RT[b, tt * 128:(tt + 1) * 128, RT_AG:RT_AG + 512], f"a_g{(nvt + i) % 2}", r=[RT], w=[gt[(nvt + i) % 2]])
                        ldv(0)
                        if NT > 1:
                            ldv(1)

                        def front(i):
                            tt = order[i]
                            c0 = tt * 128
                            v_ = vt[(nvt + i) % 2]
                            ktok = ktoks[i % 2]; scT_ = scTs[i % 2]; po = pos[i % 2]
                            for h in range(4):
                                kb.P(lambda: nc.tensor.transpose(pk[:, h * 128:(h + 1) * 128], KT[h][:, c0:c0 + 128], ident[:]), r=[KT[h], ident], w=[pk])
                            kb.A(lambda: nc.scalar.copy(ktokf[:], pk[:]), r=[pk], w=[ktokf])
                            kb.G(lambda: nc.gpsimd.tensor_scalar(out=ktok[0][:], in0=ktokf[:], scalar1=cA[:, 2304:2305], scalar2=None, op0=ALU.mult), r=[ktokf, cA], w=[ktok[0]])
                            kb.G(lambda: nc.gpsimd.tensor_scalar(out=ktok[1][:], in0=ktokf[:], scalar1=cA[:, 2305:2306], scalar2=None, op0=ALU.mult), r=[ktokf, cA], w=[ktok[1]])
                            for h in range(4):
                                for e in range(2):
                                    kb.P(lambda: nc.tensor.matmul(psc[:, h * 128:(h + 1) * 128], lhsT=KT[h][:, c0:c0 + 128], rhs=QTe[e][h][:, c0:c0 + 128], start=(e == 0), stop=(e == 1)),
                                         r=[KT[h], QTe[e][h]], w=[psc])
                            kb.V(lambda: nc.vector.tensor_tensor(out=scT_[:].rearrange("p (h t) -> p h t", h=4), in0=psc[:].rearrange("p (h t) -> p h t", h=4),
                                                                 in1=MK_d.unsqueeze(1).to_broadcast([128, 4, 128]), op=ALU.mult), r=[psc, cA], w=[scT_])
                            for h in range(4):
                                kb.P(lambda: nc.tensor.matmul(po[:, h * 128:(h + 1) * 128], lhsT=scT_[:, h * 128:(h + 1) * 128], rhs=v_[:, h * 128:(h + 1) * 128], start=(h == 0), stop=False),
                                     r=[scT_, v_], w=[po])

                        def back(i):
                            nonlocal nps, nfin
                            tt = order[i]
                            c0 = tt * 128
                            v_ = vt[(nvt + i) % 2]; g_ = gt[(nvt + i) % 2]
                            ktok = ktoks[i % 2]; po = pos[i % 2]
                            for c in (range(8) if d == 0 else range(7, -1, -1)):
                                if dbgA == 2:
                                    break
                                r0 = 32 * (c // 2)
                                e = c % 2
                                gch = tt * 8 + c
                                pS_ = pS[nps % 2]; nps += 1
                                for h in range(4):
                                    if dbgA == 4:
                                        break
                                    kb.P(lambda: nc.tensor.matmul(po[r0:r0 + 32, h * 128:(h + 1) * 128], lhsT=QTe[e][h][:, c0 + r0:c0 + r0 + 32], rhs=Sb[:, h * 128:(h + 1) * 128],
                                                                  start=False, stop=(e == (1 if d == 0 else 0)), tile_position=(0, r0)), r=[QTe[e][h], Sb], w=[po])
                                if dbgA == 3:
                                    continue
                                for h in range(4):
                                    kb.P(lambda: nc.tensor.matmul(pS_[:, h * 128:(h + 1) * 128], lhsT=ident[:], rhs=Sb[:, h * 128:(h + 1) * 128], start=True, stop=False),
                                         r=[ident, Sb], w=[pS_])
                                    kb.P(lambda: nc.tensor.matmul(pS_[:, h * 128:(h + 1) * 128], lhsT=ktok[e][r0:r0 + 32, h * 128:(h + 1) * 128], rhs=v_[r0:r0 + 32, h * 128:(h + 1) * 128],
                                                                  start=False, stop=True, tile_position=(r0, 0)), r=[ktok[e], v_], w=[pS_])
                                kb.V(lambda: nc.vector.tensor_tensor(out=Sb[:].rearrange("p (h e) -> p h e", h=4), in0=pS_[:].rearrange("p (h e) -> p h e", h=4),
                                                                     in1=EL[:, :, gch:gch + 1].to_broadcast([128, 4, 128]), op=ALU.mult), r=[pS_, EL], w=[Sb])
                            if d == 1:
                                kb.A(lambda: nc.scalar.copy(YF[:, tt, :], po[:]), r=[po], w=[YF])
                            else:
                                kb.V(lambda: nc.vector.tensor_tensor(out=yf2[:], in0=po[:], in1=YF[:, tt, :], op=ALU.add), r=[po, YF], w=[yf2])
                                kb.A(lambda: nc.scalar.activation(out=sq_[:], in_=yf2[:], func=AF.Square), r=[yf2], w=[sq_])
                                kb.V(lambda: nc.vector.tensor_reduce(out=stt_[:, 0:4], in_=sq_[:].rearrange("p (h e) -> p h e", h=4), axis=AX.X, op=ALU.add), r=[sq_], w=[stt_])
                                kb.A(lambda: nc.scalar.activation(out=stt_[:, 0:4], in_=stt_[:, 0:4], func=AF.Sqrt, scale=1.0 / 128, bias=EPS), r=[stt_], w=[stt_])
                                kb.V(lambda: nc.vector.reciprocal(out=stt_[:, 0:4], in_=stt_[:, 0:4]), r=[stt_], w=[stt_])
                                kb.G(lambda: nc.gpsimd.tensor_tensor(out=yf2[:].rearrange("p (h e) -> p h e", h=4), in0=yf2[:].rearrange("p (h e) -> p h e", h=4),
                                                                     in1=stt_[:, 0:4].unsqueeze(2).to_broadcast([128, 4, 128]), op=ALU.mult), r=[yf2, stt_], w=[yf2])
                                kb.G(lambda: nc.gpsimd.tensor_tensor(out=yf2[:], in0=yf2[:], in1=anw[:], op=ALU.mult), r=[yf2, anw], w=[yf2])
                                kb.A(lambda: nc.scalar.activation(out=sq_[:], in_=g_[:], func=AF.Silu), r=[g_], w=[sq_])
                                kb.G(lambda: nc.gpsimd.tensor_tensor(out=on_[:], in0=yf2[:], in1=sq_[:], op=ALU.mult), r=[yf2, sq_], w=[on_])
                                for h in range(4):
                                    kb.P(lambda: nc.tensor.transpose(pt[:, h * 128:(h + 1) * 128], on_[:, h * 128:(h + 1) * 128], ident[:]), r=[on_, ident], w=[pt])
                                o_ = ot_[nfin % 2]
                                kb.A(lambda: nc.scalar.copy(o_[:].rearrange("p h t -> p (h t)"), pt[:]), r=[pt], w=[o_])
                                kb.dma("sp", CC[b, 0:512, c0:c0 + 128].rearrange("(h p) t -> p h t", p=128), o_[:], f"a_o{nfin % 2}", r=[o_], w=[CC])
                                nfin += 1

                        front(0)
                        for i in range(NT):
                            if i + 1 < NT:
                                front(i + 1)
                            back(i)
                            if i + 2 < NT:
                                ldv(i + 2)
                        nvt += NT
              kb.barrier()

            _scope("P2B")
            if "B" in mixers:
              with ExitStack() as es:
                pB = kb.sb(es, "pB", [128, 8, 5], F32)
                kb.dma("sp", pB[:], pB_in[l], "c0", w=[pB])
                rowB = kb.sb(es, "rowB", [128, 552], F32)
                kb.dma("sp", rowB[:], rowB_in[l:l + 1, :].to_broadcast([128, 552]), "c0", w=[rowB])
                cB = kb.sb(es, "cB", [128, 5 * 128 + 1024], F32)
                kb.dma("sp", cB[:], cB_in[:, :], "c0", w=[cB])
                mkb = kb.sb(es, "mkb", [128, 2, 512], BF16)
                kb.V(lambda: nc.vector.tensor_copy(mkb[:].rearrange("p a b -> p (a b)"), cB[:, 640:1664]), r=[cB], w=[mkb])
                acoef = kb.sb(es, "acoef", [128, 16], F32)
                kb.A(lambda: nc.scalar.activation(out=acoef[:], in_=rowB[:, 16:32], func=AF.Exp), r=[rowB], w=[acoef])
                kb.V(lambda: nc.vector.tensor_scalar(out=acoef[:], in0=acoef[:], scalar1=-1.0, scalar2=None, op0=ALU.mult), r=[acoef], w=[acoef])
                XB = [kb.sb(es, f"XB{i}", [128, L], BF16) for i in range(8)]
                cvt = kb.sb(es, "cvt", [128, L], F32)
                rawb = kb.sb(es, "rawb", [128, L], BF16)
                YF = kb.sb(es, "YFb", [128, NT, 512], BF16)
                DTr = kb.sb(es, "DTr", [128, NT, 16], BF16)
                DTV = kb.sb(es, "DTV", [128, NT, 16], F32)
                LA = kb.sb(es, "LA", [128, NT, 16], F32)
                R1 = kb.sb(es, "R1", [128, 8, 128], F32)
                R2 = kb.sb(es, "R2", [128, 8, 128], F32)
                rels = [kb.sb(es, f"rel{j}", [128, 8, 128], F32) for j in range(2)]
                scTb = kb.sb(es, "scTb", [128, 8, 128], BF16)
                xts = [kb.sb(es, f"xtb{j}", [128, 512], BF16) for j in range(2)]
                Bts = [kb.sb(es, f"Btb{j}", [128, 256], BF16) for j in range(2)]
                vs = [kb.sb(es, f"vb{j}", [128, 512], BF16) for j in range(2)]
                vw_ = kb.sb(es, "vwb", [128, 512], BF16)
                ebs = [kb.sb(es, f"eb16{j}", [128, 16], F32) for j in range(2)]
                Sf = kb.sb(es, "Sf", [128, 512], F32)
                Sb = kb.sb(es, "Sb", [128, 512], BF16)
                yf1 = kb.sb(es, "yf1b", [128, 512], F32)
                yf2 = kb.sb(es, "yf2b", [128, 512], F32)
                sq_ = kb.sb(es, "sqb", [128, 512], F32)
                zt = [kb.sb(es, f"zt{i}", [128, 512], BF16) for i in range(2)]
                stt_ = kb.sb(es, "sttb", [128, 8], F32)
                on_ = kb.sb(es, "onb", [128, 512], BF16)
                ot_ = [kb.sb(es, f"otb{i}", [128, 4, 128], BF16) for i in range(2)]
                Dp = [kb.ps(es, f"Dp{i}", [128, 512], F32) for i in range(2)]
                pq = kb.ps(es, "pq", [128, 512], F32)
                px6 = kb.ps(es, "px6", [128, 768], BF16)
                py = kb.ps(es, "pyb", [128, 512], F32)
                pz = kb.ps(es, "pzb", [128, 512], F32)
                pS = kb.ps(es, "pSb", [128, 512], F32)
                nfin = 0; nz = 0
                for b in range(2):
                    for ct in range(8):
                        kb.dma("sp", rawb[:], RC[b, RC_XBC + ct * 128:RC_XBC + (ct + 1) * 128, :], "b_r", r=[RC], w=[rawb])
                        for (s0, s1) in segs:
                            kb.V(lambda: nc.vector.tensor_scalar(out=cvt[:, s0:s1], in0=rawb[:, s0:s1], scalar1=pB[:, ct, 2:3], scalar2=pB[:, ct, 4:5],
                                                                 op0=ALU.mult, op1=ALU.add), r=[rawb, pB], w=[cvt])
                            for (j, off) in ((0, -2), (1, -1), (3, 1)):
                                o0, o1 = max(s0, s0 - off), min(s1, s1 - off)
                                kb.V(lambda: nc.vector.scalar_tensor_tensor(out=cvt[:, o0:o1], in0=rawb[:, o0 + off:o1 + off], scalar=pB[:, ct, j:j + 1],
                                                                            in1=cvt[:, o0:o1], op0=ALU.mult, op1=ALU.add), r=[rawb, pB, cvt], w=[cvt])
                        kb.A(lambda: nc.scalar.activation(out=XB[ct][:], in_=cvt[:], func=AF.Silu), r=[cvt], w=[XB[ct]])
                    for n0 in range(0, NT, 8):
                        n1 = min(NT, n0 + 8)
                        kb.dma("sp", DTr[:, n0:n1, :], RT[b, n0 * 128:n1 * 128, RT_DT:RT_DT + 16].rearrange("(n p) c -> p n c", p=128), "b_dt",
                               r=[RT], w=[DTr], allow_slow_non_contiguous=True)
                    kb.V(lambda: nc.vector.tensor_tensor(out=DTV[:], in0=DTr[:], in1=rowB[:, 0:16].unsqueeze(1).to_broadcast([128, NT, 16]), op=ALU.add),
                         r=[DTr, rowB], w=[DTV])
                    kb.A(lambda: nc.scalar.activation(out=DTV[:], in_=DTV[:], func=AF.Exp), r=[DTV], w=[DTV])
                    kb.A(lambda: nc.scalar.activation(out=DTV[:], in_=DTV[:], func=AF.Ln, bias=1.0), r=[DTV], w=[DTV])
                    kb.V(lambda: nc.vector.tensor_tensor(out=LA[:], in0=DTV[:], in1=acoef[:].unsqueeze(1).to_broadcast([128, NT, 16]), op=ALU.mult),
                         r=[DTV, acoef], w=[LA])
                    for d in (1, 0):
                        order = list(range(NT)) if d == 0 else [1, 0] + list(range(NT - 1, 1, -1))
                        tl = 127 if d == 0 else 0
                        U_d = cB[:, d * 128:(d + 1) * 128]
                        nU_d = cB[:, 256 + d * 128:256 + (d + 1) * 128]
                        ones_ = cB[:, 512:640]
                        kb.V(lambda: nc.vector.memset(Sf[:], 0.0), w=[Sf])
                        kb.G(lambda: nc.gpsimd.memset(Sb[:], 0.0), w=[Sb])
                        zsel = {}

                        def front_a(i):
                            nonlocal nz
                            tt = order[i]
                            c0 = tt * 128
                            la8 = LA[:, tt, d * 8:(d + 1) * 8]
                            dt8 = DTV[:, tt, d * 8:(d + 1) * 8]
                            rel = rels[i % 2]; xt_ = xts[i % 2]; Bt_ = Bts[i % 2]; v_ = vs[i % 2]; eb16 = ebs[i % 2]
                            if d == 0:
                                z_ = zt[nz % 2]; nz += 1
                                zsel[i] = z_
                                kb.dma("sp", z_[:], RT[b, c0:c0 + 128, RT_BZ:RT_BZ + 512], f"b_z{nz % 2}", r=[RT], w=[z_])
                            kb.V(lambda: nc.vector.tensor_tensor(out=R1[:], in0=U_d.unsqueeze(1).to_broadcast([128, 8, 128]),
                                                                 in1=la8.unsqueeze(2).to_broadcast([128, 8, 128]), op=ALU.mult), r=[cB, LA], w=[R1])
                            kb.G(lambda: nc.gpsimd.tensor_copy(R2[:], la8.unsqueeze(2).to_broadcast([128, 8, 128])), r=[LA], w=[R2])
                            for q in range(2):
                                kb.P(lambda: nc.tensor.matmul(Dp[q][:], lhsT=ones_, rhs=R1[:, 4 * q:4 * q + 4, :].rearrange("p h t -> p (h t)"), start=True, stop=False),
                                     r=[cB, R1], w=[Dp[q]])
                                kb.P(lambda: nc.tensor.matmul(Dp[q][:], lhsT=nU_d, rhs=R2[:, 4 * q:4 * q + 4, :].rearrange("p h t -> p (h t)"), start=False, stop=False),
                                     r=[cB, R2], w=[Dp[q]])
                                kb.P(lambda: nc.tensor.matmul(Dp[q][:], lhsT=ident[:], rhs=mkb[:, d, :], start=False, stop=True), r=[ident, mkb], w=[Dp[q]])
                                kb.A(lambda: nc.scalar.activation(out=rel[:, 4 * q:4 * q + 4, :].rearrange("p h t -> p (h t)"), in_=Dp[q][:], func=AF.Exp),
                                     r=[Dp[q]], w=[rel])
                            for g in range(2):
                                kb.P(lambda: nc.tensor.matmul(pq[:, g * 128:(g + 1) * 128], lhsT=XB[4 + g][:, c0:c0 + 128], rhs=XB[6 + g][:, c0:c0 + 128], start=True, stop=True),
                                     r=[XB[4 + g], XB[6 + g]], w=[pq])
                            kb.P(lambda: nc.tensor.matmul(pq[:, 256:264], lhsT=U_d, rhs=la8, start=True, stop=True), r=[cB, LA], w=[pq])
                            kb.P(lambda: nc.tensor.matmul(pq[:, 264:272], lhsT=ones_, rhs=la8, start=True, stop=True), r=[cB, LA], w=[pq])
                            kb.A(lambda: nc.scalar.activation(out=eb16[:], in_=pq[:, 256:272], func=AF.Exp), r=[pq], w=[eb16])
                            kb.V(lambda: nc.vector.tensor_tensor(out=scTb[:].rearrange("p (g e) t -> p g e t", g=2), in0=rel[:].rearrange("p (g e) t -> p g e t", g=2),
                                                                 in1=pq[:, 0:256].rearrange("p (g t) -> p g t", g=2).unsqueeze(2).to_broadcast([128, 2, 4, 128]), op=ALU.mult),
                                 r=[rel, pq], w=[scTb])
                            for ct in range(4):
                                kb.P(lambda: nc.tensor.transpose(px6[:, ct * 128:(ct + 1) * 128], XB[ct][:, c0:c0 + 128], ident[:]), r=[XB[ct], ident], w=[px6])
                            for g in range(2):
                                kb.P(lambda: nc.tensor.transpose(px6[:, 512 + g * 128:512 + (g + 1) * 128], XB[4 + g][:, c0:c0 + 128], ident[:]), r=[XB[4 + g], ident], w=[px6])
                            kb.A(lambda: nc.scalar.copy(xt_[:], px6[:, 0:512]), r=[px6], w=[xt_])
                            kb.A(lambda: nc.scalar.copy(Bt_[:], px6[:, 512:768]), r=[px6], w=[Bt_])
                            kb.G(lambda: nc.gpsimd.tensor_tensor(out=v_[:].rearrange("p (h e) -> p h e", h=8), in0=xt_[:].rearrange("p (h e) -> p h e", h=8),
                                                                 in1=dt8.unsqueeze(2).to_broadcast([128, 8, 64]), op=ALU.mult), r=[xt_, DTV], w=[v_])

                        def y_intra(i):
                            v_ = vs[i % 2]
                            for h in range(8):
                                kb.P(lambda: nc.tensor.matmul(py[:, h * 64:(h + 1) * 64], lhsT=scTb[:, h, :], rhs=v_[:, h * 64:(h + 1) * 64], start=True, stop=True),
                                     r=[scTb, v_], w=[py])

                        def back(i):
                            nonlocal nfin
                            tt = order[i]
                            c0 = tt * 128
                            rel = rels[i % 2]; xt_ = xts[i % 2]; Bt_ = Bts[i % 2]; v_ = vs[i % 2]; eb16 = ebs[i % 2]
                            z_ = zsel.get(i)
                            for g in range(2):
                                kb.P(lambda: nc.tensor.matmul(pz[:, g * 256:(g + 1) * 256], lhsT=XB[6 + g][:, c0:c0 + 128], rhs=Sb[:, g * 256:(g + 1) * 256], start=True, stop=True),
                                     r=[XB[6 + g], Sb], w=[pz])
                            kb.V(lambda: nc.vector.tensor_tensor(out=yf1[:].rearrange("p (h e) -> p h e", h=8), in0=pz[:].rearrange("p (h e) -> p h e", h=8),
                                                                 in1=eb16[:, 0:8].unsqueeze(2).to_broadcast([128, 8, 64]), op=ALU.mult), r=[pz, eb16], w=[yf1])
                            kb.G(lambda: nc.gpsimd.tensor_tensor(out=vw_[:].rearrange("p (h e) -> p h e", h=8), in0=v_[:].rearrange("p (h e) -> p h e", h=8),
                                                                 in1=rel[:, :, tl:tl + 1].to_broadcast([128, 8, 64]), op=ALU.mult), r=[v_, rel], w=[vw_])
                            for g in range(2):
                                kb.P(lambda: nc.tensor.matmul(pS[:, g * 256:(g + 1) * 256], lhsT=Bt_[:, g * 128:(g + 1) * 128], rhs=vw_[:, g * 256:(g + 1) * 256], start=True, stop=True),
                                     r=[Bt_, vw_], w=[pS])
                            kb.G(lambda: nc.gpsimd.tensor_tensor(out=Sf[:].rearrange("p (h e) -> p h e", h=8), in0=Sf[:].rearrange("p (h e) -> p h e", h=8),
                                                                 in1=eb16[:, 8:16].unsqueeze(2).to_broadcast([128, 8, 64]), op=ALU.mult), r=[Sf, eb16], w=[Sf])
                            kb.V(lambda: nc.vector.tensor_tensor(out=Sf[:], in0=Sf[:], in1=pS[:], op=ALU.add), r=[Sf, pS], w=[Sf])
                            kb.A(lambda: nc.scalar.copy(Sb[:], Sf[:]), r=[Sf], w=[Sb])
                            if d == 1:
                                kb.V(lambda: nc.vector.tensor_tensor(out=YF[:, tt, :], in0=yf1[:], in1=py[:], op=ALU.add), r=[yf1, py], w=[YF])
                            else:
                                kb.V(lambda: nc.vector.tensor_tensor(out=yf2[:], in0=yf1[:], in1=py[:], op=ALU.add), r=[yf1, py], w=[yf2])
                                kb.G(lambda: nc.gpsimd.tensor_tensor(out=yf2[:], in0=yf2[:], in1=YF[:, tt, :], op=ALU.add), r=[yf2, YF], w=[yf2])
                                kb.G(lambda: nc.gpsimd.tensor_tensor(out=yf1[:].rearrange("p (h e) -> p h e", h=8), in0=xt_[:].rearrange("p (h e) -> p h e", h=8),
                                                                     in1=rowB[:, 32:40].unsqueeze(2).to_broadcast([128, 8, 64]), op=ALU.mult), r=[xt_, rowB], w=[yf1])
                                kb.G(lambda: nc.gpsimd.tensor_tensor(out=yf2[:], in0=yf2[:], in1=yf1[:], op=ALU.add), r=[yf2, yf1], w=[yf2])
                                kb.A(lambda: nc.scalar.activation(out=sq_[:], in_=z_[:], func=AF.Silu), r=[z_], w=[sq_])
                                kb.V(lambda: nc.vector.tensor_tensor(out=yf2[:], in0=yf2[:], in1=sq_[:], op=ALU.mult), r=[yf2, sq_], w=[yf2])
                                kb.A(lambda: nc.scalar.activation(out=sq_[:], in_=yf2[:], func=AF.Square), r=[yf2], w=[sq_])
                                kb.V(lambda: nc.vector.tensor_reduce(out=stt_[:, 0:2], in_=sq_[:].rearrange("p (g e) -> p g e", g=2), axis=AX.X, op=ALU.add), r=[sq_], w=[stt_])
                                kb.A(lambda: nc.scalar.activation(out=stt_[:, 0:2], in_=stt_[:, 0:2], func=AF.Sqrt, scale=1.0 / 256, bias=EPS), r=[stt_], w=[stt_])
                                kb.V(lambda: nc.vector.reciprocal(out=stt_[:, 0:2], in_=stt_[:, 0:2]), r=[stt_], w=[stt_])
                                kb.G(lambda: nc.gpsimd.tensor_tensor(out=yf2[:].rearrange("p (g e) -> p g e", g=2), in0=yf2[:].rearrange("p (g e) -> p g e", g=2),
                                                                     in1=stt_[:, 0:2].unsqueeze(2).to_broadcast([128, 2, 256]), op=ALU.mult), r=[yf2, stt_], w=[yf2])
                                kb.G(lambda: nc.gpsimd.tensor_tensor(out=on_[:], in0=yf2[:], in1=rowB[:, 40:552], op=ALU.mult), r=[yf2, rowB], w=[on_])
                                for h in range(4):
                                    kb.P(lambda: nc.tensor.transpose(px6[:, h * 128:(h + 1) * 128], on_[:, h * 128:(h + 1) * 128], ident[:]), r=[on_, ident], w=[px6])
                                o_ = ot_[nfin % 2]
                                kb.A(lambda: nc.scalar.copy(o_[:].rearrange("p h t -> p (h t)"), px6[:, 0:512]), r=[px6], w=[o_])
                                kb.dma("sp", CC[b, 512:1024, c0:c0 + 128].rearrange("(h p) t -> p h t", p=128), o_[:], f"b_o{nfin % 2}", r=[o_], w=[CC])
                                nfin += 1

                        front_a(0)
                        y_intra(0)
                        for i in range(NT):
                            if i + 1 < NT:
                                front_a(i + 1)
                            back(i)
                            if i + 1 < NT:
                                y_intra(i + 1)
              kb.barrier()

            _scope("P2D")
            if "D" in mixers:
              with ExitStack() as es:
                cD = kb.sb(es, "cD", [128, 520], F32)
                kb.dma("sp", cD[:], cD_in[:, :], "c0", w=[cD])
                rope = kb.sb(es, "rope", [128, 2, L], F32)
                kb.dma("sp", rope[:], ropeT_in.t.rearrange("a p t -> p a t"), "c0", w=[rope])
                lg = kb.sb(es, "lg", [128, 8], F32)
                kb.dma("sp", lg[:], ret_logit[l:l + 1, :].to_broadcast([128, 8]), "c0", w=[lg])
                kb.A(lambda: nc.scalar.activation(out=lg[:], in_=lg[:], func=AF.Exp, scale=-1.0), r=[lg], w=[lg])
                kb.A(lambda: nc.scalar.activation(out=lg[:], in_=lg[:], func=AF.Ln, bias=1.0), r=[lg], w=[lg])
                kb.V(lambda: nc.vector.tensor_scalar(out=lg[:], in0=lg[:], scalar1=-1.0, scalar2=None, op0=ALU.mult), r=[lg], w=[lg])
                Dm = kb.sb(es, "Dm", [128, 2, 4, 128], F32)
                DG = kb.sb(es, "DG", [128, 2, 4, 128], BF16)
                wcol = kb.sb(es, "wcol", [128, 2, 4], F32)
                gcol = kb.sb(es, "gcol", [128, 2, 4], F32)
                g128 = kb.sb(es, "g128", [128, 8], F32)
                kb.A(lambda: nc.scalar.activation(out=g128[:], in_=lg[:], func=AF.Exp, scale=128.0), r=[lg], w=[g128])
                for d in range(2):
                    kb.A(lambda: nc.scalar.activation(out=wcol[:, d, :], in_=lg[:, d * 4:(d + 1) * 4], func=AF.Exp, scale=cD[:, 512 + d:513 + d]),
                         r=[lg, cD], w=[wcol])
                    kb.A(lambda: nc.scalar.activation(out=gcol[:, d, :], in_=lg[:, d * 4:(d + 1) * 4], func=AF.Exp, scale=cD[:, 514 + d:515 + d]),
                         r=[lg, cD], w=[gcol])
                    for h in range(4):
                        kb.A(lambda: nc.scalar.activation(out=Dm[:, d, h, :], in_=cD[:, d * 128:(d + 1) * 128], func=AF.Exp,
                                                          scale=lg[:, d * 4 + h:d * 4 + h + 1]), r=[lg, cD], w=[Dm])
                        kb.V(lambda: nc.vector.tensor_tensor(out=Dm[:, d, h, :], in0=Dm[:, d, h, :], in1=cD[:, 256 + d * 128:256 + (d + 1) * 128],
                                                             op=ALU.mult), r=[Dm, cD], w=[Dm])
                        kb.V(lambda: nc.vector.tensor_scalar(out=DG[:, d, h, :], in0=identf[:], scalar1=g128[:, d * 4 + h:d * 4 + h + 1], scalar2=None,
                                                             op0=ALU.mult), r=[identf, g128], w=[DG])
                kb.V(lambda: nc.vector.tensor_scalar(out=wcol[:], in0=wcol[:], scalar1=float(128 ** -0.5), scalar2=None, op0=ALU.mult), r=[wcol], w=[wcol])
                QR = [kb.sb(es, f"QR{h}", [128, L], BF16) for h in range(4)]
                KR = [kb.sb(es, f"KR{h}", [128, L], BF16) for h in range(4)]
                raw = [kb.sb(es, f"raw{i}", [128, L], BF16) for i in range(2)]
                tmp = [kb.sb(es, f"tmpd{i}", [128, 1024], F32) for i in range(2)]
                YF = kb.sb(es, "YF", [128, NT, 512], BF16)
                Sst = kb.sb(es, "Sst", [128, 4, 128], BF16)
                vt = [kb.sb(es, f"vt{i}", [128, 512], BF16) for i in range(2)]
                gt = [kb.sb(es, f"gt{i}", [128, 512], BF16) for i in range(2)]
                khats = [kb.sb(es, f"khat{j}", [128, 512], BF16) for j in range(2)]
                scT_ = kb.sb(es, "scTd", [128, 512], BF16)
                yf1 = kb.sb(es, "yf1", [128, 512], F32)
                yf2 = kb.sb(es, "yf2", [128, 512], F32)
                sq_ = kb.sb(es, "sqd", [128, 512], F32)
                stt_ = kb.sb(es, "sttd", [128, 16], F32)
                on_ = kb.sb(es, "ond", [128, 512], BF16)
                ot_ = [kb.sb(es, f"otd{i}", [128, 4, 128], BF16) for i in range(2)]
                pk = kb.ps(es, "pk", [128, 512], BF16)
                pt = kb.ps(es, "pt", [128, 512], BF16)
                psc = kb.ps(es, "psc", [128, 512], F32)
                py = kb.ps(es, "py", [128, 512], F32)
                pz = kb.ps(es, "pz", [128, 512], F32)
                pS = kb.ps(es, "pS", [128, 512], F32)
                p1k = [(n0, min(1024, L - n0)) for n0 in range(0, L, 1024)]
                nvt = 0; nfin = 0
                for b in range(2):
                    for h in range(4):
                        for (dst, r0, r1) in ((QR[h], RC_DQ, RC_DQR), (KR[h], RC_DK, RC_DKR)):
                            kb.dma("sp", raw[0][:], RC[b, r0 + h * 128:r0 + (h + 1) * 128, :], "d_r0", r=[RC], w=[raw[0]])
                            kb.dma("sp", raw[1][:], RC[b, r1 + h * 128:r1 + (h + 1) * 128, :], "d_r1", r=[RC], w=[raw[1]])
                            for (n0, wn) in p1k:
                                kb.V(lambda: nc.vector.tensor_tensor(out=tmp[0][:, 0:wn], in0=raw[0][:, n0:n0 + wn], in1=rope[:, 0, n0:n0 + wn], op=ALU.mult),
                                     r=[raw[0], rope], w=[tmp[0]])
                                kb.G(lambda: nc.gpsimd.tensor_tensor(out=tmp[1][:, 0:wn], in0=raw[1][:, n0:n0 + wn], in1=rope[:, 1, n0:n0 + wn], op=ALU.mult),
                                     r=[raw[1], rope], w=[tmp[1]])
                                kb.V(lambda: nc.vector.tensor_tensor(out=dst[:, n0:n0 + wn], in0=tmp[0][:, 0:wn], in1=tmp[1][:, 0:wn], op=ALU.add),
                                     r=[tmp[0], tmp[1]], w=[dst])
                    for d in (1, 0):
                        order = list(range(NT)) if d == 0 else [1, 0] + list(range(NT - 1, 1, -1))
                        kb.V(lambda: nc.vector.memset(Sst[:], 0.0), w=[Sst])

                        def ldv(i):
                            tt = order[i]
                            kb.dma("sp", vt[(nvt + i) % 2][:], RT[b, tt * 128:(tt + 1) * 128, RT_DV:RT_DV + 512], f"d_v{(nvt + i) % 2}", r=[RT], w=[vt[(nvt + i) % 2]])
                            if d == 0:
                                kb.dma("sp", gt[(nvt + i) % 2][:], RT[b, tt * 128:(tt + 1) * 128, RT_DG:RT_DG + 512], f"d_g{(nvt + i) % 2}", r=[RT], w=[gt[(nvt + i) % 2]])
                        ldv(0)

                        def front_a(i):
                            tt = order[i]
                            c0 = tt * 128
                            khat = khats[i % 2]
                            for h in range(4):
                                kb.P(lambda: nc.tensor.transpose(pk[:, h * 128:(h + 1) * 128], KR[h][:, c0:c0 + 128], ident[:]), r=[KR[h], ident], w=[pk])
                            kb.V(lambda: nc.vector.tensor_tensor(out=khat[:].rearrange("p (h e) -> p h e", h=4), in0=pk[:].rearrange("p (h e) -> p h e", h=4),
                                                                 in1=wcol[:, d, :].unsqueeze(2).to_broadcast([128, 4, 128]), op=ALU.mult), r=[pk, wcol], w=[khat])
                            for h in range(4):
                                kb.P(lambda: nc.tensor.matmul(psc[:, h * 128:(h + 1) * 128], lhsT=KR[h][:, c0:c0 + 128], rhs=QR[h][:, c0:c0 + 128], start=True, stop=True),
                                     r=[KR[h], QR[h]], w=[psc])
                            kb.V(lambda: nc.vector.tensor_tensor(out=scT_[:], in0=psc[:], in1=Dm[:, d].rearrange("p h t -> p (h t)"), op=ALU.mult), r=[psc, Dm], w=[scT_])

                        def y_intra(i):
                            v_ = vt[(nvt + i) % 2]
                            for h in range(4):
                                kb.P(lambda: nc.tensor.matmul(py[:, h * 128:(h + 1) * 128], lhsT=scT_[:, h * 128:(h + 1) * 128], rhs=v_[:, h * 128:(h + 1) * 128], start=True, stop=True),
                                     r=[scT_, v_], w=[py])

                        def back(i):
                            nonlocal nfin
                            tt = order[i]
                            c0 = tt * 128
                            v_ = vt[(nvt + i) % 2]; g_ = gt[(nvt + i) % 2]
                            khat = khats[i % 2]
                            for h in range(4):
                                kb.P(lambda: nc.tensor.matmul(pz[:, h * 128:(h + 1) * 128], lhsT=QR[h][:, c0:c0 + 128], rhs=Sst[:, h, :], start=True, stop=True),
                                     r=[QR[h], Sst], w=[pz])
                            for h in range(4):
                                kb.P(lambda: nc.tensor.matmul(pS[:, h * 128:(h + 1) * 128], lhsT=DG[:, d, h, :], rhs=Sst[:, h, :], start=True, stop=False), r=[DG, Sst], w=[pS])
                                kb.P(lambda: nc.tensor.matmul(pS[:, h * 128:(h + 1) * 128], lhsT=khat[:, h * 128:(h + 1) * 128], rhs=v_[:, h * 128:(h + 1) * 128], start=False, stop=True),
                                     r=[khat, v_], w=[pS])
                            kb.A(lambda: nc.scalar.copy(Sst[:].rearrange("p h e -> p (h e)"), pS[:]), r=[pS], w=[Sst])
                            kb.V(lambda: nc.vector.tensor_tensor(out=yf1[:].rearrange("p (h e) -> p h e", h=4), in0=pz[:].rearrange("p (h e) -> p h e", h=4),
                                                                 in1=gcol[:, d, :].unsqueeze(2).to_broadcast([128, 4, 128]), op=ALU.mult), r=[pz, gcol], w=[yf1])
                            if d == 1:
                                kb.V(lambda: nc.vector.tensor_tensor(out=YF[:, tt, :], in0=yf1[:], in1=py[:], op=ALU.add), r=[yf1, py], w=[YF])
                            else:
                                kb.V(lambda: nc.vector.tensor_tensor(out=yf2[:], in0=yf1[:], in1=py[:], op=ALU.add), r=[yf1, py], w=[yf2])
                                kb.G(lambda: nc.gpsimd.tensor_tensor(out=yf2[:], in0=yf2[:], in1=YF[:, tt, :], op=ALU.add), r=[yf2, YF], w=[yf2])
                                y3 = yf2[:].rearrange("p (h e) -> p h e", h=4)
                                kb.V(lambda: nc.vector.tensor_reduce(out=stt_[:, 0:4], in_=y3, axis=AX.X, op=ALU.add), r=[yf2], w=[stt_])
                                kb.A(lambda: nc.scalar.activation(out=sq_[:], in_=yf2[:], func=AF.Square), r=[yf2], w=[sq_])
                                kb.V(lambda: nc.vector.tensor_reduce(out=stt_[:, 4:8], in_=sq_[:].rearrange("p (h e) -> p h e", h=4), axis=AX.X, op=ALU.add), r=[sq_], w=[stt_])
                                kb.V(lambda: nc.vector.tensor_scalar(out=stt_[:, 0:8], in0=stt_[:, 0:8], scalar1=1.0 / 128, scalar2=None, op0=ALU.mult), r=[stt_], w=[stt_])
                                kb.V(lambda: nc.vector.tensor_tensor(out=stt_[:, 8:12], in0=stt_[:, 0:4], in1=stt_[:, 0:4], op=ALU.mult), r=[stt_], w=[stt_])
                                kb.V(lambda: nc.vector.tensor_tensor(out=stt_[:, 8:12], in0=stt_[:, 4:8], in1=stt_[:, 8:12], op=ALU.subtract), r=[stt_], w=[stt_])
                                kb.A(lambda: nc.scalar.activation(out=stt_[:, 8:12], in_=stt_[:, 8:12], func=AF.Sqrt, bias=EPS), r=[stt_], w=[stt_])
                                kb.V(lambda: nc.vector.reciprocal(out=stt_[:, 8:12], in_=stt_[:, 8:12]), r=[stt_], w=[stt_])
                                kb.G(lambda: nc.gpsimd.tensor_tensor(out=y3, in0=y3, in1=stt_[:, 0:4].unsqueeze(2).to_broadcast([128, 4, 128]), op=ALU.subtract), r=[yf2, stt_], w=[yf2])
                                kb.G(lambda: nc.gpsimd.tensor_tensor(out=y3, in0=y3, in1=stt_[:, 8:12].unsqueeze(2).to_broadcast([128, 4, 128]), op=ALU.mult), r=[yf2, stt_], w=[yf2])
                                kb.A(lambda: nc.scalar.activation(out=sq_[:], in_=g_[:], func=AF.Silu), r=[g_], w=[sq_])
                                kb.G(lambda: nc.gpsimd.tensor_tensor(out=on_[:], in0=yf2[:], in1=sq_[:], op=ALU.mult), r=[yf2, sq_], w=[on_])
                                for h in range(4):
                                    kb.P(lambda: nc.tensor.transpose(pt[:, h * 128:(h + 1) * 128], on_[:, h * 128:(h + 1) * 128], ident[:]), r=[on_, ident], w=[pt])
                                o_ = ot_[nfin % 2]
                                kb.A(lambda: nc.scalar.copy(o_[:].rearrange("p h t -> p (h t)"), pt[:]), r=[pt], w=[o_])
                                kb.dma("sp", CC[b, 1536:2048, c0:c0 + 128].rearrange("(h p) t -> p h t", p=128), o_[:], f"d_o{nfin % 2}", r=[o_], w=[CC])
                                nfin += 1

                        front_a(0)
                        y_intra(0)
                        for i in range(NT):
                            if i + 1 < NT:
                                ldv(i + 1)
                                front_a(i + 1)
                            back(i)
                            if i + 1 < NT:
                                y_intra(i + 1)
                        nvt += NT
              kb.barrier()

            if stop_after == "P2":
                break
            _scope("P3")
            def ln_epilogue(es_t, pmm2, hres, gb, lng, lnb, o_):
                s_, st6, mv = es_t
                for n in range(2):
                    kb.V(lambda: nc.vector.tensor_tensor(out=s_[:, n * 512:(n + 1) * 512], in0=pmm2[n][:], in1=gb[:, n * 512:(n + 1) * 512], op=ALU.mult),
                         r=[pmm2[n], gb], w=[s_])
                kb.V(lambda: nc.vector.scalar_tensor_tensor(out=s_[:], in0=hres, scalar=float(ALPHA), in1=s_[:], op0=ALU.mult, op1=ALU.add), r=[s_] + hres_r[0], w=[s_])
                for n in range(2):
                    kb.V(lambda: nc.vector.bn_stats(out=st6[:, n, :], in_=s_[:, n * 512:(n + 1) * 512]), r=[s_], w=[st6])
                kb.V(lambda: nc.vector.bn_aggr(out=mv[:, 0:2], in_=st6[:].rearrange("p a b -> p (a b)")), r=[st6], w=[mv])
                kb.A(lambda: nc.scalar.activation(out=mv[:, 2:3], in_=mv[:, 1:2], func=AF.Sqrt, bias=EPS), r=[mv], w=[mv])
                kb.V(lambda: nc.vector.reciprocal(out=mv[:, 2:3], in_=mv[:, 2:3]), r=[mv], w=[mv])
                kb.V(lambda: nc.vector.tensor_scalar(out=s_[:], in0=s_[:], scalar1=mv[:, 0:1], scalar2=mv[:, 2:3], op0=ALU.subtract, op1=ALU.mult), r=[s_, mv], w=[s_])
                kb.G(lambda: nc.gpsimd.tensor_tensor(out=s_[:], in0=s_[:], in1=lng[:], op=ALU.mult), r=[s_, lng], w=[s_])
                kb.G(lambda: nc.gpsimd.tensor_tensor(out=o_[:], in0=s_[:], in1=lnb[:], op=ALU.add), r=[s_, lnb], w=[o_])

            hres_r = [[]]
            with ExitStack() as es:
                Wo = kb.sb(es, "Wo", [128, 16, D], BF16)
                for kc in range(16):
                    kb.dma("pool", Wo[:, kc, :], w_out_in[l, kc * 128:(kc + 1) * 128, :], "p3w", w=[Wo])
                lng = kb.sb(es, "lng", [128, D], F32); lnb = kb.sb(es, "lnb", [128, D], F32)
                kb.dma("sp", lng[:], lnp_in[l, 0:1, :].to_broadcast([128, D]), "c0", w=[lng])
                kb.dma("sp", lnb[:], lnp_in[l, 1:2, :].to_broadcast([128, D]), "c0", w=[lnb])
                gb = [kb.sb(es, f"gb{v}", [128, D], F32) for v in range(3)]
                for v in range(3):
                    kb.dma("sp", gb[v][:], MODB[0, v], "c0", r=[MODB], w=[gb[v]])
                cct = [kb.sb(es, f"cct{i}", [128, 16, 512], BF16) for i in range(2)]
                hrt = [kb.sb(es, f"hrt{i}", [128, D], F32) for i in range(2)]
                s_ = kb.sb(es, "s3", [128, D], F32); st6 = kb.sb(es, "st6", [128, 2, 6], F32); mv = kb.sb(es, "mv3", [128, 4], F32)
                o3 = [kb.sb(es, f"o3{i}", [128, D], F32) for i in range(2)]
                pm3 = [kb.ps(es, f"pm3{i}", [128, 512], F32) for i in range(4)]
                blks = [(b, t0, Tn, isc) for b in range(2) for (t0, Tn, isc) in blocks_of(LAT)]
                hsrc = h0 if l == 0 else HRES

                def ldc(i):
                    b, t0, Tn, isc = blks[i]
                    kb.dma("sp", cct[i % 2][:, :, 0:Tn], CC[b, :, t0:t0 + Tn].rearrange("(k p) t -> p k t", p=128), f"p3c{i % 2}", r=[CC], w=[cct[i % 2]])
                ldc(0)
                ntile = 0
                for i, (b, t0, Tn, isc) in enumerate(blks):
                    if i + 1 < len(blks):
                        ldc(i + 1)
                    vec = 2 if isc else b
                    for ti in range(Tn // 128):
                        c0 = t0 + ti * 128
                        hr = hrt[ntile % 2]; o_ = o3[ntile % 2]
                        kb.dma("sp", hr[:], hsrc[b, c0:c0 + 128, :], f"p3h{ntile % 2}", r=[hsrc], w=[hr])
                        pp = pm3[(ntile % 2) * 2:(ntile % 2) * 2 + 2]
                        for n in range(2):
                            for kc in range(16):
                                kb.P(lambda: nc.tensor.matmul(pp[n][:], lhsT=cct[i % 2][:, kc, ti * 128:(ti + 1) * 128], rhs=Wo[:, kc, n * 512:(n + 1) * 512],
                                                              start=(kc == 0), stop=(kc == 15)), r=[cct[i % 2], Wo], w=[pp[n]])
                        hres_r[0] = [hr]
                        ln_epilogue((s_, st6, mv), pp, hr[:], gb[vec], lng, lnb, o_)
                        kb.dma("sp", H1[b, c0:c0 + 128, :], o_[:], f"p3o{ntile % 2}", r=[o_], w=[H1])
                        ntile += 1
            kb.barrier()
            if stop_after == "P3":
                break

            _scope("P4")
            with ExitStack() as es:
                Wfi = kb.sb(es, "Wfi", [128, 8, 2 * D_FF], BF16)
                for kc in range(8):
                    kb.dma("pool", Wfi[:, kc, :], ffn_w_in[l, kc * 128:(kc + 1) * 128, :], "p4w", w=[Wfi])
                Wfo = kb.sb(es, "Wfo", [128, 22, D], BF16)
                for kc in range(22):
                    kb.dma("pool", Wfo[:, kc, :], ffn_w_out[l, kc * 128:(kc + 1) * 128, :], "p4w2", w=[Wfo])
                pF = kb.sb(es, "pF", [128, 22, 4], F32)
                kb.dma("sp", pF[:], pF_in[l], "c0", w=[pF])
                lng = kb.sb(es, "lng4", [128, D], F32); lnb = kb.sb(es, "lnb4", [128, D], F32)
                kb.dma("sp", lng[:], lnp_in[l, 2:3, :].to_broadcast([128, D]), "c0", w=[lng])
                kb.dma("sp", lnb[:], lnp_in[l, 3:4, :].to_broadcast([128, D]), "c0", w=[lnb])
                gbt = kb.sb(es, "gb4", [128, D], F32)
                h1f = [kb.sb(es, f"h1f{i}", [128, 2, D], F32) for i in range(2)]
                hal = [kb.sb(es, f"hal{i}", [2, D], F32) for i in range(2)]
                halb = kb.sb(es, "halb", [128, D], BF16)
                kb.V(lambda: nc.vector.memset(halb[:], 0.0), w=[halb])
                htmp = kb.sb(es, "htmp", [128, 8, 2], F32)
                hb4 = kb.sb(es, "hb4", [128, 2, D], BF16)
                hT4 = kb.sb(es, "hT4", [128, 8, 258], BF16)
                prodT = kb.sb(es, "prodT", [128, 22, 256], BF16)
                gs = [kb.sb(es, f"gs{i}", [128, 258], F32) for i in range(2)]
                cv = [kb.sb(es, f"cv{i}", [128, 256], F32) for i in range(2)]
                s_ = kb.sb(es, "s4", [128, D], F32); st6 = kb.sb(es, "st64", [128, 2, 6], F32); mv = kb.sb(es, "mv4", [128, 4], F32)
                o4 = [kb.sb(es, f"o4{i}", [128, D], F32) for i in range(2)]
                pT4 = [kb.ps(es, f"pT4{i}", [128, 512], BF16) for i in range(2)]
                pg4 = [kb.ps(es, f"pg4{i}", [128, 512], F32) for i in range(2)]
                pu4 = [kb.ps(es, f"pu4{i}", [128, 512], F32) for i in range(2)]
                pf4 = [kb.ps(es, f"pf4{i}", [128, 512], F32) for i in range(2)]
                blks = [(b, t0, Tn, isc) for b in range(2) for (t0, Tn, isc) in blocks_of(LAT, 256)]

                def ld4(i):
                    b, t0, Tn, isc = blks[i]
                    s0, s1 = (0, CTX) if isc else (CTX, L)
                    kb.dma("sp", h1f[i % 2][:], H1[b, t0:t0 + Tn, :].rearrange("(n p) d -> p n d", p=128), f"p4h{i % 2}", r=[H1], w=[h1f[i % 2]])
                    hl = hal[i % 2]
                    kb.G(lambda: nc.gpsimd.memset(hl[:], 0.0), w=[hl])
                    if t0 > s0:
                        kb.dma("sp", hl[0:1, :], H1[b, t0 - 1:t0, :], f"p4l{i % 2}", r=[H1], w=[hl])
                    if t0 + Tn < s1:
                        kb.dma("sp", hl[1:2, :], H1[b, t0 + Tn:t0 + Tn + 1, :], f"p4l{i % 2}", r=[H1], w=[hl])
                ld4(0)
                npt = 0; nct = 0; ntile = 0
                for i, (b, t0, Tn, isc) in enumerate(blks):
                    assert Tn == 256
                    s0, s1 = (0, CTX) if isc else (CTX, L)
                    vec = 2 if isc else b
                    h1_, hl = h1f[i % 2], hal[i % 2]
                    if i + 1 < len(blks):
                        ld4(i + 1)
                    kb.dma("sp", gbt[:], MODB[1, vec], "p4g", r=[MODB], w=[gbt])
                    kb.A(lambda: nc.scalar.copy(hb4[:, 0, :], h1_[:, 0, :]), r=[h1_], w=[hb4])
                    kb.G(lambda: nc.gpsimd.tensor_copy(hb4[:, 1, :], h1_[:, 1, :]), r=[h1_], w=[hb4])
                    kb.A(lambda: nc.scalar.copy(halb[0:2, :], hl[:]), r=[hl], w=[halb])
                    for dc in range(8):
                        p = pT4[npt % 2]; npt += 1
                        for ti in range(2):
                            kb.P(lambda: nc.tensor.transpose(p[:, ti * 128:(ti + 1) * 128], hb4[:, ti, dc * 128:(dc + 1) * 128], ident[:]), r=[hb4, ident], w=[p])
                        kb.V(lambda: nc.vector.tensor_scalar(out=hT4[:, dc, 0:256], in0=p[:, 0:256], scalar1=modc[:, dc, 3, vec:vec + 1], scalar2=modc[:, dc, 2, vec:vec + 1],
                                                             op0=ALU.mult, op1=ALU.add), r=[p, modc], w=[hT4])
                    ph = pf4[1]
                    for dc in range(8):
                        kb.P(lambda: nc.tensor.matmul(ph[:, dc * 2:(dc + 1) * 2], lhsT=halb[:, dc * 128:(dc + 1) * 128], rhs=ident[:, 0:2], start=True, stop=True),
                             r=[halb, ident], w=[ph])
                    kb.V(lambda: nc.vector.tensor_tensor(out=htmp[:], in0=ph[:, 0:16].rearrange("p (a b) -> p a b", b=2), in1=modc[:, :, 3, vec:vec + 1].to_broadcast([128, 8, 2]), op=ALU.mult),
                         r=[ph, modc], w=[htmp])
                    kb.V(lambda: nc.vector.tensor_tensor(out=hT4[:, :, 256:258], in0=htmp[:], in1=modc[:, :, 2, vec:vec + 1].to_broadcast([128, 8, 2]), op=ALU.add),
                         r=[htmp, modc], w=[hT4])
                    if not (t0 > s0):
                        kb.V(lambda: nc.vector.memset(hT4[:, :, 256:257], 0.0), w=[hT4])
                    if not (t0 + Tn < s1):
                        kb.V(lambda: nc.vector.memset(hT4[:, :, 257:258], 0.0), w=[hT4])
                    for ct in range(22):
                        pg = pg4[nct % 2]; pu = pu4[nct % 2]; g_ = gs[nct % 2]; c_ = cv[nct % 2]; nct += 1
                        for kc in range(8):
                            kb.P(lambda: nc.tensor.matmul(pg[:, 0:258], lhsT=Wfi[:, kc, ct * 128:(ct + 1) * 128], rhs=hT4[:, kc, :], start=(kc == 0), stop=(kc == 7)),
                                 r=[Wfi, hT4], w=[pg])
                        for kc in range(8):
                            kb.P(lambda: nc.tensor.matmul(pu[:, 0:256], lhsT=Wfi[:, kc, D_FF + ct * 128:D_FF + (ct + 1) * 128], rhs=hT4[:, kc, 0:256], start=(kc == 0), stop=(kc == 7)),
                                 r=[Wfi, hT4], w=[pu])
                        kb.A(lambda: nc.scalar.copy(g_[:, 1:257], pg[:, 0:256]), r=[pg], w=[g_])
                        kb.A(lambda: nc.scalar.copy(g_[:, 0:258:257], pg[:, 256:258]), r=[pg], w=[g_])
                        kb.V(lambda: nc.vector.tensor_scalar(out=c_[:], in0=g_[:, 1:257], scalar1=pF[:, ct, 1:2], scalar2=pF[:, ct, 3:4], op0=ALU.mult, op1=ALU.add),
                             r=[g_, pF], w=[c_])
                        kb.V(lambda: nc.vector.scalar_tensor_tensor(out=c_[:], in0=g_[:, 0:256], scalar=pF[:, ct, 0:1], in1=c_[:], op0=ALU.mult, op1=ALU.add), r=[g_, pF, c_], w=[c_])
                        kb.V(lambda: nc.vector.scalar_tensor_tensor(out=c_[:], in0=g_[:, 2:258], scalar=pF[:, ct, 2:3], in1=c_[:], op0=ALU.mult, op1=ALU.add), r=[g_, pF, c_], w=[c_])
                        kb.A(lambda: nc.scalar.activation(out=c_[:], in_=c_[:], func=AF.Gelu), r=[c_], w=[c_])
                        kb.V(lambda: nc.vector.tensor_tensor(out=prodT[:, ct, :], in0=c_[:], in1=pu[:, 0:256], op=ALU.mult), r=[c_, pu], w=[prodT])
                    for ti in range(2):
                        c0 = t0 + ti * 128
                        o_ = o4[ntile % 2]; ntile += 1
                        for n in range(2):
                            for ct in range(22):
                                kb.P(lambda: nc.tensor.matmul(pf4[n][:], lhsT=prodT[:, ct, ti * 128:(ti + 1) * 128], rhs=Wfo[:, ct, n * 512:(n + 1) * 512],
                                                              start=(ct == 0), stop=(ct == 21)), r=[prodT, Wfo], w=[pf4[n]])
                        hres_r[0] = [h1_]
                        ln_epilogue((s_, st6, mv), pf4, h1_[:, ti, :], gbt, lng, lnb, o_)
                        if l < NL - 1:
                            kb.dma("sp", HRES[b, c0:c0 + 128, :], o_[:], f"p4o{ntile % 2}", r=[o_], w=[HRES])
                        if l == NL - 1 and not isc:
                            kb.dma("sp", out_d[b, c0 - CTX:c0 - CTX + 128, :], o_[:], f"p4o{ntile % 2}", r=[o_], w=[out_d])
            kb.barrier()

        _scope(None)
        kb.S.wait_all("sp", [t.b for t in kb.all_dram] + [out_d.b])
    print("program built: n_ins", S.n_ins, "extra waits", S.n_wait)
    return nc


HRES = None


def _rope_tables(LAT):
    GRID_W = 64
    rows = LAT // GRID_W
    row = np.repeat(np.arange(rows), GRID_W)
    col = np.tile(np.arange(GRID_W), rows)
    n_freq = 32
    inv = (10000.0 ** (-np.arange(n_freq, dtype=np.float32) / n_freq)).astype(np.float32)
    ang = np.concatenate([row[:, None].astype(np.float32) * inv, col[:, None].astype(np.float32) * inv], axis=-1)
    ang = np.concatenate([np.zeros((CTX, 64), np.float32), ang], axis=0)
    cos = np.cos(ang).astype(np.float32)
    sin = np.sin(ang).astype(np.float32)
    cosT = np.concatenate([cos, cos], axis=1).T
    sinT = np.concatenate([sin, sin], axis=1).T
    return np.ascontiguousarray(np.stack([cosT, sinT], 0))


def _constD():
    s = np.arange(128)[:, None]
    t = np.arange(128)[None, :]
    sc = np.float32(128 ** -0.5)
    c = np.zeros((128, 520), np.float32)
    c[:, 0:128] = np.maximum(t - s, 0)
    c[:, 128:256] = np.maximum(s - t, 0)
    c[:, 256:384] = (s <= t) * sc
    c[:, 384:512] = (s >= t) * sc
    p = np.arange(128)
    c[:, 512] = 127 - p
    c[:, 513] = p
    c[:, 514] = p + 1
    c[:, 515] = 128 - p
    return c


def _constB():
    j = np.arange(128)[:, None]
    t = np.arange(128)[None, :]
    c = np.zeros((128, 5 * 128 + 1024), np.float32)
    UF = (j <= t).astype(np.float32)
    UB = (j >= t).astype(np.float32)
    c[:, 0:128] = UF
    c[:, 128:256] = UB
    c[:, 256:384] = -UF
    c[:, 384:512] = -UB
    c[:, 512:640] = 1.0
    MF = np.where(j <= t, 0.0, NEG).astype(np.float32)
    MB = np.where(j >= t, 0.0, NEG).astype(np.float32)
    c[:, 640:1152] = np.tile(MF, (1, 4))
    c[:, 1152:1664] = np.tile(MB, (1, 4))
    return c


def _constA():
    c = np.zeros((128, 2312), np.float32)
    t = np.arange(1024)
    c[:, 0:1024] = (t % 16 != 0).astype(np.float32)[None, :]
    c[:, 1024:2048] = (t % 16 != 15).astype(np.float32)[None, :]
    s = np.arange(128)[:, None]
    tt = np.arange(128)[None, :]
    same = (s // 16) == (tt // 16)
    c[:, 2048:2176] = (same & (s <= tt)).astype(np.float32)
    c[:, 2176:2304] = (same & (s >= tt)).astype(np.float32)
    p = np.arange(128)
    c[:, 2304] = ((p // 16) % 2 == 0)
    c[:, 2305] = ((p // 16) % 2 == 1)
    return c


def _shared_inputs(inp, LAT):
    f = lambda a: np.ascontiguousarray(np.asarray(a, dtype=np.float32))
    m = {}
    m["ada_w"] = f(inp["ada_w"])
    m["ada_b"] = f(inp["ada_b"])
    m["ada_bT"] = f(np.asarray(inp["ada_b"]).reshape(DEPTH, 48, 128).transpose(0, 2, 1))
    m["w_in"] = f(inp["w_in"])
    m["ident"] = np.eye(128, dtype=np.float32)
    cm = lambda a, n: np.asarray(a).reshape(DEPTH, n, 128).transpose(0, 2, 1)
    pC = np.zeros((DEPTH, 128, 4, 16), np.float32)
    for j in range(4):
        pC[:, :, :, j] = cm(np.asarray(inp["lru_conv_w"])[:, j, :], 4)
    pC[:, :, :, 4] = cm(inp["lru_conv_b"], 4)
    for d in range(2):
        pC[:, :, :, 5 + d] = cm(np.asarray(inp["lru_ba"])[:, d], 4)
        pC[:, :, :, 7 + d] = cm(np.asarray(inp["lru_bi"])[:, d], 4)
        pC[:, :, :, 9 + d] = cm(np.asarray(inp["lru_lambda"])[:, d], 4)
    m["pC"] = pC
    m["lru_wa"] = f(inp["lru_wa"])
    m["lru_wi"] = f(inp["lru_wi"])
    m["ropeT"] = _rope_tables(LAT)
    m["cD"] = _constD()
    m["ret_decay_logit"] = f(np.asarray(inp["ret_decay_logit"]).reshape(DEPTH, 8))
    pB = np.zeros((DEPTH, 128, 8, 5), np.float32)
    for j in range(4):
        pB[:, :, :, j] = cm(np.asarray(inp["ssm_conv_w"])[:, j, :], 8)
    pB[:, :, :, 4] = cm(inp["ssm_conv_b"], 8)
    m["pB"] = pB
    m["rowB"] = f(np.concatenate([np.asarray(inp["ssm_dt_bias"]).reshape(DEPTH, 16), np.asarray(inp["ssm_a_log"]).reshape(DEPTH, 16),
                                  np.asarray(inp["ssm_d"]), np.asarray(inp["ssm_norm_w"])], axis=1))
    m["cB"] = _constB()
    m["lbT"] = f(np.asarray(inp["hgrn_lb_logits"]).reshape(DEPTH, 2, 4, 128).transpose(3, 0, 1, 2).reshape(128, DEPTH, 8))
    m["hgrn_norm_w"] = f(inp["hgrn_norm_w"])
    m["cA"] = _constA()
    m["w_out"] = f(inp["w_out"])
    m["ffn_w_in"] = f(inp["ffn_w_in"])
    m["ffn_w_out"] = f(inp["ffn_w_out"])
    m["lnp"] = f(np.stack([np.asarray(inp["ln1_g"]), np.asarray(inp["ln1_b"]), np.asarray(inp["ln2_g"]), np.asarray(inp["ln2_b"])], axis=1))
    pF = np.zeros((DEPTH, 128, 22, 4), np.float32)
    for j in range(3):
        pF[:, :, :, j] = cm(np.asarray(inp["ffn_conv_w"])[:, j, :], 22)
    pF[:, :, :, 3] = cm(inp["ffn_conv_b"], 22)
    m["pF"] = pF
    return m


def _core_inputs(inp, shared, core, LAT):
    b0 = 2 * core
    m = dict(shared)
    x = np.asarray(inp["x"])
    ctx = np.asarray(inp["ctx"])
    c = np.asarray(inp["c"])
    m["h0"] = np.ascontiguousarray(np.concatenate([ctx[b0:b0 + 2], x[b0:b0 + 2, :LAT]], axis=1).astype(np.float32))
    cv = np.stack([c[b0], c[b0 + 1], np.asarray(inp["c_ctx"])], 0).astype(np.float32)
    m["cT"] = np.ascontiguousarray(cv.reshape(3, 8, 128).transpose(2, 1, 0))
    return m


def kernel(**inputs):
    LAT = int(np.asarray(inputs["x"]).shape[1])
    NB = int(np.asarray(inputs["x"]).shape[0])
    ncores = NB // 2
    nc = build_program(DEPTH, LAT)
    shared = _shared_inputs(inputs, LAT)
    in_maps = [_core_inputs(inputs, shared, core, LAT) for core in range(ncores)]
    res = run_bass_kernel_spmd(nc, in_maps, core_ids=list(range(ncores)))
    out = np.concatenate([np.asarray(r["out"], dtype=np.float32) for r in res.results], axis=0)
    return out
```
